# Optimizing a Trainium2 kernel written in Bass

```python
import math
import jax, jax.numpy as jnp
from jax import lax
import numpy as np

D_MODEL = 1024
BATCH = 8
SEQ = 4096
DEPTH = 4

GRID_W = 64
CTX_LEN = 256
N_MIXERS = 3
N_ATTN_LAYERS = len(range(0, DEPTH, N_MIXERS))
N_CONF_LAYERS = len(range(1, DEPTH, N_MIXERS))
N_SCONV_LAYERS = len(range(2, DEPTH, N_MIXERS))
N_HEADS = 8
HEAD_DIM = D_MODEL // (2 * N_HEADS)
V_HEAD_DIM = 2 * HEAD_DIM
Q_BLOCK = 128
ROPE_THETA = 10000.0
D_FF = ((8 * D_MODEL // 3 + 127) // 128) * 128
CONF_KERNEL = 31
SCONV_KERNEL = 3
N_MOD = 9
EPS = 1e-6
LAMBDA_STD = 0.1

kernel_name = "hybrid_diffattn_conformer_shortconv_dit"


def rmsnorm(x, g):
    x32 = x.astype(jnp.float32)
    y = x32 * lax.rsqrt(jnp.mean(x32 * x32, axis=-1, keepdims=True) + EPS)
    return (y * g.astype(jnp.float32)).astype(x.dtype)


def layernorm(x, g, b):
    x32 = x.astype(jnp.float32)
    mu = jnp.mean(x32, axis=-1, keepdims=True)
    var = jnp.mean(jnp.square(x32 - mu), axis=-1, keepdims=True)
    y = (x32 - mu) * lax.rsqrt(var + EPS)
    return (y * g.astype(jnp.float32) + b.astype(jnp.float32)).astype(x.dtype)


def ada_in(s, g, shift, scale):
    return rmsnorm(s, g) * (1 + scale) + shift


def swiglu(h, w_in, w_out):
    a, gt = jnp.split(h @ w_in, 2, axis=-1)
    return (jax.nn.silu(gt) * a) @ w_out


def depthwise_conv(u, w):
    return lax.conv_general_dilated(u, w[:, None, :], window_strides=(1,), padding='SAME',
                                    dimension_numbers=('NWC', 'WIO', 'NWC'),
                                    feature_group_count=u.shape[-1])


def axial_rope_tables(n_tokens, dtype):
    rows = n_tokens // GRID_W
    row_ids = jnp.repeat(jnp.arange(rows, dtype=jnp.float32), GRID_W)
    col_ids = jnp.tile(jnp.arange(GRID_W, dtype=jnp.float32), rows)
    half = HEAD_DIM // 2
    inv_freq = ROPE_THETA ** (-jnp.arange(0, half, 2, dtype=jnp.float32) / half)
    ang_r = row_ids[:, None] * inv_freq
    ang_c = col_ids[:, None] * inv_freq
    ang = jnp.concatenate([ang_r, ang_r, ang_c, ang_c], axis=-1)
    return jnp.cos(ang).astype(dtype), jnp.sin(ang).astype(dtype)


def rope2d(x, cos, sin):
    x1, x2, x3, x4 = jnp.split(x, 4, axis=-1)
    rot = jnp.concatenate([-x2, x1, -x4, x3], axis=-1)
    return x * cos[:, None, None, :] + rot * sin[:, None, None, :]


def diff_softmax_mix(q, k, v, lam):
    s = jnp.einsum('bhmqd,bhmkd->bhmqk', q, k).astype(jnp.float32) * (HEAD_DIM ** -0.5)
    p = jax.nn.softmax(s, axis=-1)
    a = p[:, :, 0] - lam * p[:, :, 1]
    return jnp.einsum('bhqk,bhkd->bhqd', a.astype(v.dtype), v)


def diff_attention(h_lat, h_ctx, w_qkv, w_o, q_g, k_g, lam_p, subln_g, lam_init, cos, sin, with_ctx_out):
    B, L, D = h_lat.shape

    def project(h, use_rope):
        n = h.shape[1]
        q, k, v = jnp.split(h @ w_qkv, 3, axis=-1)
        q = rmsnorm(q.reshape(B, n, N_HEADS, 2, HEAD_DIM), q_g)
        k = rmsnorm(k.reshape(B, n, N_HEADS, 2, HEAD_DIM), k_g)
        if use_rope:
            q = rope2d(q, cos, sin)
            k = rope2d(k, cos, sin)
        q = q.transpose(0, 2, 3, 1, 4)
        k = k.transpose(0, 2, 3, 1, 4)
        v = v.reshape(B, n, N_HEADS, V_HEAD_DIM).transpose(0, 2, 1, 3)
        return q, k, v

    lp = lam_p.astype(jnp.float32)
    lam = jnp.exp(jnp.sum(lp[0] * lp[1])) - jnp.exp(jnp.sum(lp[2] * lp[3])) + lam_init

    q_l, k_l, v_l = project(h_lat, True)
    q_c, k_c, v_c = project(h_ctx, False)
    k_all = jnp.concatenate([k_c, k_l], axis=3)
    v_all = jnp.concatenate([v_c, v_l], axis=2)

    n_blk = L // Q_BLOCK
    q_blocks = jnp.moveaxis(q_l.reshape(B, N_HEADS, 2, n_blk, Q_BLOCK, HEAD_DIM), 3, 0)
    o = lax.map(lambda qb: diff_softmax_mix(qb, k_all, v_all, lam), q_blocks)
    o = jnp.moveaxis(o, 0, 2).reshape(B, N_HEADS, L, V_HEAD_DIM)

    def finish(o):
        n = o.shape[2]
        o = rmsnorm(o, subln_g) * (1.0 - lam_init)
        return o.transpose(0, 2, 1, 3).reshape(B, n, D) @ w_o

    out_lat = finish(o)
    out_ctx = finish(diff_softmax_mix(q_c, k_c, v_c, lam)) if with_ctx_out else None
    return out_lat, out_ctx


def conformer_conv(h, w_in, b_in, dw_w, dw_b, ln_g, ln_b, w_out, b_out):
    a, g = jnp.split(h @ w_in + b_in, 2, axis=-1)
    u = a * jax.nn.sigmoid(g)
    u = depthwise_conv(u, dw_w) + dw_b
    u = jax.nn.silu(layernorm(u, ln_g, ln_b))
    return u @ w_out + b_out


def short_conv(h, w_in, dw_w, w_out):
    b, cg, xh = jnp.split(h @ w_in, 3, axis=-1)
    return (b * depthwise_conv(cg * xh, dw_w)) @ w_out


def setup_inputs(seed: int = 0) -> dict:
    key = jax.random.key(seed)
    ks = iter(jax.random.split(key, 32))
    D, F = D_MODEL, D_FF

    def w(shape, fan_in):
        return jax.random.normal(next(ks), shape, jnp.float32) * fan_in ** -0.5

    def gain(shape):
        return 1.0 + 0.02 * jax.random.normal(next(ks), shape, jnp.float32)

    def bias(shape):
        return 0.01 * jax.random.normal(next(ks), shape, jnp.float32)

    return {
        "x": jax.random.normal(next(ks), (BATCH, SEQ, D), jnp.float32),
        "c": jax.random.normal(next(ks), (BATCH, D), jnp.float32),
        "ctx": jax.random.normal(next(ks), (BATCH, CTX_LEN, D), jnp.float32),
        "c_ctx": jax.random.normal(next(ks), (D,), jnp.float32),
        "ada_w": w((DEPTH, D, N_MOD * D), D),
        "ada_b": bias((DEPTH, N_MOD * D)),
        "norm_g": gain((DEPTH, 3, D)),
        "ffn_w_in": w((DEPTH, 2, D, 2 * F), D),
        "ffn_w_out": w((DEPTH, 2, F, D), F),
        "attn_w_qkv": w((N_ATTN_LAYERS, D, 3 * D), D),
        "attn_w_o": w((N_ATTN_LAYERS, D, D), D),
        "attn_q_g": gain((N_ATTN_LAYERS, HEAD_DIM)),
        "attn_k_g": gain((N_ATTN_LAYERS, HEAD_DIM)),
        "attn_lambda": LAMBDA_STD * jax.random.normal(next(ks), (N_ATTN_LAYERS, 4, HEAD_DIM), jnp.float32),
        "attn_subln_g": gain((N_ATTN_LAYERS, V_HEAD_DIM)),
        "conv_w_in": w((N_CONF_LAYERS, D, 2 * D), D),
        "conv_b_in": bias((N_CONF_LAYERS, 2 * D)),
        "conv_dw_w": w((N_CONF_LAYERS, CONF_KERNEL, D), CONF_KERNEL),
        "conv_dw_b": bias((N_CONF_LAYERS, D)),
        "conv_ln_g": gain((N_CONF_LAYERS, D)),
        "conv_ln_b": bias((N_CONF_LAYERS, D)),
        "conv_w_out": w((N_CONF_LAYERS, D, D), D),
        "conv_b_out": bias((N_CONF_LAYERS, D)),
        "sc_w_in": w((N_SCONV_LAYERS, D, 3 * D), D),
        "sc_dw_w": w((N_SCONV_LAYERS, SCONV_KERNEL, D), SCONV_KERNEL),
        "sc_w_out": w((N_SCONV_LAYERS, D, D), D),
    }


def reference(x, c, ctx, c_ctx, ada_w, ada_b, norm_g, ffn_w_in, ffn_w_out,
              attn_w_qkv, attn_w_o, attn_q_g, attn_k_g, attn_lambda, attn_subln_g,
              conv_w_in, conv_b_in, conv_dw_w, conv_dw_b, conv_ln_g, conv_ln_b, conv_w_out, conv_b_out,
              sc_w_in, sc_dw_w, sc_w_out):
    L = x.shape[1]
    cos, sin = axial_rope_tables(L, x.dtype)
    sc_ = jax.nn.silu(c)
    sc_ctx = jax.nn.silu(c_ctx)

    for i in range(DEPTH):
        kind = i % N_MIXERS
        j = i // N_MIXERS
        last = i == DEPTH - 1
        ctx_in_needed = (not last) or kind == 0
        ctx_out_needed = not last

        ml = [m[:, None, :] for m in jnp.split(sc_ @ ada_w[i] + ada_b[i], N_MOD, axis=-1)]
        mc = jnp.split(sc_ctx @ ada_w[i] + ada_b[i], N_MOD, axis=-1)

        x = x + 0.5 * ml[2] * swiglu(ada_in(x, norm_g[i, 0], ml[0], ml[1]), ffn_w_in[i, 0], ffn_w_out[i, 0])
        if ctx_in_needed:
            ctx = ctx + 0.5 * mc[2] * swiglu(ada_in(ctx, norm_g[i, 0], mc[0], mc[1]), ffn_w_in[i, 0], ffn_w_out[i, 0])

        hl = ada_in(x, norm_g[i, 1], ml[3], ml[4])
        hc = ada_in(ctx, norm_g[i, 1], mc[3], mc[4]) if ctx_in_needed else None
        if kind == 0:
            lam_init = 0.8 - 0.6 * math.exp(-0.3 * i)
            ol, oc = diff_attention(hl, hc, attn_w_qkv[j], attn_w_o[j], attn_q_g[j], attn_k_g[j],
                                    attn_lambda[j], attn_subln_g[j], lam_init, cos, sin, ctx_out_needed)
        elif kind == 1:
            cp = (conv_w_in[j], conv_b_in[j], conv_dw_w[j], conv_dw_b[j], conv_ln_g[j], conv_ln_b[j],
                  conv_w_out[j], conv_b_out[j])
            ol = conformer_conv(hl, *cp)
            oc = conformer_conv(hc, *cp) if ctx_out_needed else None
        else:
            ol = short_conv(hl, sc_w_in[j], sc_dw_w[j], sc_w_out[j])
            oc = short_conv(hc, sc_w_in[j], sc_dw_w[j], sc_w_out[j]) if ctx_out_needed else None
        x = x + ml[5] * ol
        if ctx_out_needed:
            ctx = ctx + mc[5] * oc

        x = x + 0.5 * ml[8] * swiglu(ada_in(x, norm_g[i, 2], ml[6], ml[7]), ffn_w_in[i, 1], ffn_w_out[i, 1])
        if ctx_out_needed:
            ctx = ctx + 0.5 * mc[8] * swiglu(ada_in(ctx, norm_g[i, 2], mc[6], mc[7]), ffn_w_in[i, 1], ffn_w_out[i, 1])

    return x
```

```python
import math
import os
import numpy as np
import concourse.bass as bass
import concourse.mybir as mybir
from concourse.bass_utils import run_bass_kernel_spmd

F32 = mybir.dt.float32
BF16 = mybir.dt.bfloat16
AF = mybir.ActivationFunctionType
ALU = mybir.AluOpType
AX = mybir.AxisListType

D = 1024
DEPTH = 4
SEQ = 4096
CTX = 256
NH = 8
HD = 64
FF = 2816
NMOD = 9
EPS = 1e-6
CK = 31
SK = 3
NT_ALL = (SEQ + CTX) // 128
NCT = CTX // 128
VW = 130

DSZ = {F32: 4, BF16: 2}
ATTACH_WAITS = False


class Op:
    __slots__ = ("eng", "fn", "deps", "sig", "count", "idx", "key", "order")


class Buf:
    __slots__ = ("w", "r", "pr")

    def __init__(self):
        self.w = {}
        self.r = {}
        self.pr = {}


class Prog:
    ENG = ("pe", "act", "dve", "pool", "sp")
    GAP = 4

    def __init__(self, nc):
        self.nc = nc
        self.q = {e: [] for e in self.ENG}
        self.esem = {e: nc.alloc_semaphore("sem_" + e) for e in ("pe", "act", "dve", "pool")}
        self.ksem = {}
        self.kcnt = {}
        self.klast = {}
        self.pending = {e: {} for e in self.ENG}

    @staticmethod
    def _sid(o):
        return ("k", o.key) if o.key is not None else ("e", o.eng)

    def _mk(self, eng, fn, R=(), W=(), deps=(), key=None):
        op = Op()
        op.eng = eng
        op.fn = fn
        op.sig = False
        op.key = key
        op.idx = len(self.q[eng])
        op.count = None
        d = {}

        def add(o):
            k = self._sid(o)
            if k not in d or o.order > d[k].order:
                d[k] = o

        for o in deps:
            add(o)
        for o in self.pending[eng].values():
            add(o)
        self.pending[eng] = {}
        for b in R:
            for o in b.w.values():
                add(o)
        for b in W:
            for o in b.r.values():
                add(o)
            for o in b.pr.values():
                add(o)
        if key is not None:
            if key not in self.ksem:
                self.ksem[key] = self.nc.alloc_semaphore("k_" + "_".join(str(x) for x in key))
                self.kcnt[key] = 0
            self.kcnt[key] += 16
            op.count = self.kcnt[key]
            op.order = op.count
            self.klast[key] = op
        else:
            op.order = op.idx
        for o in d.values():
            o.sig = True
        op.deps = list(d.values())
        sid = self._sid(op)
        for b in R:
            b.r[sid] = op
        for b in W:
            if b.r and not any(b is rb for rb in R):
                b.pr = b.r
                b.r = {}
                b.w = {}
            elif any(b is rb for rb in R):
                b.pr = {k: v for k, v in b.r.items() if v is not op}
                b.r = {}
                b.w = {}
            b.w[sid] = op
        self.q[eng].append(op)
        return op

    def barrier(self):
        last = {}
        for e in ("pe", "act", "dve", "pool"):
            for o in reversed(self.q[e]):
                if o.key is None and o.fn is not None:
                    last[("e", e)] = o
                    o.sig = True
                    break
        for k, o in self.klast.items():
            last[("k", k)] = o
        self.klast = {}
        for e in self.ENG:
            self.pending[e] = dict(last)

    def mm(self, out, lhsT, rhs, start, stop, R=(), W=(), skip=False):
        if skip:
            return self._mk("pe", lambda e: e.matmul(out, lhsT=lhsT, rhs=rhs, start=start, stop=stop, skip_group_check=True), R, W)
        return self._mk("pe", lambda e: e.matmul(out, lhsT=lhsT, rhs=rhs, start=start, stop=stop), R, W)

    def tr(self, out, in_, ident, R=(), W=()):
        return self._mk("pe", lambda e: e.transpose(out=out, in_=in_, identity=ident), R, W)

    def act(self, out, in_, func, bias=None, scale=None, R=(), W=()):
        kw = {}
        if bias is not None:
            kw["bias"] = bias
        if scale is not None:
            kw["scale"] = scale
        return self._mk("act", lambda e: e.activation(out=out, in_=in_, func=func, **kw), R, W)

    def ts(self, eng, out, in0, s1, s2, op0, op1=None, R=(), W=()):
        if op1 is None:
            return self._mk(eng, lambda e: e.tensor_scalar(out=out, in0=in0, scalar1=s1, scalar2=None, op0=op0), R, W)
        return self._mk(eng, lambda e: e.tensor_scalar(out=out, in0=in0, scalar1=s1, scalar2=s2, op0=op0, op1=op1), R, W)

    def tt(self, eng, out, in0, in1, op, R=(), W=()):
        return self._mk(eng, lambda e: e.tensor_tensor(out=out, in0=in0, in1=in1, op=op), R, W)

    def stt(self, eng, out, in0, scalar, in1, op0, op1, R=(), W=()):
        return self._mk(eng, lambda e: e.scalar_tensor_tensor(out=out, in0=in0, scalar=scalar, in1=in1, op0=op0, op1=op1), R, W)

    def red(self, eng, out, in_, R=(), W=()):
        return self._mk(eng, lambda e: e.reduce_sum(out=out, in_=in_, axis=AX.X), R, W)

    def cp(self, eng, out, in_, R=(), W=()):
        return self._mk(eng, lambda e: e.tensor_copy(out=out, in_=in_), R, W)

    def rsqrt(self, out, in_, R=(), W=(), lnexp=False):
        if lnexp:
            self._mk("act", lambda e: e.activation(out=out, in_=in_, func=AF.Ln), list(R), list(W))
            return self._mk("act", lambda e: e.activation(out=out, in_=out, func=AF.Exp, scale=-0.5), list(W), list(W))
        self._mk("act", lambda e: e.activation(out=out, in_=in_, func=AF.Sqrt), list(R), list(W))
        return self._mk("dve", lambda e: e.reciprocal(out=out, in_=out), list(W), list(W))

    def recip(self, out, in_, R=(), W=()):
        return self._mk("dve", lambda e: e.reciprocal(out=out, in_=in_), R, W)

    def memset(self, eng, ap, val, R=(), W=()):
        return self._mk(eng, lambda e: e.memset(ap, val), R, W)

    def dma(self, q, out, in_, key, R=(), W=()):
        return self._mk(q, lambda e: e.dma_start(out=out, in_=in_), R, W, key=key)

    def wait_all(self, eng, ops):
        return self._mk(eng, None, deps=ops)

    def emit(self):
        nc = self.nc
        for e in ("pe", "act", "dve", "pool"):
            c = 0
            for o in self.q[e]:
                if o.key is None and o.sig:
                    c += 1
                    o.count = c
        handles = {}

        def flush(ename, e):
            waited = {}
            for o in self.q[ename]:
                need = []
                for dpn in o.deps:
                    if dpn.key is not None:
                        sem = self.ksem[dpn.key]
                        sid = ("k", dpn.key)
                    else:
                        if dpn.eng == ename:
                            if ename == "pe":
                                continue
                            if ename != "pool" and o.idx - dpn.idx > self.GAP:
                                continue
                        sem = self.esem[dpn.eng]
                        sid = ("e", dpn.eng)
                    if waited.get(sid, 0) >= dpn.count:
                        continue
                    need.append((sem, dpn.count))
                    waited[sid] = dpn.count
                if o.fn is None:
                    for sem, cnt in need:
                        e.wait_ge(sem, cnt)
                    continue
                attach = ATTACH_WAITS and o.key is None and ename in ("pe", "act", "dve")
                for sem, cnt in (need[:-1] if attach else need):
                    e.wait_ge(sem, cnt)
                ins = o.fn(e)
                if need and attach:
                    ins._wait_ge(need[-1][0], need[-1][1])
                if o.key is not None:
                    ins.then_inc(self.ksem[o.key], 16)
                elif o.sig:
                    ins.then_inc(self.esem[ename], 1)

        with nc.Block() as block:
            @block.tensor
            def _(e):
                flush("pe", e)

            @block.scalar
            def _(e):
                flush("act", e)

            @block.vector
            def _(e):
                flush("dve", e)

            @block.gpsimd
            def _(e):
                flush("pool", e)

            @block.sync
            def _(e):
                flush("sp", e)


class Arena:
    def __init__(self, nc, nbytes):
        self.t = nc.alloc_sbuf_tensor("arena", [128, nbytes // 4], F32)
        self.cap = nbytes
        self.off = 0
        self.peak = 0

    def alloc(self, shape, dtype):
        n = 1
        for s in shape:
            n *= s
        nb = (n * DSZ[dtype] + 31) // 32 * 32
        assert self.off + nb <= self.cap, f"arena overflow: {self.off + nb} > {self.cap}"
        v = self.t[:, self.off // 4:(self.off + nb) // 4]
        if dtype != F32:
            v = v.bitcast(dtype)
        v = v[:, 0:n]
        self.off += nb
        self.peak = max(self.peak, self.off)
        if len(shape) == 2:
            v = v.rearrange("p (a b) -> p a b", b=shape[1])
        elif len(shape) == 3:
            v = v.rearrange("p (a b c) -> p a b c", b=shape[1], c=shape[2])
        elif len(shape) == 4:
            v = v.rearrange("p (a b c d) -> p a b c d", b=shape[1], c=shape[2], d=shape[3])
        return v

    def mark(self):
        return self.off

    def release(self, m):
        self.off = m


def _tiles_ffn(with_ctx):
    blocks = []
    if with_ctx:
        blocks.append((0, NCT, True))
    for b in range(SEQ // 512):
        blocks.append((NCT + 4 * b, 4, False))
    return blocks


def build_program(n_layers=DEPTH, debug=False):
    nc = bass.Bass("TRN2", target_bir_lowering=False)

    def din(name, shape, dt=F32):
        return nc.dram_tensor(name, list(shape), dt, kind="ExternalInput").ap()

    x_in = din("x", [SEQ, D])
    ctx_in = din("ctx", [CTX, D])
    cvec = din("cvec", [128, 8, 2])
    ada_w = din("ada_w", [DEPTH, D, NMOD * D])
    ada_b = din("ada_b", [DEPTH, 1, NMOD * D])
    norm_gF = din("norm_gF", [128, DEPTH, 3, 8])
    ffn_w_in = din("ffn_w_in", [DEPTH, 2, D, 2 * FF])
    ffn_w_out = din("ffn_w_out", [DEPTH, 2, FF, D])
    attn_w_qkv = din("attn_w_qkv", [2, D, 3 * D])
    attn_w_o = din("attn_w_o", [2, D, D])
    attn_g = din("attn_g", [128, 2, 4, 64])
    attn_lam = din("attn_lam", [128, 2, 4, 64])
    attn_sublnF = din("attn_sublnF", [128, 2])
    rope_cos = din("rope_cos", [128, SEQ // 128, 64])
    rope_sin = din("rope_sin", [128, SEQ // 128, 64])
    conv_w_in = din("conv_w_in", [D, 2 * D])
    conv_b_inF = din("conv_b_inF", [128, 16])
    conv_dwF = din("conv_dwF", [128, CK, 8])
    conv_vecF = din("conv_vecF", [128, 3, 8])
    conv_w_out = din("conv_w_out", [D, D])
    conv_b_out = din("conv_b_out", [1, D])
    sc_w_in = din("sc_w_in", [D, 3 * D])
    sc_dwF = din("sc_dwF", [128, SK, 8])
    sc_w_out = din("sc_w_out", [D, D])

    out = nc.dram_tensor("out", [SEQ, D], F32, kind="ExternalOutput").ap()
    skind = "ExternalOutput" if debug else "Internal"
    xs = nc.dram_tensor("xs", [NT_ALL * 128, D], F32, kind=skind).ap()
    QT_d = nc.dram_tensor("QT_d", [NT_ALL, 128, 1024], BF16, kind="Internal").ap()
    KT_d = nc.dram_tensor("KT_d", [NT_ALL, 128, 1024], BF16, kind="Internal").ap()
    V_d = nc.dram_tensor("V_d", [NT_ALL, 128, NH * VW], BF16, kind="Internal").ap()
    bT_d = nc.dram_tensor("bT_d", [NT_ALL, 128, 1024], BF16, kind="Internal").ap()
    gates_d = nc.dram_tensor("gates_d", [DEPTH, 3, 2, 128, D], F32, kind="Internal").ap()

    P = Prog(nc)
    AR = Arena(nc, 207 * 1024)
    ps_all = nc.alloc_psum_tensor("ps_all", [128, 8 * 512], F32)

    def bank(i, n=1):
        return ps_all[:, i * 512:(i + n) * 512]

    def bank_bf(i):
        return ps_all[:, i * 512:(i + 1) * 512].bitcast(BF16)

    pbuf = [Buf() for _ in range(8)]

    ident = AR.alloc([128], BF16)
    identf = AR.alloc([128], F32)
    modF = AR.alloc([72, 2], F32)
    ATab = AR.alloc([3, 2, 8], F32)
    gF = AR.alloc([DEPTH, 3, 8], F32)
    scT = AR.alloc([8, 2], BF16)
    scB = [AR.alloc([8, 128], BF16) for _ in range(2)]
    ones_row = AR.alloc([128], BF16)
    small = AR.alloc([64], F32)
    b_ident, b_modF, b_ATab, b_gF, b_sc, b_ones, b_small = (Buf() for _ in range(7))

    P.memset("pool", identf, 0.0, W=[b_ident])
    P._mk("pool", lambda e: e.affine_select(out=identf, in_=identf, pattern=[[-1, 128]], compare_op=ALU.not_equal,
                                             fill=1.0, base=0, channel_multiplier=1), R=[b_ident], W=[b_ident])
    P.cp("dve", ident, identf, R=[b_ident], W=[b_ident])
    P.memset("dve", ones_row, 1.0, W=[b_ones])
    m0 = AR.mark()
    cv = AR.alloc([8, 2], F32)
    b_cv = Buf()
    P.dma("sp", cv, cvec, ("misc", 0), W=[b_cv])
    P.dma("sp", gF, norm_gF, ("misc", 1), W=[b_gF])
    cvs = AR.alloc([8, 2], F32)
    P.act(cvs, cv, AF.Silu, R=[b_cv], W=[b_cv])
    P.cp("dve", scT, cvs, R=[b_cv], W=[b_sc])
    for m in range(2):
        P.cp("dve", scB[m], cvs[:, :, m:m + 1].to_broadcast([128, 8, 128]), R=[b_cv], W=[b_sc])
    P.barrier()
    AR.release(m0)
    PERSIST = AR.mark()

    def src_tile_ap(layer, first, n, first_ffn):
        if first_ffn:
            if first < NCT:
                return ctx_in[first * 128:(first + n) * 128, :].rearrange("(t p) d -> p t d", p=128)
            f = first - NCT
            return x_in[f * 128:(f + n) * 128, :].rearrange("(t p) d -> p t d", p=128)
        return xs[first * 128:(first + n) * 128, :].rearrange("(t p) d -> p t d", p=128)

    def dst_tile_ap(first, n, final):
        if final:
            f = first - NCT
            return out[f * 128:(f + n) * 128, :].rearrange("(t p) d -> p t d", p=128)
        return xs[first * 128:(first + n) * 128, :].rearrange("(t p) d -> p t d", p=128)

    store_ops = []

    def phase_mods(l):
        m0 = AR.mark()
        NCH = 18
        adw = [AR.alloc([8, 512], BF16) for _ in range(3)]
        b_adw = [Buf() for _ in range(3)]
        brow = AR.alloc([NMOD * D], BF16)
        b_brow = Buf()
        ones2 = AR.alloc([2], BF16)
        b_o2 = Buf()
        gsb = [AR.alloc([512], F32) for _ in range(2)]
        b_gsb = [Buf() for _ in range(2)]
        P.memset("dve", ones2, 1.0, W=[b_o2])
        P.dma("pool", brow[0:1, :], ada_b[l], ("w", 0), W=[b_brow])
        psF = bank(0)[:, 0:144].rearrange("p (f m) -> p f m", m=2)
        gi = 0
        for c in range(NCH):
            s_ = c % 3
            src = ada_w[l, :, c * 512:(c + 1) * 512].rearrange("(k p) n -> p k n", p=128)
            P.dma("pool", adw[s_], src, ("ada", s_), W=[b_adw[s_]])
            for fi in range(4):
                f = 4 * c + fi
                for kc in range(8):
                    P.mm(psF[:, f, :], adw[s_][:, kc, fi * 128:(fi + 1) * 128], scT[:, kc, :], kc == 0, False,
                         R=[b_adw[s_], b_sc], W=[pbuf[0]])
                P.mm(psF[:, f, :], brow[0:1, f * 128:(f + 1) * 128], ones2[0:1, :], False, True,
                     R=[b_brow, b_o2], W=[pbuf[0]])
            n = c // 2
            if n % 3 == 2:
                s = n // 3
                half = c % 2
                for m in range(2):
                    pb = 1 + m
                    for kc in range(8):
                        P.mm(bank(pb), scB[m][:, kc, :], adw[s_][:, kc, :], kc == 0, False,
                             R=[b_adw[s_], b_sc], W=[pbuf[pb]])
                    P.mm(bank(pb), ones_row[0:1, :], brow[0:1, c * 512:(c + 1) * 512], False, True,
                         R=[b_brow, b_ones], W=[pbuf[pb]])
                    g_ = gi % 2
                    gi += 1
                    P.act(gsb[g_], bank(pb), AF.Copy, scale=(1.0 if s == 1 else 0.5), R=[pbuf[pb]], W=[b_gsb[g_]])
                    P.dma("sp", gates_d[l, s, m, :, half * 512:(half + 1) * 512], gsb[g_], ("gst", g_), R=[b_gsb[g_]])
        P.cp("dve", modF, psF, R=[pbuf[0]], W=[b_modF])
        tmp = AR.alloc([8], F32)
        b_tmp = Buf()
        for s in range(3):
            for m in range(2):
                P.ts("dve", tmp, modF[:, (3 * s + 1) * 8:(3 * s + 2) * 8, m], 1.0, None, ALU.add, R=[b_modF], W=[b_tmp])
                P.tt("dve", ATab[:, s, m, :], tmp, gF[:, l, s, :], ALU.mult, R=[b_tmp, b_gF], W=[b_ATab])
        P.barrier()
        AR.release(m0)

    class NormCtx:
        def __init__(self, ptr_bank):
            self.xn = [AR.alloc([D], F32) for _ in range(2)]
            self.b_xn = [Buf() for _ in range(2)]
            self.sq = AR.alloc([D], F32)
            self.b_sq = Buf()
            self.y = [AR.alloc([D], BF16) for _ in range(2)]
            self.b_y = [Buf() for _ in range(2)]
            self.st = AR.alloc([2, 4], F32)
            self.b_st = [Buf() for _ in range(2)]
            self.ptr_bank = ptr_bank
            self.cnt = 0

        def load_norm(self, tile_ap):
            i = self.cnt % 2
            self.cnt += 1
            P.dma("sp", self.xn[i], tile_ap, ("xn", i), W=[self.b_xn[i]])
            P.act(self.sq, self.xn[i], AF.Square, R=[self.b_xn[i]], W=[self.b_sq])
            st = self.st[:, i, :]
            P.red("dve", st[:, 0:1], self.sq, R=[self.b_sq], W=[self.b_st[i]])
            P.ts("dve", st[:, 1:2], st[:, 0:1], 1.0 / D, EPS, ALU.mult, ALU.add, R=[self.b_st[i]], W=[self.b_st[i]])
            P.rsqrt(st[:, 2:3], st[:, 1:2], R=[self.b_st[i]], W=[self.b_st[i]])
            P.act(self.y[i], self.xn[i], AF.Identity, scale=st[:, 2:3], R=[self.b_xn[i], self.b_st[i]], W=[self.b_y[i]])
            return i

    def transpose_mod(nctx, yslots, hT, b_hT, s, m, l):
        n = len(yslots)
        pb = nctx.ptr_bank
        ptv = bank_bf(pb)
        for kc in range(8):
            half = kc % 2
            for tt, ys in enumerate(yslots):
                P.tr(ptv[:, half * 512 + tt * 128: half * 512 + (tt + 1) * 128], nctx.y[ys][:, kc * 128:(kc + 1) * 128], ident,
                     R=[nctx.b_y[ys], b_ident], W=[pbuf[pb]])
            P.act(hT[:, kc, 0:n * 128], ptv[:, half * 512: half * 512 + n * 128], AF.Identity,
                  bias=modF[:, 3 * s * 8 + kc, m:m + 1], scale=ATab[:, s, m, kc:kc + 1],
                  R=[pbuf[pb], b_modF, b_ATab], W=[b_hT])


    def residual_out(nctx_x, o_bank, pb, first_tile, tt, dc, gate, b_gate, tmpb, b_tmpb, xr, b_xr, final, idx):
        i = idx % 2
        P.tt("dve", tmpb[i], o_bank, gate[:, dc * 512:(dc + 1) * 512], ALU.mult, R=[pbuf[pb], b_gate], W=[b_tmpb[i]])
        P.tt("dve", xr[:, dc * 512:(dc + 1) * 512], xr[:, dc * 512:(dc + 1) * 512], tmpb[i], ALU.add,
             R=[b_tmpb[i], b_xr], W=[b_xr])

    class ResCtx:
        def __init__(self, l, s, o_banks):
            self.l, self.s = l, s
            self.gate = AR.alloc([D], F32)
            self.b_gate = Buf()
            self.gate_m = None
            self.xr = [AR.alloc([D], F32) for _ in range(2)]
            self.b_xr = [Buf() for _ in range(2)]
            self.tmp = [AR.alloc([512], F32) for _ in range(2)]
            self.b_tmp = [Buf() for _ in range(2)]
            self.o_banks = o_banks
            self.ocnt = 0
            self.xcnt = 0

        def set_gate(self, m):
            if self.gate_m != m:
                P.dma("sp", self.gate, gates_d[self.l, self.s, m], ("gate", 0), W=[self.b_gate])
                self.gate_m = m

        def begin_tile(self, tile, first_ffn):
            i = self.xcnt % 2
            self.xcnt += 1
            P.dma("sp", self.xr[i], src_tile_ap(self.l, tile, 1, first_ffn)[:, 0, :], ("xr", i), W=[self.b_xr[i]])
            return i

        def next_obank(self):
            pb = self.o_banks[self.ocnt % len(self.o_banks)]
            self.ocnt += 1
            return pb

        def update(self, i, pb, dc):
            j = self.ocnt % 2
            P.tt("dve", self.tmp[j], bank(pb), self.gate[:, dc * 512:(dc + 1) * 512], ALU.mult,
                 R=[pbuf[pb], self.b_gate], W=[self.b_tmp[j]])
            P.tt("dve", self.xr[i][:, dc * 512:(dc + 1) * 512], self.xr[i][:, dc * 512:(dc + 1) * 512], self.tmp[j], ALU.add,
                 R=[self.b_tmp[j], self.b_xr[i]], W=[self.b_xr[i]])

        def end_tile(self, i, tile, final):
            final = final and tile >= NCT
            op = P.dma("sp", dst_tile_ap(tile, 1, final)[:, 0, :], self.xr[i], ("xst", i), R=[self.b_xr[i]])
            if final:
                store_ops.append(op)

    def phase_ffn(l, s, with_ctx, first_ffn, final):
        wi = 0 if s == 0 else 1
        m0 = AR.mark()
        w_in = AR.alloc([8, 2 * FF], BF16)
        w_out = AR.alloc([22, D], BF16)
        NG = 11
        b_win = [Buf() for _ in range(NG)]
        b_wout = [Buf() for _ in range(2)]
        wsrc = ffn_w_in[l, wi].rearrange("(k p) n -> p k n", p=128)
        for g in range(NG):
            P.dma("pool", w_in[:, :, g * 256:(g + 1) * 256], wsrc[:, :, g * 256:(g + 1) * 256], ("w", 2 * g), W=[b_win[g]])
            P.dma("pool", w_in[:, :, FF + g * 256:FF + (g + 1) * 256], wsrc[:, :, FF + g * 256:FF + (g + 1) * 256],
                  ("w", 2 * g + 1), W=[b_win[g]])
        osrc = ffn_w_out[l, wi].rearrange("(j p) d -> p j d", p=128)
        for h in range(2):
            P.dma("pool", w_out[:, h * 11:(h + 1) * 11, :], osrc[:, h * 11:(h + 1) * 11, :], ("w", 22 + h), W=[b_wout[h]])
        nctx = NormCtx(ptr_bank=6)
        rc = ResCtx(l, s, o_banks=[4, 5])
        hT = AR.alloc([8, 512], BF16)
        b_hT = Buf()
        uT = AR.alloc([22, 512], BF16)
        b_uT = Buf()
        sg = [AR.alloc([512], F32) for _ in range(2)]
        b_sg = [Buf() for _ in range(2)]
        blocks = _tiles_ffn(with_ctx)

        def stage_T(blk):
            first, n, is_ctx = blk
            ys = []
            for tt in range(n):
                ys.append(nctx.load_norm(src_tile_ap(l, first + tt, 1, first_ffn)[:, 0, :]))
                if len(ys) == 2 or tt == n - 1:
                    pass
            return ys

        def stage_T_full(blk):
            first, n, is_ctx = blk
            m = 1 if is_ctx else 0
            pb = nctx.ptr_bank
            ptv = bank_bf(pb)
            for tt in range(n):
                ys = nctx.load_norm(src_tile_ap(l, first + tt, 1, first_ffn)[:, 0, :])
                for kc in range(8):
                    P.tr(ptv[:, kc * 128:(kc + 1) * 128], nctx.y[ys][:, kc * 128:(kc + 1) * 128], ident,
                         R=[nctx.b_y[ys], b_ident], W=[pbuf[pb]])
                for kc in range(8):
                    P.act(hT[:, kc, tt * 128:(tt + 1) * 128], ptv[:, kc * 128:(kc + 1) * 128], AF.Identity,
                          bias=modF[:, 3 * s * 8 + kc, m:m + 1], scale=ATab[:, s, m, kc:kc + 1],
                          R=[pbuf[pb], b_modF, b_ATab], W=[b_hT])

        def stage_IN(blk):
            first, n, is_ctx = blk
            N = n * 128
            for j in range(22):
                g = j // 2
                pa = (j % 2) * 2
                pg = pa + 1
                for kc in range(8):
                    P.mm(bank(pa)[:, 0:N], w_in[:, kc, j * 128:(j + 1) * 128], hT[:, kc, 0:N], kc == 0, kc == 7,
                         R=[b_win[g], b_hT], W=[pbuf[pa]])
                for kc in range(8):
                    P.mm(bank(pg)[:, 0:N], w_in[:, kc, FF + j * 128:FF + (j + 1) * 128], hT[:, kc, 0:N], kc == 0, kc == 7,
                         R=[b_win[g], b_hT], W=[pbuf[pg]])
                i = j % 2
                P.act(sg[i][:, 0:N], bank(pg)[:, 0:N], AF.Silu, R=[pbuf[pg]], W=[b_sg[i]])
                P.tt("dve", uT[:, j, 0:N], bank(pa)[:, 0:N], sg[i][:, 0:N], ALU.mult, R=[pbuf[pa], b_sg[i]], W=[b_uT])

        def stage_OUT(blk):
            first, n, is_ctx = blk
            rc.set_gate(1 if is_ctx else 0)
            for tt in range(n):
                i = rc.begin_tile(first + tt, first_ffn)
                for dc in range(2):
                    pb = rc.next_obank()
                    for j in range(22):
                        P.mm(bank(pb), uT[:, j, tt * 128:(tt + 1) * 128], w_out[:, j, dc * 512:(dc + 1) * 512], j == 0, j == 21,
                             R=[b_uT, b_wout[j // 11]], W=[pbuf[pb]])
                    rc.update(i, pb, dc)
                rc.end_tile(i, first + tt, final)

        stage_T_full(blocks[0])
        for bi, blk in enumerate(blocks):
            stage_IN(blk)
            if bi + 1 < len(blocks):
                stage_T_full(blocks[bi + 1])
            stage_OUT(blk)
        P.barrier()
        AR.release(m0)

    def phase_attn_qkv(l, j_attn):
        m0 = AR.mark()
        wq = AR.alloc([8, 3 * D], BF16)
        b_wq = [Buf() for _ in range(6)]
        wsrc = attn_w_qkv[j_attn].rearrange("(k p) n -> p k n", p=128)
        for n in range(6):
            P.dma("pool", wq[:, :, n * 512:(n + 1) * 512], wsrc[:, :, n * 512:(n + 1) * 512], ("w", n), W=[b_wq[n]])
        cosT = AR.alloc([SEQ // 128, 64], F32)
        sinT = AR.alloc([SEQ // 128, 64], F32)
        gq = AR.alloc([4, 64], F32)
        b_tab = Buf()
        P.dma("sp", cosT, rope_cos, ("misc", 0), W=[b_tab])
        P.dma("sp", sinT, rope_sin, ("misc", 1), W=[b_tab])
        P.dma("sp", gq, attn_g[:, j_attn], ("misc", 2), W=[b_tab])
        nctx = NormCtx(ptr_bank=6)
        hT = [AR.alloc([8, 128], BF16) for _ in range(2)]
        b_hT = [Buf() for _ in range(2)]
        sqb = AR.alloc([2 * D], F32)
        b_sqb = Buf()
        stg = AR.alloc([64], F32)
        b_stg = Buf()
        csk = AR.alloc([2, 2, 64], F32)
        b_csk = Buf()
        t1 = AR.alloc([2 * D], F32)
        b_t1 = Buf()
        t2 = AR.alloc([2 * D], F32)
        b_t2 = Buf()
        qkh = [AR.alloc([2 * D], BF16) for _ in range(2)]
        b_qkh = [Buf() for _ in range(2)]
        qkT = [AR.alloc([2, NH, 128], BF16) for _ in range(2)]
        b_qkT = [Buf() for _ in range(2)]
        vaug = [AR.alloc([NH, VW], BF16) for _ in range(2)]
        b_vaug = [Buf() for _ in range(2)]
        for i in range(2):
            P.memset("dve", vaug[i], 1.0, W=[b_vaug[i]])
        for jt in range(NT_ALL):
            is_ctx = jt < NCT
            m = 1 if is_ctx else 0
            ys = nctx.load_norm(src_tile_ap(l, jt, 1, False)[:, 0, :])
            hs = jt % 2
            pb = nctx.ptr_bank
            ptv = bank_bf(pb)
            for kc in range(8):
                P.tr(ptv[:, kc * 128:(kc + 1) * 128], nctx.y[ys][:, kc * 128:(kc + 1) * 128], ident,
                     R=[nctx.b_y[ys], b_ident], W=[pbuf[pb]])
            for kc in range(8):
                P.act(hT[hs][:, kc, :], ptv[:, kc * 128:(kc + 1) * 128], AF.Identity,
                      bias=modF[:, 3 * 8 + kc, m:m + 1], scale=ATab[:, 1, m, kc:kc + 1],
                      R=[pbuf[pb], b_modF, b_ATab], W=[b_hT[hs]])
            for n in range(6):
                for kc in range(8):
                    P.mm(bank(n), hT[hs][:, kc, :], wq[:, kc, n * 512:(n + 1) * 512], kc == 0, kc == 7,
                         R=[b_hT[hs], b_wq[n]], W=[pbuf[n]])
            qk_ps = bank(0, 4)
            qb = [pbuf[0], pbuf[1], pbuf[2], pbuf[3]]
            P.act(sqb, qk_ps, AF.Square, R=qb, W=[b_sqb])
            P.red("dve", stg[:, 0:32], sqb.rearrange("p (g d) -> p g d", d=64), R=[b_sqb], W=[b_stg])
            P.ts("dve", stg[:, 0:32], stg[:, 0:32], 1.0 / HD, EPS, ALU.mult, ALU.add, R=[b_stg], W=[b_stg])
            P.rsqrt(stg[:, 32:64], stg[:, 0:32], R=[b_stg], W=[b_stg])
            rs_b = stg[:, 32:64].unsqueeze(2).to_broadcast([128, 32, 64])
            qi = jt % 2
            if is_ctx:
                for w in range(2):
                    P.tt("dve", t1[:, w * D:(w + 1) * D].rearrange("p (g d) -> p g d", d=64),
                         qk_ps[:, w * D:(w + 1) * D].rearrange("p (g d) -> p g d", d=64),
                         gq[:, w:w + 1, :].to_broadcast([128, 16, 64]), ALU.mult, R=qb + [b_tab], W=[b_t1])
            else:
                jl = jt - NCT
                for w in range(2):
                    P.tt("dve", csk[:, w, 0, :], cosT[:, jl, :], gq[:, w, :], ALU.mult, R=[b_tab], W=[b_csk])
                    P.tt("dve", csk[:, w, 1, :], sinT[:, jl, :], gq[:, 2 + w, :], ALU.mult, R=[b_tab], W=[b_csk])
                for w in range(2):
                    xv = qk_ps[:, w * D:(w + 1) * D]
                    P.tt("dve", t1[:, w * D:(w + 1) * D].rearrange("p (g d) -> p g d", d=64),
                         xv.rearrange("p (g d) -> p g d", d=64),
                         csk[:, w, 0:1, :].to_broadcast([128, 16, 64]), ALU.mult, R=qb + [b_csk], W=[b_t1])
                    x5 = xv.rearrange("p (g a h d) -> p g a h d", a=2, h=2, d=16)
                    o5 = t2[:, w * D:(w + 1) * D].rearrange("p (g a h d) -> p g a h d", a=2, h=2, d=16)
                    s4 = csk[:, w, 1, :].rearrange("p (a h d) -> p a h d", a=2, h=2)
                    for hh in range(2):
                        P.tt("dve", o5[:, :, :, hh, :], x5[:, :, :, 1 - hh, :],
                             s4[:, :, hh, :].unsqueeze(1).to_broadcast([128, 16, 2, 16]), ALU.mult,
                             R=qb + [b_csk], W=[b_t2])
                P.tt("dve", t1, t1, t2, ALU.add, R=[b_t1, b_t2], W=[b_t1])
            P.tt("dve", qkh[qi].rearrange("p (g d) -> p g d", d=64), t1.rearrange("p (g d) -> p g d", d=64), rs_b, ALU.mult,
                 R=[b_t1, b_stg], W=[b_qkh[qi]])
            P.act(vaug[qi][:, :, 0:128], bank(4, 2).rearrange("p (h d) -> p h d", d=128), AF.Copy,
                  R=[pbuf[4], pbuf[5]], W=[b_vaug[qi]])
            P.dma("sp", V_d[jt].rearrange("p (h d) -> p h d", d=VW), vaug[qi], ("vst", qi), R=[b_vaug[qi]])
            ptq = bank_bf(7)
            for w in range(2):
                for h in range(NH):
                    P.tr(ptq[:, h * 128:(h + 1) * 128], qkh[qi][:, w * D + h * 128: w * D + (h + 1) * 128], ident,
                         R=[b_qkh[qi], b_ident], W=[pbuf[7]])
                P.cp("dve", qkT[qi][:, w, :, :], ptq.rearrange("p (h t) -> p h t", t=128), R=[pbuf[7]], W=[b_qkT[qi]])
                dst = (QT_d if w == 0 else KT_d)[jt].rearrange("p (h t) -> p h t", t=128)
                P.dma("sp", dst, qkT[qi][:, w, :, :], ("qkst", qi * 2 + w), R=[b_qkT[qi]])
        P.barrier()
        AR.release(m0)

    def phase_attn_core(l, j_attn, ctx_out, lam_init):
        m0 = AR.mark()
        KT = AR.alloc([NT_ALL, NH, 128], BF16)
        VA = AR.alloc([NT_ALL, NH, VW], BF16)
        b_KT = [Buf() for _ in range(NT_ALL)]
        b_VA = [Buf() for _ in range(NT_ALL)]
        wo = AR.alloc([NH, D], BF16)
        b_wo = Buf()
        grp = [(0, 2)] + [(2 + 4 * i, 4) for i in range(8)]
        for gi_, (f, n) in enumerate(grp):
            o1 = P.dma("sp", KT[:, f:f + n], KT_d[f:f + n].rearrange("t p (h k) -> p t h k", k=128), ("kv", 2 * gi_),
                       W=[b_KT[t] for t in range(f, f + n)])
            o2 = P.dma("sp", VA[:, f:f + n], V_d[f:f + n].rearrange("t p (h k) -> p t h k", k=VW), ("kv", 2 * gi_ + 1),
                       W=[b_VA[t] for t in range(f, f + n)])
        P.dma("pool", wo, attn_w_o[j_attn].rearrange("(h p) d -> p h d", p=128), ("w", 0), W=[b_wo])
        lamb = AR.alloc([4, 64], F32)
        b_lam = Buf()
        P.dma("sp", lamb, attn_lam[:, j_attn], ("misc", 0), W=[b_lam])
        lw = AR.alloc([2, 64], F32)
        b_lw = Buf()
        lv = AR.alloc([8], F32)
        b_lv = Buf()
        for i in range(2):
            P.tt("dve", lw[:, i, :], lamb[:, 2 * i, :], lamb[:, 2 * i + 1, :], ALU.mult, R=[b_lam], W=[b_lw])
        P.red("dve", lv[:, 0:2], lw, R=[b_lw], W=[b_lv])
        P.act(lv[:, 2:4], lv[:, 0:2], AF.Exp, R=[b_lv], W=[b_lv])
        P.tt("dve", lv[:, 4:5], lv[:, 2:3], lv[:, 3:4], ALU.subtract, R=[b_lv], W=[b_lv])
        P.ts("dve", lv[:, 5:6], lv[:, 4:5], lam_init, -1.0, ALU.add, ALU.mult, R=[b_lv], W=[b_lv])
        sub = AR.alloc([2], F32)
        b_sub = Buf()
        P.dma("sp", sub, attn_sublnF, ("misc", 1), W=[b_sub])
        P.ts("dve", lv[:, 6:7], sub[:, j_attn:j_attn + 1], 1.0 - lam_init, None, ALU.mult, R=[b_sub, b_lv], W=[b_lv])

        onesb = AR.alloc([128], BF16)
        b_onesb = Buf()
        P.memset("dve", onesb, 1.0, W=[b_onesb])
        QT = [AR.alloc([2, NH, 128], BF16) for _ in range(2)]
        b_QT = [Buf() for _ in range(2)]
        NPT = 4
        PT = [AR.alloc([512], BF16) for _ in range(NPT)]
        b_PT = [Buf() for _ in range(NPT)]
        rinv = [AR.alloc([512], F32) for _ in range(1)] * 2
        b_rinv = [Buf()] * 2
        tq = [AR.alloc([512], F32) for _ in range(1)] * 2
        b_tq = [Buf()] * 2
        ob = [AR.alloc([256], F32) for _ in range(2)]
        b_ob = [Buf() for _ in range(2)]
        sqh = [AR.alloc([256], BF16) for _ in range(2)]
        b_sqh = [Buf() for _ in range(2)]
        rsd = [AR.alloc([256], F32) for _ in range(2)]
        b_rsd = [Buf() for _ in range(2)]
        oT = [AR.alloc([NH, 256], BF16) for _ in range(2)]
        b_oT = [Buf() for _ in range(2)]
        rc = ResCtx(l, 1, o_banks=[6])
        qblocks = []
        if ctx_out:
            qblocks.append((0, NCT, list(range(NCT)), 1))
        for b in range(SEQ // 256):
            qblocks.append((NCT + 2 * b, 2, list(range(NT_ALL)), 0))
        its = []
        for qb_i, (first, n, kcs, m) in enumerate(qblocks):
            for h in range(NH):
                for ki in range(len(kcs)):
                    its.append((qb_i, h, ki))
        st_info = {}
        scnt = [0]
        pcnt = [0]
        fcnt = [0]
        deferred = []
        dseq = [0]

        def defer(due, fn):
            deferred.append((due, dseq[0], fn))
            dseq[0] += 1

        def run_deferred(upto):
            deferred.sort(key=lambda t: (t[0], t[1]))
            while deferred and deferred[0][0] <= upto:
                _, _, fn = deferred.pop(0)
                fn()

        def S_half(i, c):
            qb_i, h, ki = its[i]
            first, n, kcs, m = qblocks[qb_i]
            kc = kcs[ki]
            qs = qb_i % 2
            if c == 0 and h == 0 and ki == 0:
                P.dma("sp", QT[qs][:, 0:n], QT_d[first:first + n].rearrange("t p (h k) -> p t h k", k=128), ("qt", qs),
                      W=[b_QT[qs]])
            pair = i % 2
            P.mm(bank(2 * pair + c)[:, 0:n * 128].rearrange("p (t k) -> p t k", k=128),
                 KT[64 * c:64 * (c + 1), kc, h, :], QT[qs][64 * c:64 * (c + 1), 0:n, h, :], True, True,
                 R=[b_KT[kc], b_QT[qs]], W=[pbuf[2 * pair + c]])
            if c == 1:
                pi = i % NPT
                sview = ps_all[:, (2 * pair) * 512:(2 * pair + 2) * 512].rearrange("p (c x) -> p c x", c=2)[:, :, 0:256]
                P.act(PT[pi].rearrange("p (c x) -> p c x", c=2), sview, AF.Exp, scale=HD ** -0.5,
                      R=[pbuf[2 * pair], pbuf[2 * pair + 1]], W=[b_PT[pi]])

        def AV_part(i, part):
            qb_i, h, ki = its[i]
            first, n, kcs, m = qblocks[qb_i]
            kc = kcs[ki]
            pi = i % NPT
            lastk = ki == len(kcs) - 1
            if part == 0:
                P.mm(bank(4), VA[:, kc, h, 0:128], PT[pi], ki == 0, lastk, R=[b_VA[kc], b_PT[pi]], W=[pbuf[4]])
                return
            P.mm(bank(5), onesb, PT[pi], ki == 0, lastk, R=[b_onesb, b_PT[pi]], W=[pbuf[5]])
            if lastk:
                os_ = qb_i % 2
                i2 = fcnt[0] % 2
                fcnt[0] += 1
                P.recip(rinv[i2], bank(5), R=[pbuf[5]], W=[b_rinv[i2]])
                P.tt("dve", tq[i2], bank(4), rinv[i2], ALU.mult, R=[pbuf[4], b_rinv[i2]], W=[b_tq[i2]])
                P.stt("dve", ob[i2], tq[i2][:, 256:512], lv[:, 5:6], tq[i2][:, 0:256], ALU.mult, ALU.add,
                      R=[b_tq[i2], b_lv], W=[b_ob[i2]])
                P.act(sqh[i2], ob[i2], AF.Square, R=[b_ob[i2]], W=[b_sqh[i2]])

                def fin_b(i2=i2, h=h, os_=os_):
                    P.mm(bank(7)[:, 0:256], onesb, sqh[i2], True, True, R=[b_onesb, b_sqh[i2]], W=[pbuf[7]])
                    P.ts("dve", rsd[i2], bank(7)[:, 0:256], 1.0 / 128, EPS, ALU.mult, ALU.add, R=[pbuf[7]], W=[b_rsd[i2]])
                    P.rsqrt(rsd[i2], rsd[i2], R=[b_rsd[i2]], W=[b_rsd[i2]], lnexp=True)
                    P.stt("dve", oT[os_][:, h, :], ob[i2], lv[:, 6:7], rsd[i2], ALU.mult, ALU.mult,
                          R=[b_ob[i2], b_lv, b_rsd[i2]], W=[b_oT[os_]])

                defer(i + 2, fin_b)
                if h == NH - 1:
                    def fin_q(first=first, n=n, m=m, os_=os_):
                        rc.set_gate(m)
                        for tt in range(n):
                            xi = rc.begin_tile(first + tt, False)
                            for dc in range(2):
                                pb = rc.next_obank()
                                for hh in range(NH):
                                    P.mm(bank(pb), oT[os_][:, hh, tt * 128:(tt + 1) * 128], wo[:, hh, dc * 512:(dc + 1) * 512],
                                         hh == 0, hh == NH - 1, R=[b_oT[os_], b_wo], W=[pbuf[pb]])
                                rc.update(xi, pb, dc)
                            rc.end_tile(xi, first + tt, False)

                    defer(i + 6, fin_q)

        nit = len(its)
        S_half(0, 0)
        P.mm(bank(7)[:, 0:128], ident, ident, True, True, R=[b_ident], W=[pbuf[7]])
        S_half(0, 1)
        for i in range(nit):
            if i + 1 < nit:
                S_half(i + 1, 0)
            AV_part(i, 0)
            if i + 1 < nit:
                S_half(i + 1, 1)
            AV_part(i, 1)
            run_deferred(i)
        run_deferred(10 ** 9)
        P.barrier()
        AR.release(m0)

    def dwconv_segments(with_ctx):
        return ([(0, NCT, 1)] if with_ctx else []) + [(NCT, SEQ // 128, 0)]

    def phase_conf(l, ctx_out):
        PAD = CK // 2
        m0 = AR.mark()
        uL = AR.alloc([8, SEQ + 2 * PAD], BF16)
        uC = AR.alloc([8, CTX + 2 * PAD], BF16)
        b_u = Buf()
        for c in range(8):
            P.memset("dve", uL[:, c, 0:PAD], 0.0, W=[b_u])
            P.memset("dve", uL[:, c, PAD + SEQ:], 0.0, W=[b_u])
            P.memset("dve", uC[:, c, 0:PAD], 0.0, W=[b_u])
            P.memset("dve", uC[:, c, PAD + CTX:], 0.0, W=[b_u])
        blocks = _tiles_ffn(ctx_out)
        m1 = AR.mark()
        w_in = AR.alloc([8, 2 * D], BF16)
        b_win = [Buf() for _ in range(4)]
        wsrc = conv_w_in.rearrange("(k p) n -> p k n", p=128)
        for g in range(4):
            P.dma("pool", w_in[:, :, g * 256:(g + 1) * 256], wsrc[:, :, g * 256:(g + 1) * 256], ("w", 2 * g), W=[b_win[g]])
            P.dma("pool", w_in[:, :, D + g * 256:D + (g + 1) * 256], wsrc[:, :, D + g * 256:D + (g + 1) * 256], ("w", 2 * g + 1),
                  W=[b_win[g]])
        binF = AR.alloc([16], F32)
        b_bin = Buf()
        P.dma("sp", binF, conv_b_inF, ("misc", 0), W=[b_bin])
        nctx = NormCtx(ptr_bank=6)
        hT = AR.alloc([8, 512], BF16)
        b_hT = Buf()
        sg = [AR.alloc([512], F32) for _ in range(2)]
        b_sg = [Buf() for _ in range(2)]
        for (first, n, is_ctx) in blocks:
            m = 1 if is_ctx else 0
            N = n * 128
            pb = nctx.ptr_bank
            ptv = bank_bf(pb)
            for tt in range(n):
                ys = nctx.load_norm(src_tile_ap(l, first + tt, 1, False)[:, 0, :])
                for kc in range(8):
                    P.tr(ptv[:, kc * 128:(kc + 1) * 128], nctx.y[ys][:, kc * 128:(kc + 1) * 128], ident,
                         R=[nctx.b_y[ys], b_ident], W=[pbuf[pb]])
                for kc in range(8):
                    P.act(hT[:, kc, tt * 128:(tt + 1) * 128], ptv[:, kc * 128:(kc + 1) * 128], AF.Identity,
                          bias=modF[:, 3 * 8 + kc, m:m + 1], scale=ATab[:, 1, m, kc:kc + 1],
                          R=[pbuf[pb], b_modF, b_ATab], W=[b_hT])
            ubuf = uC if is_ctx else uL
            t0 = PAD + (first * 128 if is_ctx else (first - NCT) * 128)
            for c in range(8):
                g = c // 2
                pa = (c % 2) * 2
                pg = pa + 1
                for kc in range(8):
                    P.mm(bank(pa)[:, 0:N], w_in[:, kc, c * 128:(c + 1) * 128], hT[:, kc, 0:N], kc == 0, kc == 7,
                         R=[b_win[g], b_hT], W=[pbuf[pa]])
                for kc in range(8):
                    P.mm(bank(pg)[:, 0:N], w_in[:, kc, D + c * 128:D + (c + 1) * 128], hT[:, kc, 0:N], kc == 0, kc == 7,
                         R=[b_win[g], b_hT], W=[pbuf[pg]])
                i = c % 2
                P.act(sg[i][:, 0:N], bank(pg)[:, 0:N], AF.Sigmoid, bias=binF[:, 8 + c:9 + c], R=[pbuf[pg], b_bin], W=[b_sg[i]])
                P.stt("dve", ubuf[:, c, t0:t0 + N], bank(pa)[:, 0:N], binF[:, c:c + 1], sg[i][:, 0:N], ALU.add, ALU.mult,
                      R=[pbuf[pa], b_sg[i], b_bin], W=[b_u])
        P.barrier()
        AR.release(m1)
        NB = 256
        dg = AR.alloc([8, CK, 128], BF16)
        b_dg = Buf()
        dwF = AR.alloc([CK, 8], F32)
        vecF = AR.alloc([3, 8], F32)
        b_dw = Buf()
        P.dma("sp", dwF, conv_dwF, ("misc", 0), W=[b_dw])
        P.dma("sp", vecF, conv_vecF, ("misc", 1), W=[b_dw])
        for c in range(8):
            for k in range(CK):
                P.ts("dve", dg[:, c, k, :], identf, dwF[:, k, c:c + 1], None, ALU.mult, R=[b_ident, b_dw], W=[b_dg])
        w_out = AR.alloc([8, D], BF16)
        b_wout = Buf()
        P.dma("pool", w_out, conv_w_out.rearrange("(k p) n -> p k n", p=128), ("w", 0), W=[b_wout])
        bo = AR.alloc([D], BF16)
        b_bo = Buf()
        P.dma("pool", bo[0:1, :], conv_b_out, ("w", 1), W=[b_bo])
        om = AR.alloc([128], BF16)
        b_om = Buf()
        P.memset("dve", om, 1.0 / D, W=[b_om])
        vT = AR.alloc([8, NB], F32)
        b_vT = Buf()
        vb = AR.alloc([8, NB], BF16)
        b_vb = Buf()
        v2 = AR.alloc([8, NB], BF16)
        b_v2 = Buf()
        mr = AR.alloc([3, NB], F32)
        b_mr = Buf()
        zt = [AR.alloc([NB], F32) for _ in range(2)]
        b_zt = [Buf() for _ in range(2)]
        sT = AR.alloc([8, NB], BF16)
        b_sT = Buf()
        rc = ResCtx(l, 1, o_banks=[6, 7])
        segs = ([(0, CTX, uC, 1)] if ctx_out else []) + [(NCT, SEQ, uL, 0)]
        ccnt = [0]
        for (ft, ntok, ubuf, m) in segs:
            rc.set_gate(m)
            for b0 in range(0, ntok, NB):
                for c in range(8):
                    pb = ccnt[0] % 2
                    ccnt[0] += 1
                    for k in range(CK):
                        P.mm(bank(pb)[:, 0:NB], dg[:, c, k, :], ubuf[:, c, b0 + k:b0 + k + NB], k == 0, k == CK - 1,
                             R=[b_dg, b_u], W=[pbuf[pb]])
                    P.act(vT[:, c, :], bank(pb)[:, 0:NB], AF.Identity, bias=vecF[:, 0, c:c + 1], R=[pbuf[pb], b_dw], W=[b_vT])
                    P.cp("dve", vb[:, c, :], vT[:, c, :], R=[b_vT], W=[b_vb])
                    P.tt("dve", v2[:, c, :], vT[:, c, :], vT[:, c, :], ALU.mult, R=[b_vT], W=[b_v2])
                for c in range(8):
                    P.mm(bank(2)[:, 0:NB], om, vb[:, c, :], c == 0, c == 7, R=[b_om, b_vb], W=[pbuf[2]])
                for c in range(8):
                    P.mm(bank(3)[:, 0:NB], om, v2[:, c, :], c == 0, c == 7, R=[b_om, b_v2], W=[pbuf[3]])
                P.cp("dve", mr[:, 0, :], bank(2)[:, 0:NB], R=[pbuf[2]], W=[b_mr])
                P.tt("dve", mr[:, 2, :], mr[:, 0, :], mr[:, 0, :], ALU.mult, R=[b_mr], W=[b_mr])
                P.tt("dve", mr[:, 1, :], bank(3)[:, 0:NB], mr[:, 2, :], ALU.subtract, R=[pbuf[3], b_mr], W=[b_mr])
                P.ts("dve", mr[:, 1, :], mr[:, 1, :], EPS, None, ALU.add, R=[b_mr], W=[b_mr])
                P.rsqrt(mr[:, 1, :], mr[:, 1, :], R=[b_mr], W=[b_mr])
                for c in range(8):
                    zi = c % 2
                    P.tt("dve", zt[zi], vT[:, c, :], mr[:, 0, :], ALU.subtract, R=[b_vT, b_mr], W=[b_zt[zi]])
                    P.tt("dve", zt[zi], zt[zi], mr[:, 1, :], ALU.mult, R=[b_zt[zi], b_mr], W=[b_zt[zi]])
                    P.act(sT[:, c, :], zt[zi], AF.Silu, bias=vecF[:, 2, c:c + 1], scale=vecF[:, 1, c:c + 1],
                          R=[b_zt[zi], b_dw], W=[b_sT])
                for tt in range(NB // 128):
                    tile = ft + b0 // 128 + tt
                    i = rc.begin_tile(tile, False)
                    for dc in range(2):
                        pb = rc.next_obank()
                        for c in range(8):
                            P.mm(bank(pb), sT[:, c, tt * 128:(tt + 1) * 128], w_out[:, c, dc * 512:(dc + 1) * 512], c == 0, False,
                                 R=[b_sT, b_wout], W=[pbuf[pb]])
                        P.mm(bank(pb), ones_row[0:1, :], bo[0:1, dc * 512:(dc + 1) * 512], False, True,
                             R=[b_ones, b_bo], W=[pbuf[pb]])
                        rc.update(i, pb, dc)
                    rc.end_tile(i, tile, False)
        P.barrier()
        AR.release(m0)

    def phase_sconv(l, ctx_out):
        PAD = SK // 2
        m0 = AR.mark()
        pL = AR.alloc([8, SEQ + 2 * PAD], BF16)
        pC = AR.alloc([8, CTX + 2 * PAD], BF16)
        b_p = Buf()
        for c in range(8):
            P.memset("dve", pL[:, c, 0:PAD], 0.0, W=[b_p])
            P.memset("dve", pL[:, c, PAD + SEQ:], 0.0, W=[b_p])
            P.memset("dve", pC[:, c, 0:PAD], 0.0, W=[b_p])
            P.memset("dve", pC[:, c, PAD + CTX:], 0.0, W=[b_p])
        blocks = _tiles_ffn(ctx_out)
        m1 = AR.mark()
        w_in = AR.alloc([8, 3 * D], BF16)
        b_win = [Buf() for _ in range(4)]
        wsrc = sc_w_in.rearrange("(k p) n -> p k n", p=128)
        for g in range(4):
            for part in range(3):
                P.dma("pool", w_in[:, :, part * D + g * 256:part * D + (g + 1) * 256],
                      wsrc[:, :, part * D + g * 256:part * D + (g + 1) * 256], ("w", 3 * g + part), W=[b_win[g]])
        nctx = NormCtx(ptr_bank=6)
        hT = AR.alloc([8, 512], BF16)
        b_hT = Buf()
        xh = [AR.alloc([512], F32) for _ in range(2)]
        b_xh = [Buf() for _ in range(2)]
        bT = [AR.alloc([8, 128], BF16) for _ in range(8)]
        b_bT = [Buf() for _ in range(8)]
        btc = [0]
        for (first, n, is_ctx) in blocks:
            m = 1 if is_ctx else 0
            N = n * 128
            pb = nctx.ptr_bank
            ptv = bank_bf(pb)
            for tt in range(n):
                ys = nctx.load_norm(src_tile_ap(l, first + tt, 1, False)[:, 0, :])
                for kc in range(8):
                    P.tr(ptv[:, kc * 128:(kc + 1) * 128], nctx.y[ys][:, kc * 128:(kc + 1) * 128], ident,
                         R=[nctx.b_y[ys], b_ident], W=[pbuf[pb]])
                for kc in range(8):
                    P.act(hT[:, kc, tt * 128:(tt + 1) * 128], ptv[:, kc * 128:(kc + 1) * 128], AF.Identity,
                          bias=modF[:, 3 * 8 + kc, m:m + 1], scale=ATab[:, 1, m, kc:kc + 1],
                          R=[pbuf[pb], b_modF, b_ATab], W=[b_hT])
            pbuf_ = pC if is_ctx else pL
            t0 = PAD + (first * 128 if is_ctx else (first - NCT) * 128)
            slots = []
            for tt in range(n):
                slots.append(btc[0] % 8)
                btc[0] += 1
            for c in range(8):
                g = c // 2
                base = (c % 2) * 3
                for part in range(3):
                    for kc in range(8):
                        P.mm(bank(base + part)[:, 0:N], w_in[:, kc, part * D + c * 128:part * D + (c + 1) * 128], hT[:, kc, 0:N],
                             kc == 0, kc == 7, R=[b_win[g], b_hT], W=[pbuf[base + part]])
                i = c % 2
                P.act(xh[i][:, 0:N], bank(base + 2)[:, 0:N], AF.Copy, R=[pbuf[base + 2]], W=[b_xh[i]])
                P.tt("dve", pbuf_[:, c, t0:t0 + N], bank(base + 1)[:, 0:N], xh[i][:, 0:N], ALU.mult,
                     R=[pbuf[base + 1], b_xh[i]], W=[b_p])
                for tt in range(n):
                    P.act(bT[slots[tt]][:, c, :], bank(base)[:, tt * 128:(tt + 1) * 128], AF.Copy, R=[pbuf[base]],
                          W=[b_bT[slots[tt]]])
            for tt in range(n):
                P.dma("sp", bT_d[first + tt].rearrange("p (c t) -> p c t", t=128), bT[slots[tt]], ("bst", slots[tt]),
                      R=[b_bT[slots[tt]]])
        P.barrier()
        AR.release(m1)
        dg = AR.alloc([8, SK, 128], BF16)
        b_dg = Buf()
        dwF = AR.alloc([SK, 8], F32)
        b_dw = Buf()
        P.dma("sp", dwF, sc_dwF, ("misc", 0), W=[b_dw])
        for c in range(8):
            for k in range(SK):
                P.ts("dve", dg[:, c, k, :], identf, dwF[:, k, c:c + 1], None, ALU.mult, R=[b_ident, b_dw], W=[b_dg])
        w_out = AR.alloc([8, D], BF16)
        b_wout = Buf()
        P.dma("pool", w_out, sc_w_out.rearrange("(k p) n -> p k n", p=128), ("w", 0), W=[b_wout])
        bTl = [AR.alloc([4, 8, 128], BF16) for _ in range(2)]
        b_bTl = [Buf() for _ in range(2)]
        yT = AR.alloc([8, 512], BF16)
        b_yT = Buf()
        rc = ResCtx(l, 1, o_banks=[6, 7])
        ccnt = [0]
        for bi, (first, n, is_ctx) in enumerate(blocks):
            m = 1 if is_ctx else 0
            N = n * 128
            rc.set_gate(m)
            pbuf_ = pC if is_ctx else pL
            b0 = first * 128 if is_ctx else (first - NCT) * 128
            bs = bi % 2
            P.dma("sp", bTl[bs][:, 0:n], bT_d[first:first + n].rearrange("t p (c k) -> p t c k", k=128), ("btl", bs),
                  W=[b_bTl[bs]])
            for c in range(8):
                pb = ccnt[0] % 4
                ccnt[0] += 1
                for k in range(SK):
                    P.mm(bank(pb)[:, 0:N], dg[:, c, k, :], pbuf_[:, c, b0 + k:b0 + k + N], k == 0, k == SK - 1,
                         R=[b_dg, b_p], W=[pbuf[pb]])
                P.tt("dve", yT[:, c, 0:N].rearrange("p (t k) -> p t k", k=128), bank(pb)[:, 0:N].rearrange("p (t k) -> p t k", k=128),
                     bTl[bs][:, 0:n, c, :], ALU.mult, R=[pbuf[pb], b_bTl[bs]], W=[b_yT])
            for tt in range(n):
                i = rc.begin_tile(first + tt, False)
                for dc in range(2):
                    pb = rc.next_obank()
                    for c in range(8):
                        P.mm(bank(pb), yT[:, c, tt * 128:(tt + 1) * 128], w_out[:, c, dc * 512:(dc + 1) * 512], c == 0, c == 7,
                             R=[b_yT, b_wout], W=[pbuf[pb]])
                    rc.update(i, pb, dc)
                rc.end_tile(i, first + tt, False)
        P.barrier()
        AR.release(m0)

    for l in range(n_layers):
        kind = l % 3
        j = l // 3
        last = l == DEPTH - 1
        ctx_in_needed = (not last) or kind == 0
        ctx_out_needed = not last
        phase_mods(l)
        phase_ffn(l, 0, ctx_in_needed, first_ffn=(l == 0), final=False)
        if kind == 0:
            lam_init = 0.8 - 0.6 * math.exp(-0.3 * l)
            phase_attn_qkv(l, j)
            phase_attn_core(l, j, ctx_out_needed, lam_init)
        elif kind == 1:
            phase_conf(l, ctx_out_needed)
        else:
            phase_sconv(l, ctx_out_needed)
        phase_ffn(l, 2, ctx_out_needed, first_ffn=False, final=(l == n_layers - 1))
    P.wait_all("sp", store_ops)
    P.emit()
    return nc, AR.peak, {e: len(q) for e, q in P.q.items()}


def _fm(v, nchunk):
    return np.ascontiguousarray(np.asarray(v, np.float32).reshape(nchunk, 128).T)


def _rope_tables():
    rows = SEQ // 64
    row_ids = np.repeat(np.arange(rows, dtype=np.float32), 64)
    col_ids = np.tile(np.arange(64, dtype=np.float32), rows)
    half = HD // 2
    inv_freq = (np.float32(10000.0) ** (-np.arange(0, half, 2, dtype=np.float32) / np.float32(half))).astype(np.float32)
    ang_r = row_ids[:, None] * inv_freq
    ang_c = col_ids[:, None] * inv_freq
    ang = np.concatenate([ang_r, ang_r, ang_c, ang_c], axis=-1)
    cos = np.cos(ang).astype(np.float32)
    sin = np.sin(ang).astype(np.float32)
    sgn = np.concatenate([-np.ones(16), np.ones(16), -np.ones(16), np.ones(16)]).astype(np.float32)
    sin_s = sin * sgn
    to = lambda t: np.ascontiguousarray(t.reshape(SEQ // 128, 128, 64).transpose(1, 0, 2))
    return to(cos), to(sin_s)


def _swap_idx():
    return np.concatenate([np.arange(16, 32), np.arange(0, 16), np.arange(48, 64), np.arange(32, 48)])


def make_in_maps(inputs):
    f = lambda k: np.asarray(inputs[k], np.float32)
    x, c, ctx, c_ctx = f("x"), f("c"), f("ctx"), f("c_ctx")
    B = x.shape[0]
    cos, sin_s = _rope_tables()
    norm_g = f("norm_g")
    norm_gF = np.ascontiguousarray(norm_g.reshape(DEPTH, 3, 8, 128).transpose(3, 0, 1, 2))
    qg, kg = f("attn_q_g"), f("attn_k_g")
    sw = _swap_idx()
    ag = np.stack([qg, kg, qg[:, sw], kg[:, sw]], axis=1)
    attn_g = np.ascontiguousarray(np.broadcast_to(ag[None], (128, 2, 4, 64)))
    attn_lam = np.ascontiguousarray(np.broadcast_to(f("attn_lambda")[None], (128, 2, 4, 64)))
    attn_sublnF = np.ascontiguousarray(f("attn_subln_g").T)
    conv_b_inF = _fm(f("conv_b_in")[0], 16)
    conv_dwF = np.ascontiguousarray(f("conv_dw_w")[0].reshape(CK, 8, 128).transpose(2, 0, 1))
    conv_vecF = np.ascontiguousarray(np.stack([_fm(f("conv_dw_b")[0], 8), _fm(f("conv_ln_g")[0], 8), _fm(f("conv_ln_b")[0], 8)], axis=1))
    sc_dwF = np.ascontiguousarray(f("sc_dw_w")[0].reshape(SK, 8, 128).transpose(2, 0, 1))
    shared = {
        "ada_w": f("ada_w"), "ada_b": f("ada_b").reshape(DEPTH, 1, NMOD * D), "norm_gF": norm_gF,
        "ffn_w_in": f("ffn_w_in"), "ffn_w_out": f("ffn_w_out"),
        "attn_w_qkv": f("attn_w_qkv"), "attn_w_o": f("attn_w_o"), "attn_g": attn_g, "attn_lam": attn_lam,
        "attn_sublnF": attn_sublnF, "rope_cos": cos, "rope_sin": sin_s,
        "conv_w_in": f("conv_w_in")[0], "conv_b_inF": conv_b_inF, "conv_dwF": conv_dwF, "conv_vecF": conv_vecF,
        "conv_w_out": f("conv_w_out")[0], "conv_b_out": f("conv_b_out").reshape(1, D),
        "sc_w_in": f("sc_w_in")[0], "sc_dwF": sc_dwF, "sc_w_out": f("sc_w_out")[0],
    }
    maps = []
    for b in range(B):
        cv = np.ascontiguousarray(np.stack([c[b].reshape(8, 128).T, c_ctx.reshape(8, 128).T], axis=-1))
        mp = dict(shared)
        mp.update({"x": np.ascontiguousarray(x[b]), "ctx": np.ascontiguousarray(ctx[b]), "cvec": cv})
        maps.append(mp)
    return maps


def run(inputs, n_layers=DEPTH, debug=False, trace=False):
    nc, peak, counts = build_program(n_layers=n_layers, debug=debug)
    maps = make_in_maps(inputs)
    res = run_bass_kernel_spmd(nc, maps, core_ids=list(range(len(maps))), trace=trace)
    out = np.stack([r["out"] for r in res.results], axis=0)
    if debug:
        return out, res
    return out


def kernel(**inputs):
    return run(inputs).astype(np.float32)
```

```python
import math
import os
import numpy as np
import concourse.bass as bass
import concourse.mybir as mybir
from concourse.bass_utils import run_bass_kernel_spmd

F32 = mybir.dt.float32
BF16 = mybir.dt.bfloat16
AF = mybir.ActivationFunctionType
ALU = mybir.AluOpType
AX = mybir.AxisListType

D = 1024
DEPTH = 4
SEQ = 4096
CTX = 256
NH = 8
HD = 64
FF = 2816
NMOD = 9
EPS = 1e-6
CK = 31
SK = 3
NT_ALL = (SEQ + CTX) // 128
NCT = CTX // 128
VW = 130

DSZ = {F32: 4, BF16: 2}
ATTACH_WAITS = False


class Op:
    __slots__ = ("eng", "fn", "deps", "sig", "count", "idx", "key", "order")


class Buf:
    __slots__ = ("w", "r", "pr")

    def __init__(self):
        self.w = {}
        self.r = {}
        self.pr = {}


class Prog:
    ENG = ("pe", "act", "dve", "pool", "sp")
    GAP = 4

    def __init__(self, nc):
        self.nc = nc
        self.q = {e: [] for e in self.ENG}
        self.esem = {e: nc.alloc_semaphore("sem_" + e) for e in ("pe", "act", "dve", "pool")}
        self.ksem = {}
        self.kcnt = {}
        self.klast = {}
        self.pending = {e: {} for e in self.ENG}

    @staticmethod
    def _sid(o):
        return ("k", o.key) if o.key is not None else ("e", o.eng)

    def _mk(self, eng, fn, R=(), W=(), deps=(), key=None):
        op = Op()
        op.eng = eng
        op.fn = fn
        op.sig = False
        op.key = key
        op.idx = len(self.q[eng])
        op.count = None
        d = {}

        def add(o):
            k = self._sid(o)
            if k not in d or o.order > d[k].order:
                d[k] = o

        for o in deps:
            add(o)
        for o in self.pending[eng].values():
            add(o)
        self.pending[eng] = {}
        for b in R:
            for o in b.w.values():
                add(o)
        for b in W:
            for o in b.r.values():
                add(o)
            for o in b.pr.values():
                add(o)
        if key is not None:
            if key not in self.ksem:
                self.ksem[key] = self.nc.alloc_semaphore("k_" + "_".join(str(x) for x in key))
                self.kcnt[key] = 0
            self.kcnt[key] += 16
            op.count = self.kcnt[key]
            op.order = op.count
            self.klast[key] = op
        else:
            op.order = op.idx
        for o in d.values():
            o.sig = True
        op.deps = list(d.values())
        sid = self._sid(op)
        for b in R:
            b.r[sid] = op
        for b in W:
            if b.r and not any(b is rb for rb in R):
                b.pr = b.r
                b.r = {}
                b.w = {}
            elif any(b is rb for rb in R):
                b.pr = {k: v for k, v in b.r.items() if v is not op}
                b.r = {}
                b.w = {}
            b.w[sid] = op
        self.q[eng].append(op)
        return op

    def barrier(self):
        last = {}
        for e in ("pe", "act", "dve", "pool"):
            for o in reversed(self.q[e]):
                if o.key is None and o.fn is not None:
                    last[("e", e)] = o
                    o.sig = True
                    break
        for k, o in self.klast.items():
            last[("k", k)] = o
        self.klast = {}
        for e in self.ENG:
            self.pending[e] = dict(last)

    def mm(self, out, lhsT, rhs, start, stop, R=(), W=(), skip=False):
        if skip:
            return self._mk("pe", lambda e: e.matmul(out, lhsT=lhsT, rhs=rhs, start=start, stop=stop, skip_group_check=True), R, W)
        return self._mk("pe", lambda e: e.matmul(out, lhsT=lhsT, rhs=rhs, start=start, stop=stop), R, W)

    def tr(self, out, in_, ident, R=(), W=()):
        return self._mk("pe", lambda e: e.transpose(out=out, in_=in_, identity=ident), R, W)

    def act(self, out, in_, func, bias=None, scale=None, R=(), W=()):
        kw = {}
        if bias is not None:
            kw["bias"] = bias
        if scale is not None:
            kw["scale"] = scale
        return self._mk("act", lambda e: e.activation(out=out, in_=in_, func=func, **kw), R, W)

    def ts(self, eng, out, in0, s1, s2, op0, op1=None, R=(), W=()):
        if op1 is None:
            return self._mk(eng, lambda e: e.tensor_scalar(out=out, in0=in0, scalar1=s1, scalar2=None, op0=op0), R, W)
        return self._mk(eng, lambda e: e.tensor_scalar(out=out, in0=in0, scalar1=s1, scalar2=s2, op0=op0, op1=op1), R, W)

    def tt(self, eng, out, in0, in1, op, R=(), W=()):
        return self._mk(eng, lambda e: e.tensor_tensor(out=out, in0=in0, in1=in1, op=op), R, W)

    def stt(self, eng, out, in0, scalar, in1, op0, op1, R=(), W=()):
        return self._mk(eng, lambda e: e.scalar_tensor_tensor(out=out, in0=in0, scalar=scalar, in1=in1, op0=op0, op1=op1), R, W)

    def red(self, eng, out, in_, R=(), W=()):
        return self._mk(eng, lambda e: e.reduce_sum(out=out, in_=in_, axis=AX.X), R, W)

    def cp(self, eng, out, in_, R=(), W=()):
        return self._mk(eng, lambda e: e.tensor_copy(out=out, in_=in_), R, W)

    def rsqrt(self, out, in_, R=(), W=(), lnexp=False):
        if lnexp:
            self._mk("act", lambda e: e.activation(out=out, in_=in_, func=AF.Ln), list(R), list(W))
            return self._mk("act", lambda e: e.activation(out=out, in_=out, func=AF.Exp, scale=-0.5), list(W), list(W))
        self._mk("act", lambda e: e.activation(out=out, in_=in_, func=AF.Sqrt), list(R), list(W))
        return self._mk("dve", lambda e: e.reciprocal(out=out, in_=out), list(W), list(W))

    def recip(self, out, in_, R=(), W=()):
        return self._mk("dve", lambda e: e.reciprocal(out=out, in_=in_), R, W)

    def memset(self, eng, ap, val, R=(), W=()):
        return self._mk(eng, lambda e: e.memset(ap, val), R, W)

    def dma(self, q, out, in_, key, R=(), W=()):
        return self._mk(q, lambda e: e.dma_start(out=out, in_=in_), R, W, key=key)

    def wait_all(self, eng, ops):
        return self._mk(eng, None, deps=ops)

    def emit(self):
        nc = self.nc
        for e in ("pe", "act", "dve", "pool"):
            c = 0
            for o in self.q[e]:
                if o.key is None and o.sig:
                    c += 1
                    o.count = c
        handles = {}

        def flush(ename, e):
            waited = {}
            for o in self.q[ename]:
                need = []
                for dpn in o.deps:
                    if dpn.key is not None:
                        sem = self.ksem[dpn.key]
                        sid = ("k", dpn.key)
                    else:
                        if dpn.eng == ename:
                            if ename == "pe":
                                continue
                            if ename != "pool" and o.idx - dpn.idx > self.GAP:
                                continue
                        sem = self.esem[dpn.eng]
                        sid = ("e", dpn.eng)
                    if waited.get(sid, 0) >= dpn.count:
                        continue
                    need.append((sem, dpn.count))
                    waited[sid] = dpn.count
                if o.fn is None:
                    for sem, cnt in need:
                        e.wait_ge(sem, cnt)
                    continue
                attach = ATTACH_WAITS and o.key is None and ename in ("pe", "act", "dve")
                for sem, cnt in (need[:-1] if attach else need):
                    e.wait_ge(sem, cnt)
                ins = o.fn(e)
                if need and attach:
                    ins._wait_ge(need[-1][0], need[-1][1])
                if o.key is not None:
                    ins.then_inc(self.ksem[o.key], 16)
                elif o.sig:
                    ins.then_inc(self.esem[ename], 1)

        with nc.Block() as block:
            @block.tensor
            def _(e):
                flush("pe", e)

            @block.scalar
            def _(e):
                flush("act", e)

            @block.vector
            def _(e):
                flush("dve", e)

            @block.gpsimd
            def _(e):
                flush("pool", e)

            @block.sync
            def _(e):
                flush("sp", e)


class Arena:
    def __init__(self, nc, nbytes):
        self.t = nc.alloc_sbuf_tensor("arena", [128, nbytes // 4], F32)
        self.cap = nbytes
        self.off = 0
        self.peak = 0

    def alloc(self, shape, dtype):
        n = 1
        for s in shape:
            n *= s
        nb = (n * DSZ[dtype] + 31) // 32 * 32
        assert self.off + nb <= self.cap, f"arena overflow: {self.off + nb} > {self.cap}"
        v = self.t[:, self.off // 4:(self.off + nb) // 4]
        if dtype != F32:
            v = v.bitcast(dtype)
        v = v[:, 0:n]
        self.off += nb
        self.peak = max(self.peak, self.off)
        if len(shape) == 2:
            v = v.rearrange("p (a b) -> p a b", b=shape[1])
        elif len(shape) == 3:
            v = v.rearrange("p (a b c) -> p a b c", b=shape[1], c=shape[2])
        elif len(shape) == 4:
            v = v.rearrange("p (a b c d) -> p a b c d", b=shape[1], c=shape[2], d=shape[3])
        return v

    def mark(self):
        return self.off

    def release(self, m):
        self.off = m


def _tiles_ffn(with_ctx):
    blocks = []
    if with_ctx:
        blocks.append((0, NCT, True))
    for b in range(SEQ // 512):
        blocks.append((NCT + 4 * b, 4, False))
    return blocks


def build_program(n_layers=DEPTH, debug=False):
    nc = bass.Bass("TRN2", target_bir_lowering=False)

    def din(name, shape, dt=F32):
        return nc.dram_tensor(name, list(shape), dt, kind="ExternalInput").ap()

    x_in = din("x", [SEQ, D])
    ctx_in = din("ctx", [CTX, D])
    cvec = din("cvec", [128, 8, 2])
    ada_w = din("ada_w", [DEPTH, D, NMOD * D])
    ada_b = din("ada_b", [DEPTH, 1, NMOD * D])
    norm_gF = din("norm_gF", [128, DEPTH, 3, 8])
    ffn_w_in = din("ffn_w_in", [DEPTH, 2, D, 2 * FF])
    ffn_w_out = din("ffn_w_out", [DEPTH, 2, FF, D])
    attn_w_qkv = din("attn_w_qkv", [2, D, 3 * D])
    attn_w_o = din("attn_w_o", [2, D, D])
    attn_g = din("attn_g", [128, 2, 4, 64])
    attn_lam = din("attn_lam", [128, 2, 4, 64])
    attn_sublnF = din("attn_sublnF", [128, 2])
    rope_cos = din("rope_cos", [128, SEQ // 128, 64])
    rope_sin = din("rope_sin", [128, SEQ // 128, 64])
    conv_w_in = din("conv_w_in", [D, 2 * D])
    conv_b_inF = din("conv_b_inF", [128, 16])
    conv_dwF = din("conv_dwF", [128, CK, 8])
    conv_vecF = din("conv_vecF", [128, 3, 8])
    conv_w_out = din("conv_w_out", [D, D])
    conv_b_out = din("conv_b_out", [1, D])
    sc_w_in = din("sc_w_in", [D, 3 * D])
    sc_dwF = din("sc_dwF", [128, SK, 8])
    sc_w_out = din("sc_w_out", [D, D])

    out = nc.dram_tensor("out", [SEQ, D], F32, kind="ExternalOutput").ap()
    skind = "ExternalOutput" if debug else "Internal"
    xs = nc.dram_tensor("xs", [NT_ALL * 128, D], F32, kind=skind).ap()
    QT_d = nc.dram_tensor("QT_d", [NT_ALL, 128, 1024], BF16, kind="Internal").ap()
    KT_d = nc.dram_tensor("KT_d", [NT_ALL, 128, 1024], BF16, kind="Internal").ap()
    V_d = nc.dram_tensor("V_d", [NT_ALL, 128, NH * VW], BF16, kind="Internal").ap()
    bT_d = nc.dram_tensor("bT_d", [NT_ALL, 128, 1024], BF16, kind="Internal").ap()
    gates_d = nc.dram_tensor("gates_d", [DEPTH, 3, 2, 128, D], F32, kind="Internal").ap()

    P = Prog(nc)
    AR = Arena(nc, 207 * 1024)
    ps_all = nc.alloc_psum_tensor("ps_all", [128, 8 * 512], F32)

    def bank(i, n=1):
        return ps_all[:, i * 512:(i + n) * 512]

    def bank_bf(i):
        return ps_all[:, i * 512:(i + 1) * 512].bitcast(BF16)

    pbuf = [Buf() for _ in range(8)]

    ident = AR.alloc([128], BF16)
    identf = AR.alloc([128], F32)
    modF = AR.alloc([72, 2], F32)
    ATab = AR.alloc([3, 2, 8], F32)
    gF = AR.alloc([DEPTH, 3, 8], F32)
    scT = AR.alloc([8, 2], BF16)
    scB = [AR.alloc([8, 128], BF16) for _ in range(2)]
    ones_row = AR.alloc([128], BF16)
    small = AR.alloc([64], F32)
    b_ident, b_modF, b_ATab, b_gF, b_sc, b_ones, b_small = (Buf() for _ in range(7))

    P.memset("pool", identf, 0.0, W=[b_ident])
    P._mk("pool", lambda e: e.affine_select(out=identf, in_=identf, pattern=[[-1, 128]], compare_op=ALU.not_equal,
                                             fill=1.0, base=0, channel_multiplier=1), R=[b_ident], W=[b_ident])
    P.cp("dve", ident, identf, R=[b_ident], W=[b_ident])
    P.memset("dve", ones_row, 1.0, W=[b_ones])
    m0 = AR.mark()
    cv = AR.alloc([8, 2], F32)
    b_cv = Buf()
    P.dma("sp", cv, cvec, ("misc", 0), W=[b_cv])
    P.dma("sp", gF, norm_gF, ("misc", 1), W=[b_gF])
    cvs = AR.alloc([8, 2], F32)
    P.act(cvs, cv, AF.Silu, R=[b_cv], W=[b_cv])
    P.cp("dve", scT, cvs, R=[b_cv], W=[b_sc])
    for m in range(2):
        P.cp("dve", scB[m], cvs[:, :, m:m + 1].to_broadcast([128, 8, 128]), R=[b_cv], W=[b_sc])
    P.barrier()
    AR.release(m0)
    PERSIST = AR.mark()

    def src_tile_ap(layer, first, n, first_ffn):
        if first_ffn:
            if first < NCT:
                return ctx_in[first * 128:(first + n) * 128, :].rearrange("(t p) d -> p t d", p=128)
            f = first - NCT
            return x_in[f * 128:(f + n) * 128, :].rearrange("(t p) d -> p t d", p=128)
        return xs[first * 128:(first + n) * 128, :].rearrange("(t p) d -> p t d", p=128)

    def dst_tile_ap(first, n, final):
        if final:
            f = first - NCT
            return out[f * 128:(f + n) * 128, :].rearrange("(t p) d -> p t d", p=128)
        return xs[first * 128:(first + n) * 128, :].rearrange("(t p) d -> p t d", p=128)

    store_ops = []

    def phase_mods(l):
        m0 = AR.mark()
        NCH = 18
        adw = [AR.alloc([8, 512], BF16) for _ in range(3)]
        b_adw = [Buf() for _ in range(3)]
        brow = AR.alloc([NMOD * D], BF16)
        b_brow = Buf()
        ones2 = AR.alloc([2], BF16)
        b_o2 = Buf()
        gsb = [AR.alloc([512], F32) for _ in range(2)]
        b_gsb = [Buf() for _ in range(2)]
        P.memset("dve", ones2, 1.0, W=[b_o2])
        P.dma("pool", brow[0:1, :], ada_b[l], ("w", 0), W=[b_brow])
        psF = bank(0)[:, 0:144].rearrange("p (f m) -> p f m", m=2)
        gi = 0
        for c in range(NCH):
            s_ = c % 3
            src = ada_w[l, :, c * 512:(c + 1) * 512].rearrange("(k p) n -> p k n", p=128)
            P.dma("pool", adw[s_], src, ("ada", s_), W=[b_adw[s_]])
            for fi in range(4):
                f = 4 * c + fi
                for kc in range(8):
                    P.mm(psF[:, f, :], adw[s_][:, kc, fi * 128:(fi + 1) * 128], scT[:, kc, :], kc == 0, False,
                         R=[b_adw[s_], b_sc], W=[pbuf[0]])
                P.mm(psF[:, f, :], brow[0:1, f * 128:(f + 1) * 128], ones2[0:1, :], False, True,
                     R=[b_brow, b_o2], W=[pbuf[0]])
            n = c // 2
            if n % 3 == 2:
                s = n // 3
                half = c % 2
                for m in range(2):
                    pb = 1 + m
                    for kc in range(8):
                        P.mm(bank(pb), scB[m][:, kc, :], adw[s_][:, kc, :], kc == 0, False,
                             R=[b_adw[s_], b_sc], W=[pbuf[pb]])
                    P.mm(bank(pb), ones_row[0:1, :], brow[0:1, c * 512:(c + 1) * 512], False, True,
                         R=[b_brow, b_ones], W=[pbuf[pb]])
                    g_ = gi % 2
                    gi += 1
                    P.act(gsb[g_], bank(pb), AF.Copy, scale=(1.0 if s == 1 else 0.5), R=[pbuf[pb]], W=[b_gsb[g_]])
                    P.dma("sp", gates_d[l, s, m, :, half * 512:(half + 1) * 512], gsb[g_], ("gst", g_), R=[b_gsb[g_]])
        P.cp("dve", modF, psF, R=[pbuf[0]], W=[b_modF])
        tmp = AR.alloc([8], F32)
        b_tmp = Buf()
        for s in range(3):
            for m in range(2):
                P.ts("dve", tmp, modF[:, (3 * s + 1) * 8:(3 * s + 2) * 8, m], 1.0, None, ALU.add, R=[b_modF], W=[b_tmp])
                P.tt("dve", ATab[:, s, m, :], tmp, gF[:, l, s, :], ALU.mult, R=[b_tmp, b_gF], W=[b_ATab])
        P.barrier()
        AR.release(m0)

    class NormCtx:
        def __init__(self, ptr_bank):
            self.xn = [AR.alloc([D], F32) for _ in range(2)]
            self.b_xn = [Buf() for _ in range(2)]
            self.sq = AR.alloc([D], F32)
            self.b_sq = Buf()
            self.y = [AR.alloc([D], BF16) for _ in range(2)]
            self.b_y = [Buf() for _ in range(2)]
            self.st = AR.alloc([2, 4], F32)
            self.b_st = [Buf() for _ in range(2)]
            self.ptr_bank = ptr_bank
            self.cnt = 0

        def load_norm(self, tile_ap):
            i = self.cnt % 2
            self.cnt += 1
            P.dma("sp", self.xn[i], tile_ap, ("xn", i), W=[self.b_xn[i]])
            P.act(self.sq, self.xn[i], AF.Square, R=[self.b_xn[i]], W=[self.b_sq])
            st = self.st[:, i, :]
            P.red("dve", st[:, 0:1], self.sq, R=[self.b_sq], W=[self.b_st[i]])
            P.ts("dve", st[:, 1:2], st[:, 0:1], 1.0 / D, EPS, ALU.mult, ALU.add, R=[self.b_st[i]], W=[self.b_st[i]])
            P.rsqrt(st[:, 2:3], st[:, 1:2], R=[self.b_st[i]], W=[self.b_st[i]])
            P.act(self.y[i], self.xn[i], AF.Identity, scale=st[:, 2:3], R=[self.b_xn[i], self.b_st[i]], W=[self.b_y[i]])
            return i

    def transpose_mod(nctx, yslots, hT, b_hT, s, m, l):
        n = len(yslots)
        pb = nctx.ptr_bank
        ptv = bank_bf(pb)
        for kc in range(8):
            half = kc % 2
            for tt, ys in enumerate(yslots):
                P.tr(ptv[:, half * 512 + tt * 128: half * 512 + (tt + 1) * 128], nctx.y[ys][:, kc * 128:(kc + 1) * 128], ident,
                     R=[nctx.b_y[ys], b_ident], W=[pbuf[pb]])
            P.act(hT[:, kc, 0:n * 128], ptv[:, half * 512: half * 512 + n * 128], AF.Identity,
                  bias=modF[:, 3 * s * 8 + kc, m:m + 1], scale=ATab[:, s, m, kc:kc + 1],
                  R=[pbuf[pb], b_modF, b_ATab], W=[b_hT])


    def residual_out(nctx_x, o_bank, pb, first_tile, tt, dc, gate, b_gate, tmpb, b_tmpb, xr, b_xr, final, idx):
        i = idx % 2
        P.tt("dve", tmpb[i], o_bank, gate[:, dc * 512:(dc + 1) * 512], ALU.mult, R=[pbuf[pb], b_gate], W=[b_tmpb[i]])
        P.tt("dve", xr[:, dc * 512:(dc + 1) * 512], xr[:, dc * 512:(dc + 1) * 512], tmpb[i], ALU.add,
             R=[b_tmpb[i], b_xr], W=[b_xr])

    class ResCtx:
        def __init__(self, l, s, o_banks):
            self.l, self.s = l, s
            self.gate = AR.alloc([D], F32)
            self.b_gate = Buf()
            self.gate_m = None
            self.xr = [AR.alloc([D], F32) for _ in range(2)]
            self.b_xr = [Buf() for _ in range(2)]
            self.tmp = [AR.alloc([512], F32) for _ in range(2)]
            self.b_tmp = [Buf() for _ in range(2)]
            self.o_banks = o_banks
            self.ocnt = 0
            self.xcnt = 0

        def set_gate(self, m):
            if self.gate_m != m:
                P.dma("sp", self.gate, gates_d[self.l, self.s, m], ("gate", 0), W=[self.b_gate])
                self.gate_m = m

        def begin_tile(self, tile, first_ffn):
            i = self.xcnt % 2
            self.xcnt += 1
            P.dma("sp", self.xr[i], src_tile_ap(self.l, tile, 1, first_ffn)[:, 0, :], ("xr", i), W=[self.b_xr[i]])
            return i

        def next_obank(self):
            pb = self.o_banks[self.ocnt % len(self.o_banks)]
            self.ocnt += 1
            return pb

        def update(self, i, pb, dc):
            j = self.ocnt % 2
            P.tt("dve", self.tmp[j], bank(pb), self.gate[:, dc * 512:(dc + 1) * 512], ALU.mult,
                 R=[pbuf[pb], self.b_gate], W=[self.b_tmp[j]])
            P.tt("dve", self.xr[i][:, dc * 512:(dc + 1) * 512], self.xr[i][:, dc * 512:(dc + 1) * 512], self.tmp[j], ALU.add,
                 R=[self.b_tmp[j], self.b_xr[i]], W=[self.b_xr[i]])

        def end_tile(self, i, tile, final):
            final = final and tile >= NCT
            op = P.dma("sp", dst_tile_ap(tile, 1, final)[:, 0, :], self.xr[i], ("xst", i), R=[self.b_xr[i]])
            if final:
                store_ops.append(op)

    def phase_ffn(l, s, with_ctx, first_ffn, final):
        wi = 0 if s == 0 else 1
        m0 = AR.mark()
        w_in = AR.alloc([8, 2 * FF], BF16)
        w_out = AR.alloc([22, D], BF16)
        NG = 11
        b_win = [Buf() for _ in range(NG)]
        b_wout = [Buf() for _ in range(2)]
        wsrc = ffn_w_in[l, wi].rearrange("(k p) n -> p k n", p=128)
        for g in range(NG):
            P.dma("pool", w_in[:, :, g * 256:(g + 1) * 256], wsrc[:, :, g * 256:(g + 1) * 256], ("w", 2 * g), W=[b_win[g]])
            P.dma("pool", w_in[:, :, FF + g * 256:FF + (g + 1) * 256], wsrc[:, :, FF + g * 256:FF + (g + 1) * 256],
                  ("w", 2 * g + 1), W=[b_win[g]])
        osrc = ffn_w_out[l, wi].rearrange("(j p) d -> p j d", p=128)
        for h in range(2):
            P.dma("pool", w_out[:, h * 11:(h + 1) * 11, :], osrc[:, h * 11:(h + 1) * 11, :], ("w", 22 + h), W=[b_wout[h]])
        nctx = NormCtx(ptr_bank=6)
        rc = ResCtx(l, s, o_banks=[4, 5])
        hT = AR.alloc([8, 512], BF16)
        b_hT = Buf()
        uT = AR.alloc([22, 512], BF16)
        b_uT = Buf()
        sg = [AR.alloc([512], F32) for _ in range(2)]
        b_sg = [Buf() for _ in range(2)]
        blocks = _tiles_ffn(with_ctx)

        def stage_T(blk):
            first, n, is_ctx = blk
            ys = []
            for tt in range(n):
                ys.append(nctx.load_norm(src_tile_ap(l, first + tt, 1, first_ffn)[:, 0, :]))
                if len(ys) == 2 or tt == n - 1:
                    pass
            return ys

        def stage_T_full(blk):
            first, n, is_ctx = blk
            m = 1 if is_ctx else 0
            pb = nctx.ptr_bank
            ptv = bank_bf(pb)
            for tt in range(n):
                ys = nctx.load_norm(src_tile_ap(l, first + tt, 1, first_ffn)[:, 0, :])
                for kc in range(8):
                    P.tr(ptv[:, kc * 128:(kc + 1) * 128], nctx.y[ys][:, kc * 128:(kc + 1) * 128], ident,
                         R=[nctx.b_y[ys], b_ident], W=[pbuf[pb]])
                for kc in range(8):
                    P.act(hT[:, kc, tt * 128:(tt + 1) * 128], ptv[:, kc * 128:(kc + 1) * 128], AF.Identity,
                          bias=modF[:, 3 * s * 8 + kc, m:m + 1], scale=ATab[:, s, m, kc:kc + 1],
                          R=[pbuf[pb], b_modF, b_ATab], W=[b_hT])

        def stage_IN(blk):
            first, n, is_ctx = blk
            N = n * 128
            for j in range(22):
                g = j // 2
                pa = (j % 2) * 2
                pg = pa + 1
                for kc in range(8):
                    P.mm(bank(pa)[:, 0:N], w_in[:, kc, j * 128:(j + 1) * 128], hT[:, kc, 0:N], kc == 0, kc == 7,
                         R=[b_win[g], b_hT], W=[pbuf[pa]])
                for kc in range(8):
                    P.mm(bank(pg)[:, 0:N], w_in[:, kc, FF + j * 128:FF + (j + 1) * 128], hT[:, kc, 0:N], kc == 0, kc == 7,
                         R=[b_win[g], b_hT], W=[pbuf[pg]])
                i = j % 2
                P.act(sg[i][:, 0:N], bank(pg)[:, 0:N], AF.Silu, R=[pbuf[pg]], W=[b_sg[i]])
                P.tt("dve", uT[:, j, 0:N], bank(pa)[:, 0:N], sg[i][:, 0:N], ALU.mult, R=[pbuf[pa], b_sg[i]], W=[b_uT])

        def stage_OUT(blk):
            first, n, is_ctx = blk
            rc.set_gate(1 if is_ctx else 0)
            for tt in range(n):
                i = rc.begin_tile(first + tt, first_ffn)
                for dc in range(2):
                    pb = rc.next_obank()
                    for j in range(22):
                        P.mm(bank(pb), uT[:, j, tt * 128:(tt + 1) * 128], w_out[:, j, dc * 512:(dc + 1) * 512], j == 0, j == 21,
                             R=[b_uT, b_wout[j // 11]], W=[pbuf[pb]])
                    rc.update(i, pb, dc)
                rc.end_tile(i, first + tt, final)

        stage_T_full(blocks[0])
        for bi, blk in enumerate(blocks):
            stage_IN(blk)
            if bi + 1 < len(blocks):
                stage_T_full(blocks[bi + 1])
            stage_OUT(blk)
        P.barrier()
        AR.release(m0)

    def phase_attn_qkv(l, j_attn):
        m0 = AR.mark()
        wq = AR.alloc([8, 3 * D], BF16)
        b_wq = [Buf() for _ in range(6)]
        wsrc = attn_w_qkv[j_attn].rearrange("(k p) n -> p k n", p=128)
        for n in range(6):
            P.dma("pool", wq[:, :, n * 512:(n + 1) * 512], wsrc[:, :, n * 512:(n + 1) * 512], ("w", n), W=[b_wq[n]])
        cosT = AR.alloc([SEQ // 128, 64], F32)
        sinT = AR.alloc([SEQ // 128, 64], F32)
        gq = AR.alloc([4, 64], F32)
        b_tab = Buf()
        P.dma("sp", cosT, rope_cos, ("misc", 0), W=[b_tab])
        P.dma("sp", sinT, rope_sin, ("misc", 1), W=[b_tab])
        P.dma("sp", gq, attn_g[:, j_attn], ("misc", 2), W=[b_tab])
        nctx = NormCtx(ptr_bank=6)
        hT = [AR.alloc([8, 128], BF16) for _ in range(2)]
        b_hT = [Buf() for _ in range(2)]
        sqb = AR.alloc([2 * D], F32)
        b_sqb = Buf()
        stg = AR.alloc([64], F32)
        b_stg = Buf()
        csk = AR.alloc([2, 2, 64], F32)
        b_csk = Buf()
        t1 = AR.alloc([2 * D], F32)
        b_t1 = Buf()
        t2 = AR.alloc([2 * D], F32)
        b_t2 = Buf()
        qkh = [AR.alloc([2 * D], BF16) for _ in range(2)]
        b_qkh = [Buf() for _ in range(2)]
        qkT = [AR.alloc([2, NH, 128], BF16) for _ in range(2)]
        b_qkT = [Buf() for _ in range(2)]
        vaug = [AR.alloc([NH, VW], BF16) for _ in range(2)]
        b_vaug = [Buf() for _ in range(2)]
        for i in range(2):
            P.memset("dve", vaug[i], 1.0, W=[b_vaug[i]])
        for jt in range(NT_ALL):
            is_ctx = jt < NCT
            m = 1 if is_ctx else 0
            ys = nctx.load_norm(src_tile_ap(l, jt, 1, False)[:, 0, :])
            hs = jt % 2
            pb = nctx.ptr_bank
            ptv = bank_bf(pb)
            for kc in range(8):
                P.tr(ptv[:, kc * 128:(kc + 1) * 128], nctx.y[ys][:, kc * 128:(kc + 1) * 128], ident,
                     R=[nctx.b_y[ys], b_ident], W=[pbuf[pb]])
            for kc in range(8):
                P.act(hT[hs][:, kc, :], ptv[:, kc * 128:(kc + 1) * 128], AF.Identity,
                      bias=modF[:, 3 * 8 + kc, m:m + 1], scale=ATab[:, 1, m, kc:kc + 1],
                      R=[pbuf[pb], b_modF, b_ATab], W=[b_hT[hs]])
            for n in range(6):
                for kc in range(8):
                    P.mm(bank(n), hT[hs][:, kc, :], wq[:, kc, n * 512:(n + 1) * 512], kc == 0, kc == 7,
                         R=[b_hT[hs], b_wq[n]], W=[pbuf[n]])
            qk_ps = bank(0, 4)
            qb = [pbuf[0], pbuf[1], pbuf[2], pbuf[3]]
            P.act(sqb, qk_ps, AF.Square, R=qb, W=[b_sqb])
            P.red("dve", stg[:, 0:32], sqb.rearrange("p (g d) -> p g d", d=64), R=[b_sqb], W=[b_stg])
            P.ts("dve", stg[:, 0:32], stg[:, 0:32], 1.0 / HD, EPS, ALU.mult, ALU.add, R=[b_stg], W=[b_stg])
            P.rsqrt(stg[:, 32:64], stg[:, 0:32], R=[b_stg], W=[b_stg])
            rs_b = stg[:, 32:64].unsqueeze(2).to_broadcast([128, 32, 64])
            qi = jt % 2
            if is_ctx:
                for w in range(2):
                    P.tt("dve", t1[:, w * D:(w + 1) * D].rearrange("p (g d) -> p g d", d=64),
                         qk_ps[:, w * D:(w + 1) * D].rearrange("p (g d) -> p g d", d=64),
                         gq[:, w:w + 1, :].to_broadcast([128, 16, 64]), ALU.mult, R=qb + [b_tab], W=[b_t1])
            else:
                jl = jt - NCT
                for w in range(2):
                    P.tt("dve", csk[:, w, 0, :], cosT[:, jl, :], gq[:, w, :], ALU.mult, R=[b_tab], W=[b_csk])
                    P.tt("dve", csk[:, w, 1, :], sinT[:, jl, :], gq[:, 2 + w, :], ALU.mult, R=[b_tab], W=[b_csk])
                for w in range(2):
                    xv = qk_ps[:, w * D:(w + 1) * D]
                    P.tt("dve", t1[:, w * D:(w + 1) * D].rearrange("p (g d) -> p g d", d=64),
                         xv.rearrange("p (g d) -> p g d", d=64),
                         csk[:, w, 0:1, :].to_broadcast([128, 16, 64]), ALU.mult, R=qb + [b_csk], W=[b_t1])
                    x5 = xv.rearrange("p (g a h d) -> p g a h d", a=2, h=2, d=16)
                    o5 = t2[:, w * D:(w + 1) * D].rearrange("p (g a h d) -> p g a h d", a=2, h=2, d=16)
                    s4 = csk[:, w, 1, :].rearrange("p (a h d) -> p a h d", a=2, h=2)
                    for hh in range(2):
                        P.tt("dve", o5[:, :, :, hh, :], x5[:, :, :, 1 - hh, :],
                             s4[:, :, hh, :].unsqueeze(1).to_broadcast([128, 16, 2, 16]), ALU.mult,
                             R=qb + [b_csk], W=[b_t2])
                P.tt("dve", t1, t1, t2, ALU.add, R=[b_t1, b_t2], W=[b_t1])
            P.tt("dve", qkh[qi].rearrange("p (g d) -> p g d", d=64), t1.rearrange("p (g d) -> p g d", d=64), rs_b, ALU.mult,
                 R=[b_t1, b_stg], W=[b_qkh[qi]])
            P.act(vaug[qi][:, :, 0:128], bank(4, 2).rearrange("p (h d) -> p h d", d=128), AF.Copy,
                  R=[pbuf[4], pbuf[5]], W=[b_vaug[qi]])
            P.dma("sp", V_d[jt].rearrange("p (h d) -> p h d", d=VW), vaug[qi], ("vst", qi), R=[b_vaug[qi]])
            ptq = bank_bf(7)
            for w in range(2):
                for h in range(NH):
                    P.tr(ptq[:, h * 128:(h + 1) * 128], qkh[qi][:, w * D + h * 128: w * D + (h + 1) * 128], ident,
                         R=[b_qkh[qi], b_ident], W=[pbuf[7]])
                P.cp("dve", qkT[qi][:, w, :, :], ptq.rearrange("p (h t) -> p h t", t=128), R=[pbuf[7]], W=[b_qkT[qi]])
                dst = (QT_d if w == 0 else KT_d)[jt].rearrange("p (h t) -> p h t", t=128)
                P.dma("sp", dst, qkT[qi][:, w, :, :], ("qkst", qi * 2 + w), R=[b_qkT[qi]])
        P.barrier()
        AR.release(m0)

    def phase_attn_core(l, j_attn, ctx_out, lam_init):
        m0 = AR.mark()
        KT = AR.alloc([NT_ALL, NH, 128], BF16)
        VA = AR.alloc([NT_ALL, NH, VW], BF16)
        b_KT = [Buf() for _ in range(NT_ALL)]
        b_VA = [Buf() for _ in range(NT_ALL)]
        wo = AR.alloc([NH, D], BF16)
        b_wo = Buf()
        grp = [(0, 2)] + [(2 + 4 * i, 4) for i in range(8)]
        for gi_, (f, n) in enumerate(grp):
            o1 = P.dma("sp", KT[:, f:f + n], KT_d[f:f + n].rearrange("t p (h k) -> p t h k", k=128), ("kv", 2 * gi_),
                       W=[b_KT[t] for t in range(f, f + n)])
            o2 = P.dma("sp", VA[:, f:f + n], V_d[f:f + n].rearrange("t p (h k) -> p t h k", k=VW), ("kv", 2 * gi_ + 1),
                       W=[b_VA[t] for t in range(f, f + n)])
        P.dma("pool", wo, attn_w_o[j_attn].rearrange("(h p) d -> p h d", p=128), ("w", 0), W=[b_wo])
        lamb = AR.alloc([4, 64], F32)
        b_lam = Buf()
        P.dma("sp", lamb, attn_lam[:, j_attn], ("misc", 0), W=[b_lam])
        lw = AR.alloc([2, 64], F32)
        b_lw = Buf()
        lv = AR.alloc([8], F32)
        b_lv = Buf()
        for i in range(2):
            P.tt("dve", lw[:, i, :], lamb[:, 2 * i, :], lamb[:, 2 * i + 1, :], ALU.mult, R=[b_lam], W=[b_lw])
        P.red("dve", lv[:, 0:2], lw, R=[b_lw], W=[b_lv])
        P.act(lv[:, 2:4], lv[:, 0:2], AF.Exp, R=[b_lv], W=[b_lv])
        P.tt("dve", lv[:, 4:5], lv[:, 2:3], lv[:, 3:4], ALU.subtract, R=[b_lv], W=[b_lv])
        P.ts("dve", lv[:, 5:6], lv[:, 4:5], lam_init, -1.0, ALU.add, ALU.mult, R=[b_lv], W=[b_lv])
        sub = AR.alloc([2], F32)
        b_sub = Buf()
        P.dma("sp", sub, attn_sublnF, ("misc", 1), W=[b_sub])
        P.ts("dve", lv[:, 6:7], sub[:, j_attn:j_attn + 1], 1.0 - lam_init, None, ALU.mult, R=[b_sub, b_lv], W=[b_lv])

        onesb = AR.alloc([128], BF16)
        b_onesb = Buf()
        P.memset("dve", onesb, 1.0, W=[b_onesb])
        QT = [AR.alloc([2, NH, 128], BF16) for _ in range(2)]
        b_QT = [Buf() for _ in range(2)]
        NPT = 4
        PT = [AR.alloc([512], BF16) for _ in range(NPT)]
        b_PT = [Buf() for _ in range(NPT)]
        rinv = [AR.alloc([512], F32) for _ in range(1)] * 2
        b_rinv = [Buf()] * 2
        tq = [AR.alloc([512], F32) for _ in range(1)] * 2
        b_tq = [Buf()] * 2
        ob = [AR.alloc([256], F32) for _ in range(2)]
        b_ob = [Buf() for _ in range(2)]
        sqh = [AR.alloc([256], BF16) for _ in range(2)]
        b_sqh = [Buf() for _ in range(2)]
        rsd = [AR.alloc([256], F32) for _ in range(2)]
        b_rsd = [Buf() for _ in range(2)]
        oT = [AR.alloc([NH, 256], BF16) for _ in range(2)]
        b_oT = [Buf() for _ in range(2)]
        rc = ResCtx(l, 1, o_banks=[6])
        qblocks = []
        if ctx_out:
            qblocks.append((0, NCT, list(range(NCT)), 1))
        for b in range(SEQ // 256):
            qblocks.append((NCT + 2 * b, 2, list(range(NT_ALL)), 0))
        its = []
        for qb_i, (first, n, kcs, m) in enumerate(qblocks):
            for h in range(NH):
                for ki in range(len(kcs)):
                    its.append((qb_i, h, ki))
        st_info = {}
        scnt = [0]
        pcnt = [0]
        fcnt = [0]
        deferred = []
        dseq = [0]

        def defer(due, fn):
            deferred.append((due, dseq[0], fn))
            dseq[0] += 1

        def run_deferred(upto):
            deferred.sort(key=lambda t: (t[0], t[1]))
            while deferred and deferred[0][0] <= upto:
                _, _, fn = deferred.pop(0)
                fn()

        def S_half(i, c):
            qb_i, h, ki = its[i]
            first, n, kcs, m = qblocks[qb_i]
            kc = kcs[ki]
            qs = qb_i % 2
            if c == 0 and h == 0 and ki == 0:
                P.dma("sp", QT[qs][:, 0:n], QT_d[first:first + n].rearrange("t p (h k) -> p t h k", k=128), ("qt", qs),
                      W=[b_QT[qs]])
            pair = i % 2
            P.mm(bank(2 * pair + c)[:, 0:n * 128].rearrange("p (t k) -> p t k", k=128),
                 KT[64 * c:64 * (c + 1), kc, h, :], QT[qs][64 * c:64 * (c + 1), 0:n, h, :], True, True,
                 R=[b_KT[kc], b_QT[qs]], W=[pbuf[2 * pair + c]])
            if c == 1:
                pi = i % NPT
                sview = ps_all[:, (2 * pair) * 512:(2 * pair + 2) * 512].rearrange("p (c x) -> p c x", c=2)[:, :, 0:256]
                P.act(PT[pi].rearrange("p (c x) -> p c x", c=2), sview, AF.Exp, scale=HD ** -0.5,
                      R=[pbuf[2 * pair], pbuf[2 * pair + 1]], W=[b_PT[pi]])

        def AV_part(i, part):
            qb_i, h, ki = its[i]
            first, n, kcs, m = qblocks[qb_i]
            kc = kcs[ki]
            pi = i % NPT
            lastk = ki == len(kcs) - 1
            if part == 0:
                P.mm(bank(4), VA[:, kc, h, 0:128], PT[pi], ki == 0, lastk, R=[b_VA[kc], b_PT[pi]], W=[pbuf[4]])
                return
            P.mm(bank(5), onesb, PT[pi], ki == 0, lastk, R=[b_onesb, b_PT[pi]], W=[pbuf[5]])
            if lastk:
                os_ = qb_i % 2
                i2 = fcnt[0] % 2
                fcnt[0] += 1
                P.recip(rinv[i2], bank(5), R=[pbuf[5]], W=[b_rinv[i2]])
                P.tt("dve", tq[i2], bank(4), rinv[i2], ALU.mult, R=[pbuf[4], b_rinv[i2]], W=[b_tq[i2]])
                P.stt("dve", ob[i2], tq[i2][:, 256:512], lv[:, 5:6], tq[i2][:, 0:256], ALU.mult, ALU.add,
                      R=[b_tq[i2], b_lv], W=[b_ob[i2]])
                P.act(sqh[i2], ob[i2], AF.Square, R=[b_ob[i2]], W=[b_sqh[i2]])

                def fin_b(i2=i2, h=h, os_=os_):
                    P.mm(bank(7)[:, 0:256], onesb, sqh[i2], True, True, R=[b_onesb, b_sqh[i2]], W=[pbuf[7]])
                    P.ts("dve", rsd[i2], bank(7)[:, 0:256], 1.0 / 128, EPS, ALU.mult, ALU.add, R=[pbuf[7]], W=[b_rsd[i2]])
                    P.rsqrt(rsd[i2], rsd[i2], R=[b_rsd[i2]], W=[b_rsd[i2]], lnexp=True)
                    P.stt("dve", oT[os_][:, h, :], ob[i2], lv[:, 6:7], rsd[i2], ALU.mult, ALU.mult,
                          R=[b_ob[i2], b_lv, b_rsd[i2]], W=[b_oT[os_]])

                defer(i + 3, fin_b)
                if h == NH - 1:
                    def fin_q(first=first, n=n, m=m, os_=os_):
                        rc.set_gate(m)
                        for tt in range(n):
                            xi = rc.begin_tile(first + tt, False)
                            for dc in range(2):
                                pb = rc.next_obank()
                                for hh in range(NH):
                                    P.mm(bank(pb), oT[os_][:, hh, tt * 128:(tt + 1) * 128], wo[:, hh, dc * 512:(dc + 1) * 512],
                                         hh == 0, hh == NH - 1, R=[b_oT[os_], b_wo], W=[pbuf[pb]])
                                rc.update(xi, pb, dc)
                            rc.end_tile(xi, first + tt, False)

                    defer(i + 7, fin_q)

        nit = len(its)

        def dummy():
            P.mm(bank(7)[:, 0:128], ident, ident, True, True, R=[b_ident], W=[pbuf[7]])

        S_half(0, 0)
        dummy()
        S_half(0, 1)
        dummy()
        for i in range(nit + 1):
            if i + 1 < nit:
                S_half(i + 1, 0)
            if i >= 1:
                AV_part(i - 1, 1)
            else:
                dummy()
            if i + 1 < nit:
                S_half(i + 1, 1)
            if i < nit:
                AV_part(i, 0)
            run_deferred(i)
        run_deferred(10 ** 9)
        P.barrier()
        AR.release(m0)

    def dwconv_segments(with_ctx):
        return ([(0, NCT, 1)] if with_ctx else []) + [(NCT, SEQ // 128, 0)]

    def phase_conf(l, ctx_out):
        PAD = CK // 2
        m0 = AR.mark()
        uL = AR.alloc([8, SEQ + 2 * PAD], BF16)
        uC = AR.alloc([8, CTX + 2 * PAD], BF16)
        b_u = Buf()
        for c in range(8):
            P.memset("dve", uL[:, c, 0:PAD], 0.0, W=[b_u])
            P.memset("dve", uL[:, c, PAD + SEQ:], 0.0, W=[b_u])
            P.memset("dve", uC[:, c, 0:PAD], 0.0, W=[b_u])
            P.memset("dve", uC[:, c, PAD + CTX:], 0.0, W=[b_u])
        blocks = _tiles_ffn(ctx_out)
        m1 = AR.mark()
        w_in = AR.alloc([8, 2 * D], BF16)
        b_win = [Buf() for _ in range(4)]
        wsrc = conv_w_in.rearrange("(k p) n -> p k n", p=128)
        for g in range(4):
            P.dma("pool", w_in[:, :, g * 256:(g + 1) * 256], wsrc[:, :, g * 256:(g + 1) * 256], ("w", 2 * g), W=[b_win[g]])
            P.dma("pool", w_in[:, :, D + g * 256:D + (g + 1) * 256], wsrc[:, :, D + g * 256:D + (g + 1) * 256], ("w", 2 * g + 1),
                  W=[b_win[g]])
        binF = AR.alloc([16], F32)
        b_bin = Buf()
        P.dma("sp", binF, conv_b_inF, ("misc", 0), W=[b_bin])
        nctx = NormCtx(ptr_bank=6)
        hT = AR.alloc([8, 512], BF16)
        b_hT = Buf()
        sg = [AR.alloc([512], F32) for _ in range(2)]
        b_sg = [Buf() for _ in range(2)]
        for (first, n, is_ctx) in blocks:
            m = 1 if is_ctx else 0
            N = n * 128
            pb = nctx.ptr_bank
            ptv = bank_bf(pb)
            for tt in range(n):
                ys = nctx.load_norm(src_tile_ap(l, first + tt, 1, False)[:, 0, :])
                for kc in range(8):
                    P.tr(ptv[:, kc * 128:(kc + 1) * 128], nctx.y[ys][:, kc * 128:(kc + 1) * 128], ident,
                         R=[nctx.b_y[ys], b_ident], W=[pbuf[pb]])
                for kc in range(8):
                    P.act(hT[:, kc, tt * 128:(tt + 1) * 128], ptv[:, kc * 128:(kc + 1) * 128], AF.Identity,
                          bias=modF[:, 3 * 8 + kc, m:m + 1], scale=ATab[:, 1, m, kc:kc + 1],
                          R=[pbuf[pb], b_modF, b_ATab], W=[b_hT])
            ubuf = uC if is_ctx else uL
            t0 = PAD + (first * 128 if is_ctx else (first - NCT) * 128)
            for c in range(8):
                g = c // 2
                pa = (c % 2) * 2
                pg = pa + 1
                for kc in range(8):
                    P.mm(bank(pa)[:, 0:N], w_in[:, kc, c * 128:(c + 1) * 128], hT[:, kc, 0:N], kc == 0, kc == 7,
                         R=[b_win[g], b_hT], W=[pbuf[pa]])
                for kc in range(8):
                    P.mm(bank(pg)[:, 0:N], w_in[:, kc, D + c * 128:D + (c + 1) * 128], hT[:, kc, 0:N], kc == 0, kc == 7,
                         R=[b_win[g], b_hT], W=[pbuf[pg]])
                i = c % 2
                P.act(sg[i][:, 0:N], bank(pg)[:, 0:N], AF.Sigmoid, bias=binF[:, 8 + c:9 + c], R=[pbuf[pg], b_bin], W=[b_sg[i]])
                P.stt("dve", ubuf[:, c, t0:t0 + N], bank(pa)[:, 0:N], binF[:, c:c + 1], sg[i][:, 0:N], ALU.add, ALU.mult,
                      R=[pbuf[pa], b_sg[i], b_bin], W=[b_u])
        P.barrier()
        AR.release(m1)
        NB = 256
        dg = AR.alloc([8, CK, 128], BF16)
        b_dg = Buf()
        dwF = AR.alloc([CK, 8], F32)
        vecF = AR.alloc([3, 8], F32)
        b_dw = Buf()
        P.dma("sp", dwF, conv_dwF, ("misc", 0), W=[b_dw])
        P.dma("sp", vecF, conv_vecF, ("misc", 1), W=[b_dw])
        for c in range(8):
            for k in range(CK):
                P.ts("dve", dg[:, c, k, :], identf, dwF[:, k, c:c + 1], None, ALU.mult, R=[b_ident, b_dw], W=[b_dg])
        w_out = AR.alloc([8, D], BF16)
        b_wout = Buf()
        P.dma("pool", w_out, conv_w_out.rearrange("(k p) n -> p k n", p=128), ("w", 0), W=[b_wout])
        bo = AR.alloc([D], BF16)
        b_bo = Buf()
        P.dma("pool", bo[0:1, :], conv_b_out, ("w", 1), W=[b_bo])
        om = AR.alloc([128], BF16)
        b_om = Buf()
        P.memset("dve", om, 1.0 / D, W=[b_om])
        vT = AR.alloc([8, NB], F32)
        b_vT = Buf()
        vb = AR.alloc([8, NB], BF16)
        b_vb = Buf()
        v2 = AR.alloc([8, NB], BF16)
        b_v2 = Buf()
        mr = AR.alloc([3, NB], F32)
        b_mr = Buf()
        zt = [AR.alloc([NB], F32) for _ in range(2)]
        b_zt = [Buf() for _ in range(2)]
        sT = AR.alloc([8, NB], BF16)
        b_sT = Buf()
        rc = ResCtx(l, 1, o_banks=[6, 7])
        segs = ([(0, CTX, uC, 1)] if ctx_out else []) + [(NCT, SEQ, uL, 0)]
        ccnt = [0]
        for (ft, ntok, ubuf, m) in segs:
            rc.set_gate(m)
            for b0 in range(0, ntok, NB):
                for c in range(8):
                    pb = ccnt[0] % 2
                    ccnt[0] += 1
                    for k in range(CK):
                        P.mm(bank(pb)[:, 0:NB], dg[:, c, k, :], ubuf[:, c, b0 + k:b0 + k + NB], k == 0, k == CK - 1,
                             R=[b_dg, b_u], W=[pbuf[pb]])
                    P.act(vT[:, c, :], bank(pb)[:, 0:NB], AF.Identity, bias=vecF[:, 0, c:c + 1], R=[pbuf[pb], b_dw], W=[b_vT])
                    P.cp("dve", vb[:, c, :], vT[:, c, :], R=[b_vT], W=[b_vb])
                    P.tt("dve", v2[:, c, :], vT[:, c, :], vT[:, c, :], ALU.mult, R=[b_vT], W=[b_v2])
                for c in range(8):
                    P.mm(bank(2)[:, 0:NB], om, vb[:, c, :], c == 0, c == 7, R=[b_om, b_vb], W=[pbuf[2]])
                for c in range(8):
                    P.mm(bank(3)[:, 0:NB], om, v2[:, c, :], c == 0, c == 7, R=[b_om, b_v2], W=[pbuf[3]])
                P.cp("dve", mr[:, 0, :], bank(2)[:, 0:NB], R=[pbuf[2]], W=[b_mr])
                P.tt("dve", mr[:, 2, :], mr[:, 0, :], mr[:, 0, :], ALU.mult, R=[b_mr], W=[b_mr])
                P.tt("dve", mr[:, 1, :], bank(3)[:, 0:NB], mr[:, 2, :], ALU.subtract, R=[pbuf[3], b_mr], W=[b_mr])
                P.ts("dve", mr[:, 1, :], mr[:, 1, :], EPS, None, ALU.add, R=[b_mr], W=[b_mr])
                P.rsqrt(mr[:, 1, :], mr[:, 1, :], R=[b_mr], W=[b_mr])
                for c in range(8):
                    zi = c % 2
                    P.tt("dve", zt[zi], vT[:, c, :], mr[:, 0, :], ALU.subtract, R=[b_vT, b_mr], W=[b_zt[zi]])
                    P.tt("dve", zt[zi], zt[zi], mr[:, 1, :], ALU.mult, R=[b_zt[zi], b_mr], W=[b_zt[zi]])
                    P.act(sT[:, c, :], zt[zi], AF.Silu, bias=vecF[:, 2, c:c + 1], scale=vecF[:, 1, c:c + 1],
                          R=[b_zt[zi], b_dw], W=[b_sT])
                for tt in range(NB // 128):
                    tile = ft + b0 // 128 + tt
                    i = rc.begin_tile(tile, False)
                    for dc in range(2):
                        pb = rc.next_obank()
                        for c in range(8):
                            P.mm(bank(pb), sT[:, c, tt * 128:(tt + 1) * 128], w_out[:, c, dc * 512:(dc + 1) * 512], c == 0, False,
                                 R=[b_sT, b_wout], W=[pbuf[pb]])
                        P.mm(bank(pb), ones_row[0:1, :], bo[0:1, dc * 512:(dc + 1) * 512], False, True,
                             R=[b_ones, b_bo], W=[pbuf[pb]])
                        rc.update(i, pb, dc)
                    rc.end_tile(i, tile, False)
        P.barrier()
        AR.release(m0)

    def phase_sconv(l, ctx_out):
        PAD = SK // 2
        m0 = AR.mark()
        pL = AR.alloc([8, SEQ + 2 * PAD], BF16)
        pC = AR.alloc([8, CTX + 2 * PAD], BF16)
        b_p = Buf()
        for c in range(8):
            P.memset("dve", pL[:, c, 0:PAD], 0.0, W=[b_p])
            P.memset("dve", pL[:, c, PAD + SEQ:], 0.0, W=[b_p])
            P.memset("dve", pC[:, c, 0:PAD], 0.0, W=[b_p])
            P.memset("dve", pC[:, c, PAD + CTX:], 0.0, W=[b_p])
        blocks = _tiles_ffn(ctx_out)
        m1 = AR.mark()
        w_in = AR.alloc([8, 3 * D], BF16)
        b_win = [Buf() for _ in range(4)]
        wsrc = sc_w_in.rearrange("(k p) n -> p k n", p=128)
        for g in range(4):
            for part in range(3):
                P.dma("pool", w_in[:, :, part * D + g * 256:part * D + (g + 1) * 256],
                      wsrc[:, :, part * D + g * 256:part * D + (g + 1) * 256], ("w", 3 * g + part), W=[b_win[g]])
        nctx = NormCtx(ptr_bank=6)
        hT = AR.alloc([8, 512], BF16)
        b_hT = Buf()
        xh = [AR.alloc([512], F32) for _ in range(2)]
        b_xh = [Buf() for _ in range(2)]
        bT = [AR.alloc([8, 128], BF16) for _ in range(8)]
        b_bT = [Buf() for _ in range(8)]
        btc = [0]
        for (first, n, is_ctx) in blocks:
            m = 1 if is_ctx else 0
            N = n * 128
            pb = nctx.ptr_bank
            ptv = bank_bf(pb)
            for tt in range(n):
                ys = nctx.load_norm(src_tile_ap(l, first + tt, 1, False)[:, 0, :])
                for kc in range(8):
                    P.tr(ptv[:, kc * 128:(kc + 1) * 128], nctx.y[ys][:, kc * 128:(kc + 1) * 128], ident,
                         R=[nctx.b_y[ys], b_ident], W=[pbuf[pb]])
                for kc in range(8):
                    P.act(hT[:, kc, tt * 128:(tt + 1) * 128], ptv[:, kc * 128:(kc + 1) * 128], AF.Identity,
                          bias=modF[:, 3 * 8 + kc, m:m + 1], scale=ATab[:, 1, m, kc:kc + 1],
                          R=[pbuf[pb], b_modF, b_ATab], W=[b_hT])
            pbuf_ = pC if is_ctx else pL
            t0 = PAD + (first * 128 if is_ctx else (first - NCT) * 128)
            slots = []
            for tt in range(n):
                slots.append(btc[0] % 8)
                btc[0] += 1
            for c in range(8):
                g = c // 2
                base = (c % 2) * 3
                for part in range(3):
                    for kc in range(8):
                        P.mm(bank(base + part)[:, 0:N], w_in[:, kc, part * D + c * 128:part * D + (c + 1) * 128], hT[:, kc, 0:N],
                             kc == 0, kc == 7, R=[b_win[g], b_hT], W=[pbuf[base + part]])
                i = c % 2
                P.act(xh[i][:, 0:N], bank(base + 2)[:, 0:N], AF.Copy, R=[pbuf[base + 2]], W=[b_xh[i]])
                P.tt("dve", pbuf_[:, c, t0:t0 + N], bank(base + 1)[:, 0:N], xh[i][:, 0:N], ALU.mult,
                     R=[pbuf[base + 1], b_xh[i]], W=[b_p])
                for tt in range(n):
                    P.act(bT[slots[tt]][:, c, :], bank(base)[:, tt * 128:(tt + 1) * 128], AF.Copy, R=[pbuf[base]],
                          W=[b_bT[slots[tt]]])
            for tt in range(n):
                P.dma("sp", bT_d[first + tt].rearrange("p (c t) -> p c t", t=128), bT[slots[tt]], ("bst", slots[tt]),
                      R=[b_bT[slots[tt]]])
        P.barrier()
        AR.release(m1)
        dg = AR.alloc([8, SK, 128], BF16)
        b_dg = Buf()
        dwF = AR.alloc([SK, 8], F32)
        b_dw = Buf()
        P.dma("sp", dwF, sc_dwF, ("misc", 0), W=[b_dw])
        for c in range(8):
            for k in range(SK):
                P.ts("dve", dg[:, c, k, :], identf, dwF[:, k, c:c + 1], None, ALU.mult, R=[b_ident, b_dw], W=[b_dg])
        w_out = AR.alloc([8, D], BF16)
        b_wout = Buf()
        P.dma("pool", w_out, sc_w_out.rearrange("(k p) n -> p k n", p=128), ("w", 0), W=[b_wout])
        bTl = [AR.alloc([4, 8, 128], BF16) for _ in range(2)]
        b_bTl = [Buf() for _ in range(2)]
        yT = AR.alloc([8, 512], BF16)
        b_yT = Buf()
        rc = ResCtx(l, 1, o_banks=[6, 7])
        ccnt = [0]
        for bi, (first, n, is_ctx) in enumerate(blocks):
            m = 1 if is_ctx else 0
            N = n * 128
            rc.set_gate(m)
            pbuf_ = pC if is_ctx else pL
            b0 = first * 128 if is_ctx else (first - NCT) * 128
            bs = bi % 2
            P.dma("sp", bTl[bs][:, 0:n], bT_d[first:first + n].rearrange("t p (c k) -> p t c k", k=128), ("btl", bs),
                  W=[b_bTl[bs]])
            for c in range(8):
                pb = ccnt[0] % 4
                ccnt[0] += 1
                for k in range(SK):
                    P.mm(bank(pb)[:, 0:N], dg[:, c, k, :], pbuf_[:, c, b0 + k:b0 + k + N], k == 0, k == SK - 1,
                         R=[b_dg, b_p], W=[pbuf[pb]])
                P.tt("dve", yT[:, c, 0:N].rearrange("p (t k) -> p t k", k=128), bank(pb)[:, 0:N].rearrange("p (t k) -> p t k", k=128),
                     bTl[bs][:, 0:n, c, :], ALU.mult, R=[pbuf[pb], b_bTl[bs]], W=[b_yT])
            for tt in range(n):
                i = rc.begin_tile(first + tt, False)
                for dc in range(2):
                    pb = rc.next_obank()
                    for c in range(8):
                        P.mm(bank(pb), yT[:, c, tt * 128:(tt + 1) * 128], w_out[:, c, dc * 512:(dc + 1) * 512], c == 0, c == 7,
                             R=[b_yT, b_wout], W=[pbuf[pb]])
                    rc.update(i, pb, dc)
                rc.end_tile(i, first + tt, False)
        P.barrier()
        AR.release(m0)

    for l in range(n_layers):
        kind = l % 3
        j = l // 3
        last = l == DEPTH - 1
        ctx_in_needed = (not last) or kind == 0
        ctx_out_needed = not last
        phase_mods(l)
        phase_ffn(l, 0, ctx_in_needed, first_ffn=(l == 0), final=False)
        if kind == 0:
            lam_init = 0.8 - 0.6 * math.exp(-0.3 * l)
            phase_attn_qkv(l, j)
            phase_attn_core(l, j, ctx_out_needed, lam_init)
        elif kind == 1:
            phase_conf(l, ctx_out_needed)
        else:
            phase_sconv(l, ctx_out_needed)
        phase_ffn(l, 2, ctx_out_needed, first_ffn=False, final=(l == n_layers - 1))
    P.wait_all("sp", store_ops)
    P.emit()
    return nc, AR.peak, {e: len(q) for e, q in P.q.items()}


def _fm(v, nchunk):
    return np.ascontiguousarray(np.asarray(v, np.float32).reshape(nchunk, 128).T)


def _rope_tables():
    rows = SEQ // 64
    row_ids = np.repeat(np.arange(rows, dtype=np.float32), 64)
    col_ids = np.tile(np.arange(64, dtype=np.float32), rows)
    half = HD // 2
    inv_freq = (np.float32(10000.0) ** (-np.arange(0, half, 2, dtype=np.float32) / np.float32(half))).astype(np.float32)
    ang_r = row_ids[:, None] * inv_freq
    ang_c = col_ids[:, None] * inv_freq
    ang = np.concatenate([ang_r, ang_r, ang_c, ang_c], axis=-1)
    cos = np.cos(ang).astype(np.float32)
    sin = np.sin(ang).astype(np.float32)
    sgn = np.concatenate([-np.ones(16), np.ones(16), -np.ones(16), np.ones(16)]).astype(np.float32)
    sin_s = sin * sgn
    to = lambda t: np.ascontiguousarray(t.reshape(SEQ // 128, 128, 64).transpose(1, 0, 2))
    return to(cos), to(sin_s)


def _swap_idx():
    return np.concatenate([np.arange(16, 32), np.arange(0, 16), np.arange(48, 64), np.arange(32, 48)])


def make_in_maps(inputs):
    f = lambda k: np.asarray(inputs[k], np.float32)
    x, c, ctx, c_ctx = f("x"), f("c"), f("ctx"), f("c_ctx")
    B = x.shape[0]
    cos, sin_s = _rope_tables()
    norm_g = f("norm_g")
    norm_gF = np.ascontiguousarray(norm_g.reshape(DEPTH, 3, 8, 128).transpose(3, 0, 1, 2))
    qg, kg = f("attn_q_g"), f("attn_k_g")
    sw = _swap_idx()
    ag = np.stack([qg, kg, qg[:, sw], kg[:, sw]], axis=1)
    attn_g = np.ascontiguousarray(np.broadcast_to(ag[None], (128, 2, 4, 64)))
    attn_lam = np.ascontiguousarray(np.broadcast_to(f("attn_lambda")[None], (128, 2, 4, 64)))
    attn_sublnF = np.ascontiguousarray(f("attn_subln_g").T)
    conv_b_inF = _fm(f("conv_b_in")[0], 16)
    conv_dwF = np.ascontiguousarray(f("conv_dw_w")[0].reshape(CK, 8, 128).transpose(2, 0, 1))
    conv_vecF = np.ascontiguousarray(np.stack([_fm(f("conv_dw_b")[0], 8), _fm(f("conv_ln_g")[0], 8), _fm(f("conv_ln_b")[0], 8)], axis=1))
    sc_dwF = np.ascontiguousarray(f("sc_dw_w")[0].reshape(SK, 8, 128).transpose(2, 0, 1))
    shared = {
        "ada_w": f("ada_w"), "ada_b": f("ada_b").reshape(DEPTH, 1, NMOD * D), "norm_gF": norm_gF,
        "ffn_w_in": f("ffn_w_in"), "ffn_w_out": f("ffn_w_out"),
        "attn_w_qkv": f("attn_w_qkv"), "attn_w_o": f("attn_w_o"), "attn_g": attn_g, "attn_lam": attn_lam,
        "attn_sublnF": attn_sublnF, "rope_cos": cos, "rope_sin": sin_s,
        "conv_w_in": f("conv_w_in")[0], "conv_b_inF": conv_b_inF, "conv_dwF": conv_dwF, "conv_vecF": conv_vecF,
        "conv_w_out": f("conv_w_out")[0], "conv_b_out": f("conv_b_out").reshape(1, D),
        "sc_w_in": f("sc_w_in")[0], "sc_dwF": sc_dwF, "sc_w_out": f("sc_w_out")[0],
    }
    maps = []
    for b in range(B):
        cv = np.ascontiguousarray(np.stack([c[b].reshape(8, 128).T, c_ctx.reshape(8, 128).T], axis=-1))
        mp = dict(shared)
        mp.update({"x": np.ascontiguousarray(x[b]), "ctx": np.ascontiguousarray(ctx[b]), "cvec": cv})
        maps.append(mp)
    return maps


def run(inputs, n_layers=DEPTH, debug=False, trace=False):
    nc, peak, counts = build_program(n_layers=n_layers, debug=debug)
    maps = make_in_maps(inputs)
    res = run_bass_kernel_spmd(nc, maps, core_ids=list(range(len(maps))), trace=trace)
    out = np.stack([r["out"] for r in res.results], axis=0)
    if debug:
        return out, res
    return out


def kernel(**inputs):
    return run(inputs).astype(np.float32)
```

```python
import math
import os
import numpy as np
import concourse.bass as bass
import concourse.mybir as mybir
from concourse.bass_utils import run_bass_kernel_spmd

F32 = mybir.dt.float32
BF16 = mybir.dt.bfloat16
AF = mybir.ActivationFunctionType
ALU = mybir.AluOpType
AX = mybir.AxisListType

D = 1024
DEPTH = 4
SEQ = 4096
CTX = 256
NH = 8
HD = 64
FF = 2816
NMOD = 9
EPS = 1e-6
CK = 31
SK = 3
NT_ALL = (SEQ + CTX) // 128
NCT = CTX // 128
VW = 130

DSZ = {F32: 4, BF16: 2}
ATTACH_WAITS = False


class Op:
    __slots__ = ("eng", "fn", "deps", "sig", "count", "idx", "key", "order")


class Buf:
    __slots__ = ("w", "r", "pr")

    def __init__(self):
        self.w = {}
        self.r = {}
        self.pr = {}


class Prog:
    ENG = ("pe", "act", "dve", "pool", "sp")
    GAP = 4

    def __init__(self, nc):
        self.nc = nc
        self.q = {e: [] for e in self.ENG}
        self.esem = {e: nc.alloc_semaphore("sem_" + e) for e in ("pe", "act", "dve", "pool")}
        self.ksem = {}
        self.kcnt = {}
        self.klast = {}
        self.pending = {e: {} for e in self.ENG}

    @staticmethod
    def _sid(o):
        return ("k", o.key) if o.key is not None else ("e", o.eng)

    def _mk(self, eng, fn, R=(), W=(), deps=(), key=None):
        op = Op()
        op.eng = eng
        op.fn = fn
        op.sig = False
        op.key = key
        op.idx = len(self.q[eng])
        op.count = None
        d = {}

        def add(o):
            k = self._sid(o)
            if k not in d or o.order > d[k].order:
                d[k] = o

        for o in deps:
            add(o)
        for o in self.pending[eng].values():
            add(o)
        self.pending[eng] = {}
        for b in R:
            for o in b.w.values():
                add(o)
        for b in W:
            for o in b.r.values():
                add(o)
            for o in b.pr.values():
                add(o)
        if key is not None:
            if key not in self.ksem:
                self.ksem[key] = self.nc.alloc_semaphore("k_" + "_".join(str(x) for x in key))
                self.kcnt[key] = 0
            self.kcnt[key] += 16
            op.count = self.kcnt[key]
            op.order = op.count
            self.klast[key] = op
        else:
            op.order = op.idx
        for o in d.values():
            o.sig = True
        op.deps = list(d.values())
        sid = self._sid(op)
        for b in R:
            b.r[sid] = op
        for b in W:
            if b.r and not any(b is rb for rb in R):
                b.pr = b.r
                b.r = {}
                b.w = {}
            elif any(b is rb for rb in R):
                b.pr = {k: v for k, v in b.r.items() if v is not op}
                b.r = {}
                b.w = {}
            b.w[sid] = op
        self.q[eng].append(op)
        return op

    def barrier(self):
        last = {}
        for e in ("pe", "act", "dve", "pool"):
            for o in reversed(self.q[e]):
                if o.key is None and o.fn is not None:
                    last[("e", e)] = o
                    o.sig = True
                    break
        for k, o in self.klast.items():
            last[("k", k)] = o
        self.klast = {}
        for e in self.ENG:
            self.pending[e] = dict(last)

    def mm(self, out, lhsT, rhs, start, stop, R=(), W=(), skip=False):
        if skip:
            return self._mk("pe", lambda e: e.matmul(out, lhsT=lhsT, rhs=rhs, start=start, stop=stop, skip_group_check=True), R, W)
        return self._mk("pe", lambda e: e.matmul(out, lhsT=lhsT, rhs=rhs, start=start, stop=stop), R, W)

    def tr(self, out, in_, ident, R=(), W=()):
        return self._mk("pe", lambda e: e.transpose(out=out, in_=in_, identity=ident), R, W)

    def act(self, out, in_, func, bias=None, scale=None, R=(), W=()):
        kw = {}
        if bias is not None:
            kw["bias"] = bias
        if scale is not None:
            kw["scale"] = scale
        return self._mk("act", lambda e: e.activation(out=out, in_=in_, func=func, **kw), R, W)

    def ts(self, eng, out, in0, s1, s2, op0, op1=None, R=(), W=()):
        if op1 is None:
            return self._mk(eng, lambda e: e.tensor_scalar(out=out, in0=in0, scalar1=s1, scalar2=None, op0=op0), R, W)
        return self._mk(eng, lambda e: e.tensor_scalar(out=out, in0=in0, scalar1=s1, scalar2=s2, op0=op0, op1=op1), R, W)

    def tt(self, eng, out, in0, in1, op, R=(), W=()):
        return self._mk(eng, lambda e: e.tensor_tensor(out=out, in0=in0, in1=in1, op=op), R, W)

    def stt(self, eng, out, in0, scalar, in1, op0, op1, R=(), W=()):
        return self._mk(eng, lambda e: e.scalar_tensor_tensor(out=out, in0=in0, scalar=scalar, in1=in1, op0=op0, op1=op1), R, W)

    def red(self, eng, out, in_, R=(), W=()):
        return self._mk(eng, lambda e: e.reduce_sum(out=out, in_=in_, axis=AX.X), R, W)

    def cp(self, eng, out, in_, R=(), W=()):
        return self._mk(eng, lambda e: e.tensor_copy(out=out, in_=in_), R, W)

    def rsqrt(self, out, in_, R=(), W=(), lnexp=False):
        if lnexp:
            self._mk("act", lambda e: e.activation(out=out, in_=in_, func=AF.Ln), list(R), list(W))
            return self._mk("act", lambda e: e.activation(out=out, in_=out, func=AF.Exp, scale=-0.5), list(W), list(W))
        self._mk("act", lambda e: e.activation(out=out, in_=in_, func=AF.Sqrt), list(R), list(W))
        return self._mk("dve", lambda e: e.reciprocal(out=out, in_=out), list(W), list(W))

    def recip(self, out, in_, R=(), W=()):
        return self._mk("dve", lambda e: e.reciprocal(out=out, in_=in_), R, W)

    def memset(self, eng, ap, val, R=(), W=()):
        return self._mk(eng, lambda e: e.memset(ap, val), R, W)

    def dma(self, q, out, in_, key, R=(), W=()):
        return self._mk(q, lambda e: e.dma_start(out=out, in_=in_), R, W, key=key)

    def wait_all(self, eng, ops):
        return self._mk(eng, None, deps=ops)

    def emit(self):
        nc = self.nc
        for e in ("pe", "act", "dve", "pool"):
            c = 0
            for o in self.q[e]:
                if o.key is None and o.sig:
                    c += 1
                    o.count = c
        handles = {}

        def flush(ename, e):
            waited = {}
            for o in self.q[ename]:
                need = []
                for dpn in o.deps:
                    if dpn.key is not None:
                        sem = self.ksem[dpn.key]
                        sid = ("k", dpn.key)
                    else:
                        if dpn.eng == ename:
                            if ename == "pe":
                                continue
                            if ename != "pool" and o.idx - dpn.idx > self.GAP:
                                continue
                        sem = self.esem[dpn.eng]
                        sid = ("e", dpn.eng)
                    if waited.get(sid, 0) >= dpn.count:
                        continue
                    need.append((sem, dpn.count))
                    waited[sid] = dpn.count
                if o.fn is None:
                    for sem, cnt in need:
                        e.wait_ge(sem, cnt)
                    continue
                attach = ATTACH_WAITS and o.key is None and ename in ("pe", "act", "dve")
                for sem, cnt in (need[:-1] if attach else need):
                    e.wait_ge(sem, cnt)
                ins = o.fn(e)
                if need and attach:
                    ins._wait_ge(need[-1][0], need[-1][1])
                if o.key is not None:
                    ins.then_inc(self.ksem[o.key], 16)
                elif o.sig:
                    ins.then_inc(self.esem[ename], 1)

        with nc.Block() as block:
            @block.tensor
            def _(e):
                flush("pe", e)

            @block.scalar
            def _(e):
                flush("act", e)

            @block.vector
            def _(e):
                flush("dve", e)

            @block.gpsimd
            def _(e):
                flush("pool", e)

            @block.sync
            def _(e):
                flush("sp", e)


class Arena:
    def __init__(self, nc, nbytes):
        self.t = nc.alloc_sbuf_tensor("arena", [128, nbytes // 4], F32)
        self.cap = nbytes
        self.off = 0
        self.peak = 0

    def alloc(self, shape, dtype):
        n = 1
        for s in shape:
            n *= s
        nb = (n * DSZ[dtype] + 31) // 32 * 32
        assert self.off + nb <= self.cap, f"arena overflow: {self.off + nb} > {self.cap}"
        v = self.t[:, self.off // 4:(self.off + nb) // 4]
        if dtype != F32:
            v = v.bitcast(dtype)
        v = v[:, 0:n]
        self.off += nb
        self.peak = max(self.peak, self.off)
        if len(shape) == 2:
            v = v.rearrange("p (a b) -> p a b", b=shape[1])
        elif len(shape) == 3:
            v = v.rearrange("p (a b c) -> p a b c", b=shape[1], c=shape[2])
        elif len(shape) == 4:
            v = v.rearrange("p (a b c d) -> p a b c d", b=shape[1], c=shape[2], d=shape[3])
        return v

    def mark(self):
        return self.off

    def release(self, m):
        self.off = m


def _tiles_ffn(with_ctx):
    blocks = []
    for b in range(SEQ // 512):
        blocks.append((NCT + 4 * b, 4, False))
    if with_ctx:
        blocks.append((0, NCT, True))
    return blocks


def build_program(n_layers=DEPTH, debug=False):
    nc = bass.Bass("TRN2", target_bir_lowering=False)

    def din(name, shape, dt=F32):
        return nc.dram_tensor(name, list(shape), dt, kind="ExternalInput").ap()

    x_in = din("x", [SEQ, D])
    ctx_in = din("ctx", [CTX, D])
    cvec = din("cvec", [128, 8, 2])
    ada_w = din("ada_w", [DEPTH, D, NMOD * D])
    ada_b = din("ada_b", [DEPTH, 1, NMOD * D])
    norm_gF = din("norm_gF", [128, DEPTH, 3, 8])
    ffn_w_in = din("ffn_w_in", [DEPTH, 2, D, 2 * FF])
    ffn_w_out = din("ffn_w_out", [DEPTH, 2, FF, D])
    attn_w_qkv = din("attn_w_qkv", [2, D, 3 * D])
    attn_w_o = din("attn_w_o", [2, D, D])
    attn_g = din("attn_g", [128, 2, 4, 64])
    attn_lam = din("attn_lam", [128, 2, 4, 64])
    attn_sublnF = din("attn_sublnF", [128, 2])
    rope_cos = din("rope_cos", [128, SEQ // 128, 64])
    rope_sin = din("rope_sin", [128, SEQ // 128, 64])
    conv_w_in = din("conv_w_in", [D, 2 * D])
    conv_b_inF = din("conv_b_inF", [128, 16])
    conv_dwF = din("conv_dwF", [128, CK, 8])
    conv_vecF = din("conv_vecF", [128, 3, 8])
    conv_w_out = din("conv_w_out", [D, D])
    conv_b_out = din("conv_b_out", [1, D])
    sc_w_in = din("sc_w_in", [D, 3 * D])
    sc_dwF = din("sc_dwF", [128, SK, 8])
    sc_w_out = din("sc_w_out", [D, D])

    out = nc.dram_tensor("out", [SEQ, D], F32, kind="ExternalOutput").ap()
    skind = "ExternalOutput" if debug else "Internal"
    xs = nc.dram_tensor("xs", [NT_ALL * 128, D], F32, kind=skind).ap()
    QT_d = nc.dram_tensor("QT_d", [NT_ALL, 128, 1024], BF16, kind="Internal").ap()
    KT_d = nc.dram_tensor("KT_d", [NT_ALL, 128, 1024], BF16, kind="Internal").ap()
    V_d = nc.dram_tensor("V_d", [NT_ALL, 128, NH * VW], BF16, kind="Internal").ap()
    bT_d = nc.dram_tensor("bT_d", [NT_ALL, 128, 1024], BF16, kind="Internal").ap()
    gates_d = nc.dram_tensor("gates_d", [DEPTH, 3, 2, 128, D], F32, kind="Internal").ap()

    P = Prog(nc)
    AR = Arena(nc, 207 * 1024)
    ps_all = nc.alloc_psum_tensor("ps_all", [128, 8 * 512], F32)

    def bank(i, n=1):
        return ps_all[:, i * 512:(i + n) * 512]

    def bank_bf(i):
        return ps_all[:, i * 512:(i + 1) * 512].bitcast(BF16)

    pbuf = [Buf() for _ in range(8)]

    ident = AR.alloc([128], BF16)
    identf = AR.alloc([128], F32)
    modF = AR.alloc([72, 2], F32)
    ATab = AR.alloc([3, 2, 8], F32)
    gF = AR.alloc([DEPTH, 3, 8], F32)
    scT = AR.alloc([8, 2], BF16)
    scB = [AR.alloc([8, 128], BF16) for _ in range(2)]
    ones_row = AR.alloc([128], BF16)
    small = AR.alloc([64], F32)
    b_ident, b_modF, b_ATab, b_gF, b_sc, b_ones, b_small = (Buf() for _ in range(7))

    P.memset("pool", identf, 0.0, W=[b_ident])
    P._mk("pool", lambda e: e.affine_select(out=identf, in_=identf, pattern=[[-1, 128]], compare_op=ALU.not_equal,
                                             fill=1.0, base=0, channel_multiplier=1), R=[b_ident], W=[b_ident])
    P.cp("dve", ident, identf, R=[b_ident], W=[b_ident])
    P.memset("dve", ones_row, 1.0, W=[b_ones])
    m0 = AR.mark()
    cv = AR.alloc([8, 2], F32)
    b_cv = Buf()
    P.dma("sp", cv, cvec, ("misc", 0), W=[b_cv])
    P.dma("sp", gF, norm_gF, ("misc", 1), W=[b_gF])
    cvs = AR.alloc([8, 2], F32)
    P.act(cvs, cv, AF.Silu, R=[b_cv], W=[b_cv])
    P.cp("dve", scT, cvs, R=[b_cv], W=[b_sc])
    for m in range(2):
        P.cp("dve", scB[m], cvs[:, :, m:m + 1].to_broadcast([128, 8, 128]), R=[b_cv], W=[b_sc])
    P.barrier()
    AR.release(m0)
    PERSIST = AR.mark()

    def src_tile_ap(layer, first, n, first_ffn):
        if first_ffn:
            if first < NCT:
                return ctx_in[first * 128:(first + n) * 128, :].rearrange("(t p) d -> p t d", p=128)
            f = first - NCT
            return x_in[f * 128:(f + n) * 128, :].rearrange("(t p) d -> p t d", p=128)
        return xs[first * 128:(first + n) * 128, :].rearrange("(t p) d -> p t d", p=128)

    def dst_tile_ap(first, n, final):
        if final:
            f = first - NCT
            return out[f * 128:(f + n) * 128, :].rearrange("(t p) d -> p t d", p=128)
        return xs[first * 128:(first + n) * 128, :].rearrange("(t p) d -> p t d", p=128)

    store_ops = []

    def phase_mods(l):
        m0 = AR.mark()
        NCH = 18
        adw = [AR.alloc([8, 512], BF16) for _ in range(3)]
        b_adw = [Buf() for _ in range(3)]
        brow = AR.alloc([NMOD * D], BF16)
        b_brow = Buf()
        ones2 = AR.alloc([2], BF16)
        b_o2 = Buf()
        gsb = [AR.alloc([512], F32) for _ in range(2)]
        b_gsb = [Buf() for _ in range(2)]
        P.memset("dve", ones2, 1.0, W=[b_o2])
        P.dma("pool", brow[0:1, :], ada_b[l], ("w", 0), W=[b_brow])
        psF = bank(0)[:, 0:144].rearrange("p (f m) -> p f m", m=2)
        gi = 0
        for c in range(NCH):
            s_ = c % 3
            src = ada_w[l, :, c * 512:(c + 1) * 512].rearrange("(k p) n -> p k n", p=128)
            P.dma("pool", adw[s_], src, ("ada", s_), W=[b_adw[s_]])
            for fi in range(4):
                f = 4 * c + fi
                for kc in range(8):
                    P.mm(psF[:, f, :], adw[s_][:, kc, fi * 128:(fi + 1) * 128], scT[:, kc, :], kc == 0, False,
                         R=[b_adw[s_], b_sc], W=[pbuf[0]])
                P.mm(psF[:, f, :], brow[0:1, f * 128:(f + 1) * 128], ones2[0:1, :], False, True,
                     R=[b_brow, b_o2], W=[pbuf[0]])
            n = c // 2
            if n % 3 == 2:
                s = n // 3
                half = c % 2
                for m in range(2):
                    pb = 1 + m
                    for kc in range(8):
                        P.mm(bank(pb), scB[m][:, kc, :], adw[s_][:, kc, :], kc == 0, False,
                             R=[b_adw[s_], b_sc], W=[pbuf[pb]])
                    P.mm(bank(pb), ones_row[0:1, :], brow[0:1, c * 512:(c + 1) * 512], False, True,
                         R=[b_brow, b_ones], W=[pbuf[pb]])
                    g_ = gi % 2
                    gi += 1
                    P.act(gsb[g_], bank(pb), AF.Copy, scale=(1.0 if s == 1 else 0.5), R=[pbuf[pb]], W=[b_gsb[g_]])
                    P.dma("sp", gates_d[l, s, m, :, half * 512:(half + 1) * 512], gsb[g_], ("gst", g_), R=[b_gsb[g_]])
        P.cp("dve", modF, psF, R=[pbuf[0]], W=[b_modF])
        tmp = AR.alloc([8], F32)
        b_tmp = Buf()
        for s in range(3):
            for m in range(2):
                P.ts("dve", tmp, modF[:, (3 * s + 1) * 8:(3 * s + 2) * 8, m], 1.0, None, ALU.add, R=[b_modF], W=[b_tmp])
                P.tt("dve", ATab[:, s, m, :], tmp, gF[:, l, s, :], ALU.mult, R=[b_tmp, b_gF], W=[b_ATab])
        P.barrier()
        AR.release(m0)

    class NormCtx:
        def __init__(self, ptr_bank):
            self.xn = [AR.alloc([D], F32) for _ in range(2)]
            self.b_xn = [Buf() for _ in range(2)]
            self.sq = AR.alloc([D], F32)
            self.b_sq = Buf()
            self.y = [AR.alloc([D], BF16) for _ in range(2)]
            self.b_y = [Buf() for _ in range(2)]
            self.st = AR.alloc([2, 4], F32)
            self.b_st = [Buf() for _ in range(2)]
            self.ptr_bank = ptr_bank
            self.cnt = 0

        def load_norm(self, tile_ap):
            i = self.cnt % 2
            self.cnt += 1
            P.dma("sp", self.xn[i], tile_ap, ("xn", i), W=[self.b_xn[i]])
            P.act(self.sq, self.xn[i], AF.Square, R=[self.b_xn[i]], W=[self.b_sq])
            st = self.st[:, i, :]
            P.red("dve", st[:, 0:1], self.sq, R=[self.b_sq], W=[self.b_st[i]])
            P.ts("dve", st[:, 1:2], st[:, 0:1], 1.0 / D, EPS, ALU.mult, ALU.add, R=[self.b_st[i]], W=[self.b_st[i]])
            P.rsqrt(st[:, 2:3], st[:, 1:2], R=[self.b_st[i]], W=[self.b_st[i]])
            P.act(self.y[i], self.xn[i], AF.Identity, scale=st[:, 2:3], R=[self.b_xn[i], self.b_st[i]], W=[self.b_y[i]])
            return i

    def transpose_mod(nctx, yslots, hT, b_hT, s, m, l):
        n = len(yslots)
        pb = nctx.ptr_bank
        ptv = bank_bf(pb)
        for kc in range(8):
            half = kc % 2
            for tt, ys in enumerate(yslots):
                P.tr(ptv[:, half * 512 + tt * 128: half * 512 + (tt + 1) * 128], nctx.y[ys][:, kc * 128:(kc + 1) * 128], ident,
                     R=[nctx.b_y[ys], b_ident], W=[pbuf[pb]])
            P.act(hT[:, kc, 0:n * 128], ptv[:, half * 512: half * 512 + n * 128], AF.Identity,
                  bias=modF[:, 3 * s * 8 + kc, m:m + 1], scale=ATab[:, s, m, kc:kc + 1],
                  R=[pbuf[pb], b_modF, b_ATab], W=[b_hT])


    def residual_out(nctx_x, o_bank, pb, first_tile, tt, dc, gate, b_gate, tmpb, b_tmpb, xr, b_xr, final, idx):
        i = idx % 2
        P.tt("dve", tmpb[i], o_bank, gate[:, dc * 512:(dc + 1) * 512], ALU.mult, R=[pbuf[pb], b_gate], W=[b_tmpb[i]])
        P.tt("dve", xr[:, dc * 512:(dc + 1) * 512], xr[:, dc * 512:(dc + 1) * 512], tmpb[i], ALU.add,
             R=[b_tmpb[i], b_xr], W=[b_xr])

    class ResCtx:
        def __init__(self, l, s, o_banks):
            self.l, self.s = l, s
            self.gate = AR.alloc([D], F32)
            self.b_gate = Buf()
            self.gate_m = None
            self.xr = [AR.alloc([D], F32) for _ in range(2)]
            self.b_xr = [Buf() for _ in range(2)]
            self.tmp = [AR.alloc([512], F32) for _ in range(2)]
            self.b_tmp = [Buf() for _ in range(2)]
            self.o_banks = o_banks
            self.ocnt = 0
            self.xcnt = 0

        def set_gate(self, m):
            if self.gate_m != m:
                P.dma("sp", self.gate, gates_d[self.l, self.s, m], ("gate", 0), W=[self.b_gate])
                self.gate_m = m

        def begin_tile(self, tile, first_ffn):
            i = self.xcnt % 2
            self.xcnt += 1
            P.dma("sp", self.xr[i], src_tile_ap(self.l, tile, 1, first_ffn)[:, 0, :], ("xr", i), W=[self.b_xr[i]])
            return i

        def next_obank(self):
            pb = self.o_banks[self.ocnt % len(self.o_banks)]
            self.ocnt += 1
            return pb

        def update(self, i, pb, dc):
            j = self.ocnt % 2
            P.tt("dve", self.tmp[j], bank(pb), self.gate[:, dc * 512:(dc + 1) * 512], ALU.mult,
                 R=[pbuf[pb], self.b_gate], W=[self.b_tmp[j]])
            P.tt("dve", self.xr[i][:, dc * 512:(dc + 1) * 512], self.xr[i][:, dc * 512:(dc + 1) * 512], self.tmp[j], ALU.add,
                 R=[self.b_tmp[j], self.b_xr[i]], W=[self.b_xr[i]])

        def end_tile(self, i, tile, final):
            final = final and tile >= NCT
            op = P.dma("sp", dst_tile_ap(tile, 1, final)[:, 0, :], self.xr[i], ("xst", i), R=[self.b_xr[i]])
            if final:
                store_ops.append(op)

    def phase_ffn(l, s, with_ctx, first_ffn, final):
        wi = 0 if s == 0 else 1
        m0 = AR.mark()
        w_in = AR.alloc([8, 2 * FF], BF16)
        w_out = AR.alloc([22, D], BF16)
        NG = 11
        b_win = [Buf() for _ in range(NG)]
        b_wout = [Buf() for _ in range(2)]
        wsrc = ffn_w_in[l, wi].rearrange("(k p) n -> p k n", p=128)
        for g in range(NG):
            P.dma("pool", w_in[:, :, g * 256:(g + 1) * 256], wsrc[:, :, g * 256:(g + 1) * 256], ("w", 2 * g), W=[b_win[g]])
            P.dma("pool", w_in[:, :, FF + g * 256:FF + (g + 1) * 256], wsrc[:, :, FF + g * 256:FF + (g + 1) * 256],
                  ("w", 2 * g + 1), W=[b_win[g]])
        osrc = ffn_w_out[l, wi].rearrange("(j p) d -> p j d", p=128)
        for h in range(2):
            P.dma("pool", w_out[:, h * 11:(h + 1) * 11, :], osrc[:, h * 11:(h + 1) * 11, :], ("w", 22 + h), W=[b_wout[h]])
        nctx = NormCtx(ptr_bank=6)
        rc = ResCtx(l, s, o_banks=[4, 5])
        hT = AR.alloc([8, 512], BF16)
        b_hT = Buf()
        uT = AR.alloc([22, 512], BF16)
        b_uT = Buf()
        sg = [AR.alloc([512], F32) for _ in range(2)]
        b_sg = [Buf() for _ in range(2)]
        blocks = _tiles_ffn(with_ctx)

        def stage_T(blk):
            first, n, is_ctx = blk
            ys = []
            for tt in range(n):
                ys.append(nctx.load_norm(src_tile_ap(l, first + tt, 1, first_ffn)[:, 0, :]))
                if len(ys) == 2 or tt == n - 1:
                    pass
            return ys

        def stage_T_full(blk):
            first, n, is_ctx = blk
            m = 1 if is_ctx else 0
            pb = nctx.ptr_bank
            ptv = bank_bf(pb)
            for tt in range(n):
                ys = nctx.load_norm(src_tile_ap(l, first + tt, 1, first_ffn)[:, 0, :])
                for kc in range(8):
                    P.tr(ptv[:, kc * 128:(kc + 1) * 128], nctx.y[ys][:, kc * 128:(kc + 1) * 128], ident,
                         R=[nctx.b_y[ys], b_ident], W=[pbuf[pb]])
                for kc in range(8):
                    P.act(hT[:, kc, tt * 128:(tt + 1) * 128], ptv[:, kc * 128:(kc + 1) * 128], AF.Identity,
                          bias=modF[:, 3 * s * 8 + kc, m:m + 1], scale=ATab[:, s, m, kc:kc + 1],
                          R=[pbuf[pb], b_modF, b_ATab], W=[b_hT])

        def stage_IN(blk):
            first, n, is_ctx = blk
            N = n * 128
            for j in range(22):
                g = j // 2
                pa = (j % 2) * 2
                pg = pa + 1
                for kc in range(8):
                    P.mm(bank(pa)[:, 0:N], w_in[:, kc, j * 128:(j + 1) * 128], hT[:, kc, 0:N], kc == 0, kc == 7,
                         R=[b_win[g], b_hT], W=[pbuf[pa]])
                for kc in range(8):
                    P.mm(bank(pg)[:, 0:N], w_in[:, kc, FF + j * 128:FF + (j + 1) * 128], hT[:, kc, 0:N], kc == 0, kc == 7,
                         R=[b_win[g], b_hT], W=[pbuf[pg]])
                i = j % 2
                P.act(sg[i][:, 0:N], bank(pg)[:, 0:N], AF.Silu, R=[pbuf[pg]], W=[b_sg[i]])
                P.tt("dve", uT[:, j, 0:N], bank(pa)[:, 0:N], sg[i][:, 0:N], ALU.mult, R=[pbuf[pa], b_sg[i]], W=[b_uT])

        def stage_OUT(blk):
            first, n, is_ctx = blk
            rc.set_gate(1 if is_ctx else 0)
            for tt in range(n):
                i = rc.begin_tile(first + tt, first_ffn)
                for dc in range(2):
                    pb = rc.next_obank()
                    for j in range(22):
                        P.mm(bank(pb), uT[:, j, tt * 128:(tt + 1) * 128], w_out[:, j, dc * 512:(dc + 1) * 512], j == 0, j == 21,
                             R=[b_uT, b_wout[j // 11]], W=[pbuf[pb]])
                    rc.update(i, pb, dc)
                rc.end_tile(i, first + tt, final)

        stage_T_full(blocks[0])
        for bi, blk in enumerate(blocks):
            stage_IN(blk)
            if bi + 1 < len(blocks):
                stage_T_full(blocks[bi + 1])
            stage_OUT(blk)
        P.barrier()
        AR.release(m0)

    def phase_attn_qkv(l, j_attn):
        m0 = AR.mark()
        wq = AR.alloc([8, 3 * D], BF16)
        b_wq = [Buf() for _ in range(6)]
        wsrc = attn_w_qkv[j_attn].rearrange("(k p) n -> p k n", p=128)
        for n in range(6):
            P.dma("pool", wq[:, :, n * 512:(n + 1) * 512], wsrc[:, :, n * 512:(n + 1) * 512], ("w", n), W=[b_wq[n]])
        cosT = AR.alloc([SEQ // 128, 64], F32)
        sinT = AR.alloc([SEQ // 128, 64], F32)
        gq = AR.alloc([4, 64], F32)
        b_tab = Buf()
        P.dma("sp", cosT, rope_cos, ("misc", 0), W=[b_tab])
        P.dma("sp", sinT, rope_sin, ("misc", 1), W=[b_tab])
        P.dma("sp", gq, attn_g[:, j_attn], ("misc", 2), W=[b_tab])
        nctx = NormCtx(ptr_bank=6)
        hT = [AR.alloc([8, 128], BF16) for _ in range(2)]
        b_hT = [Buf() for _ in range(2)]
        sqb = AR.alloc([2 * D], F32)
        b_sqb = Buf()
        qks = AR.alloc([2 * D], F32)
        b_qks = Buf()
        stg = AR.alloc([64], F32)
        b_stg = Buf()
        csk = AR.alloc([2, 2, 64], F32)
        b_csk = Buf()
        t1 = AR.alloc([2 * D], F32)
        b_t1 = Buf()
        t2 = AR.alloc([2 * D], F32)
        b_t2 = Buf()
        qkh = [AR.alloc([2 * D], BF16) for _ in range(2)]
        b_qkh = [Buf() for _ in range(2)]
        qkT = [AR.alloc([2, NH, 128], BF16) for _ in range(2)]
        b_qkT = [Buf() for _ in range(2)]
        vaug = [AR.alloc([NH, VW], BF16) for _ in range(2)]
        b_vaug = [Buf() for _ in range(2)]
        for i in range(2):
            P.memset("dve", vaug[i], 1.0, W=[b_vaug[i]])
        pending_qkt = [None]

        def qkt(jt, qi):
            ptq = bank_bf(7)
            for w in range(2):
                for h in range(NH):
                    P.tr(ptq[:, h * 128:(h + 1) * 128], qkh[qi][:, w * D + h * 128: w * D + (h + 1) * 128], ident,
                         R=[b_qkh[qi], b_ident], W=[pbuf[7]])
                P.cp("dve", qkT[qi][:, w, :, :], ptq.rearrange("p (h t) -> p h t", t=128), R=[pbuf[7]], W=[b_qkT[qi]])
                dst = (QT_d if w == 0 else KT_d)[jt].rearrange("p (h t) -> p h t", t=128)
                P.dma("sp", dst, qkT[qi][:, w, :, :], ("qkst", qi * 2 + w), R=[b_qkT[qi]])

        ys_next = nctx.load_norm(src_tile_ap(l, 0, 1, False)[:, 0, :])
        for jt in range(NT_ALL):
            is_ctx = jt < NCT
            m = 1 if is_ctx else 0
            ys = ys_next
            if jt + 1 < NT_ALL:
                ys_next = nctx.load_norm(src_tile_ap(l, jt + 1, 1, False)[:, 0, :])
            hs = jt % 2
            pb = nctx.ptr_bank
            ptv = bank_bf(pb)
            for kc in range(8):
                P.tr(ptv[:, kc * 128:(kc + 1) * 128], nctx.y[ys][:, kc * 128:(kc + 1) * 128], ident,
                     R=[nctx.b_y[ys], b_ident], W=[pbuf[pb]])
            for kc in range(8):
                P.act(hT[hs][:, kc, :], ptv[:, kc * 128:(kc + 1) * 128], AF.Identity,
                      bias=modF[:, 3 * 8 + kc, m:m + 1], scale=ATab[:, 1, m, kc:kc + 1],
                      R=[pbuf[pb], b_modF, b_ATab], W=[b_hT[hs]])
            for n in range(6):
                for kc in range(8):
                    P.mm(bank(n), hT[hs][:, kc, :], wq[:, kc, n * 512:(n + 1) * 512], kc == 0, kc == 7,
                         R=[b_hT[hs], b_wq[n]], W=[pbuf[n]])
            P.act(qks, bank(0, 4), AF.Copy, R=[pbuf[0], pbuf[1], pbuf[2], pbuf[3]], W=[b_qks])
            qi = jt % 2
            P.act(vaug[qi][:, :, 0:128], bank(4, 2).rearrange("p (h d) -> p h d", d=128), AF.Copy,
                  R=[pbuf[4], pbuf[5]], W=[b_vaug[qi]])
            P.dma("sp", V_d[jt].rearrange("p (h d) -> p h d", d=VW), vaug[qi], ("vst", qi), R=[b_vaug[qi]])
            if pending_qkt[0] is not None:
                qkt(*pending_qkt[0])
            qk_ps = qks
            qb = [b_qks]
            P.act(sqb, qk_ps, AF.Square, R=qb, W=[b_sqb])
            P.red("dve", stg[:, 0:32], sqb.rearrange("p (g d) -> p g d", d=64), R=[b_sqb], W=[b_stg])
            P.ts("dve", stg[:, 0:32], stg[:, 0:32], 1.0 / HD, EPS, ALU.mult, ALU.add, R=[b_stg], W=[b_stg])
            P.rsqrt(stg[:, 32:64], stg[:, 0:32], R=[b_stg], W=[b_stg])
            rs_b = stg[:, 32:64].unsqueeze(2).to_broadcast([128, 32, 64])
            qi = jt % 2
            if is_ctx:
                for w in range(2):
                    P.tt("dve", t1[:, w * D:(w + 1) * D].rearrange("p (g d) -> p g d", d=64),
                         qk_ps[:, w * D:(w + 1) * D].rearrange("p (g d) -> p g d", d=64),
                         gq[:, w:w + 1, :].to_broadcast([128, 16, 64]), ALU.mult, R=qb + [b_tab], W=[b_t1])
            else:
                jl = jt - NCT
                for w in range(2):
                    P.tt("dve", csk[:, w, 0, :], cosT[:, jl, :], gq[:, w, :], ALU.mult, R=[b_tab], W=[b_csk])
                    P.tt("dve", csk[:, w, 1, :], sinT[:, jl, :], gq[:, 2 + w, :], ALU.mult, R=[b_tab], W=[b_csk])
                for w in range(2):
                    xv = qk_ps[:, w * D:(w + 1) * D]
                    P.tt("dve", t1[:, w * D:(w + 1) * D].rearrange("p (g d) -> p g d", d=64),
                         xv.rearrange("p (g d) -> p g d", d=64),
                         csk[:, w, 0:1, :].to_broadcast([128, 16, 64]), ALU.mult, R=qb + [b_csk], W=[b_t1])
                    x5 = xv.rearrange("p (g a h d) -> p g a h d", a=2, h=2, d=16)
                    o5 = t2[:, w * D:(w + 1) * D].rearrange("p (g a h d) -> p g a h d", a=2, h=2, d=16)
                    s4 = csk[:, w, 1, :].rearrange("p (a h d) -> p a h d", a=2, h=2)
                    for hh in range(2):
                        P.tt("dve", o5[:, :, :, hh, :], x5[:, :, :, 1 - hh, :],
                             s4[:, :, hh, :].unsqueeze(1).to_broadcast([128, 16, 2, 16]), ALU.mult,
                             R=qb + [b_csk], W=[b_t2])
                P.tt("dve", t1, t1, t2, ALU.add, R=[b_t1, b_t2], W=[b_t1])
            P.tt("dve", qkh[qi].rearrange("p (g d) -> p g d", d=64), t1.rearrange("p (g d) -> p g d", d=64), rs_b, ALU.mult,
                 R=[b_t1, b_stg], W=[b_qkh[qi]])
            pending_qkt[0] = (jt, qi)
        qkt(*pending_qkt[0])
        P.barrier()
        AR.release(m0)

    def phase_attn_core(l, j_attn, ctx_out, lam_init):
        m0 = AR.mark()
        KT = AR.alloc([NT_ALL, NH, 128], BF16)
        VA = AR.alloc([NT_ALL, NH, VW], BF16)
        b_KT = [Buf() for _ in range(NT_ALL)]
        b_VA = [Buf() for _ in range(NT_ALL)]
        wo = AR.alloc([NH, D], BF16)
        b_wo = Buf()
        grp = [(0, 2)] + [(2 + 4 * i, 4) for i in range(8)]
        for gi_, (f, n) in enumerate(grp):
            o1 = P.dma("sp", KT[:, f:f + n], KT_d[f:f + n].rearrange("t p (h k) -> p t h k", k=128), ("kv", 2 * gi_),
                       W=[b_KT[t] for t in range(f, f + n)])
            o2 = P.dma("sp", VA[:, f:f + n], V_d[f:f + n].rearrange("t p (h k) -> p t h k", k=VW), ("kv", 2 * gi_ + 1),
                       W=[b_VA[t] for t in range(f, f + n)])
        P.dma("pool", wo, attn_w_o[j_attn].rearrange("(h p) d -> p h d", p=128), ("w", 0), W=[b_wo])
        lamb = AR.alloc([4, 64], F32)
        b_lam = Buf()
        P.dma("sp", lamb, attn_lam[:, j_attn], ("misc", 0), W=[b_lam])
        lw = AR.alloc([2, 64], F32)
        b_lw = Buf()
        lv = AR.alloc([8], F32)
        b_lv = Buf()
        for i in range(2):
            P.tt("dve", lw[:, i, :], lamb[:, 2 * i, :], lamb[:, 2 * i + 1, :], ALU.mult, R=[b_lam], W=[b_lw])
        P.red("dve", lv[:, 0:2], lw, R=[b_lw], W=[b_lv])
        P.act(lv[:, 2:4], lv[:, 0:2], AF.Exp, R=[b_lv], W=[b_lv])
        P.tt("dve", lv[:, 4:5], lv[:, 2:3], lv[:, 3:4], ALU.subtract, R=[b_lv], W=[b_lv])
        P.ts("dve", lv[:, 5:6], lv[:, 4:5], lam_init, -1.0, ALU.add, ALU.mult, R=[b_lv], W=[b_lv])
        sub = AR.alloc([2], F32)
        b_sub = Buf()
        P.dma("sp", sub, attn_sublnF, ("misc", 1), W=[b_sub])
        P.ts("dve", lv[:, 6:7], sub[:, j_attn:j_attn + 1], 1.0 - lam_init, None, ALU.mult, R=[b_sub, b_lv], W=[b_lv])

        QT = [AR.alloc([4, NH, 128], BF16) for _ in range(1)]
        b_QT = [Buf() for _ in range(1)]
        NPT = 4
        PT = [AR.alloc([512], BF16) for _ in range(NPT)]
        b_PT = [Buf() for _ in range(NPT)]
        ow = AR.alloc([8, 128], F32)
        b_ow = [Buf() for _ in range(8)]
        rr = AR.alloc([8, 4], F32)
        b_rr = [Buf() for _ in range(8)]
        on = [AR.alloc([128], BF16) for _ in range(4)]
        b_on = [Buf() for _ in range(4)]
        oT = AR.alloc([NH, 512], BF16)
        b_oT = Buf()
        rc = ResCtx(l, 1, o_banks=[7])
        acc_ps = bank(3, 3)
        accv = []
        b_acc = []
        for tt in range(4):
            row = []
            for c in range(2):
                idx = tt * 2 + c
                bnk = idx // 3
                off = bnk * 512 + (idx % 3) * 132
                row.append(acc_ps[:, off:off + 129])
            accv.append(row)
            b_acc.append([Buf(), Buf()])
        pto = bank_bf(6)
        scnt = [0]
        pcnt = [0]
        owc = [0]

        qblocks = []
        if ctx_out:
            qblocks.append((0, NCT, list(range(NCT)), 1))
        for b in range(SEQ // 512):
            qblocks.append((NCT + 4 * b, 4, list(range(NT_ALL)), 0))

        for qb_i, (first, n, kcs, m) in enumerate(qblocks):
            N = n * 128
            qs = 0
            P.dma("sp", QT[qs][:, 0:n], QT_d[first:first + n].rearrange("t p (h k) -> p t h k", k=128), ("qt", qs),
                  W=[b_QT[qs]])
            rc.set_gate(m)
            deferred = [None]

            def finish_pe(h_):
                for tt in range(n):
                    P.tr(pto[:, tt * 128:(tt + 1) * 128], on[tt], ident, R=[b_on[tt], b_ident], W=[pbuf[6]])
                P.act(oT[:, h_, 0:N], pto[:, 0:N], AF.Identity, scale=lv[:, 6:7], R=[pbuf[6], b_lv], W=[b_oT])

            its = [(h, ki, c) for h in range(NH) for ki in range(len(kcs)) for c in range(2)]
            info = {}
            seen = [set()]

            def S_stage(i):
                h, ki, c = its[i]
                kc = kcs[ki]
                sb = scnt[0] % 3
                scnt[0] += 1
                P.mm(bank(sb)[:, 0:N].rearrange("p (t k) -> p t k", k=128), KT[64 * c:64 * (c + 1), kc, h, :],
                     QT[qs][64 * c:64 * (c + 1), 0:n, h, :], True, True,
                     R=[b_KT[kc], b_QT[qs]], W=[pbuf[sb]])
                pi = pcnt[0] % NPT
                pcnt[0] += 1
                P.act(PT[pi][:, 0:N], bank(sb)[:, 0:N], AF.Exp, scale=HD ** -0.5, R=[pbuf[sb]], W=[b_PT[pi]])
                info[i] = pi

            def AV_stage(i):
                h, ki, c = its[i]
                kc = kcs[ki]
                pi = info.pop(i)
                if c == 0 and ki == min(2, len(kcs) - 1) and deferred[0] is not None:
                    finish_pe(deferred[0])
                    deferred[0] = None
                for tt in range(n):
                    idx = tt * 2 + c
                    first_in_bank = False
                    if ki == 0:
                        if c == 0 and tt == 0:
                            seen[0] = set()
                        if idx // 3 not in seen[0]:
                            seen[0].add(idx // 3)
                            first_in_bank = True
                    P.mm(accv[tt][c], PT[pi][:, tt * 128:(tt + 1) * 128], VA[:, kc, h, 0:129],
                         first_in_bank, ki == len(kcs) - 1, R=[b_PT[pi], b_VA[kc]], W=[b_acc[tt][c]], skip=True)
                if not (ki == len(kcs) - 1 and c == 1):
                    return
                o1s = []
                for tt in range(n):
                    ri = (h * 4 + tt) % 8
                    r = rr[:, ri, :]
                    wi_ = owc[0] % 8
                    owc[0] += 1
                    o1 = ow[:, wi_, :]
                    o1s.append((o1, wi_, r, ri))
                    for c in range(2):
                        P.recip(r[:, c:c + 1], accv[tt][c][:, 128:129], R=[b_acc[tt][c]], W=[b_rr[ri]])
                    P.ts("dve", o1, accv[tt][0][:, 0:128], r[:, 0:1], None, ALU.mult, R=[b_acc[tt][0], b_rr[ri]], W=[b_ow[wi_]])
                    P.tt("dve", r[:, 1:2], r[:, 1:2], lv[:, 5:6], ALU.mult, R=[b_rr[ri], b_lv], W=[b_rr[ri]])
                    P.stt("dve", o1, accv[tt][1][:, 0:128], r[:, 1:2], o1, ALU.mult, ALU.add,
                          R=[b_acc[tt][1], b_rr[ri], b_ow[wi_]], W=[b_ow[wi_]])
                for tt in range(n):
                    o1, wi_, r, ri = o1s[tt]
                    wj = owc[0] % 8
                    owc[0] += 1
                    P.act(ow[:, wj, :], o1, AF.Square, R=[b_ow[wi_]], W=[b_ow[wj]])
                    P.red("dve", r[:, 2:3], ow[:, wj, :], R=[b_ow[wj]], W=[b_rr[ri]])
                    P.ts("dve", r[:, 2:3], r[:, 2:3], 1.0 / 128, EPS, ALU.mult, ALU.add, R=[b_rr[ri]], W=[b_rr[ri]])
                    P.rsqrt(r[:, 3:4], r[:, 2:3], R=[b_rr[ri]], W=[b_rr[ri]], lnexp=True)
                    P.ts("dve", on[tt], o1, r[:, 3:4], None, ALU.mult, R=[b_ow[wi_], b_rr[ri]], W=[b_on[tt]])
                deferred[0] = h

            S_stage(0)
            P.mm(bank(7)[:, 0:128], ident, ident, True, True, R=[b_ident], W=[pbuf[7]])
            for i in range(len(its)):
                if i + 1 < len(its):
                    S_stage(i + 1)
                AV_stage(i)
            finish_pe(deferred[0])
            for tt in range(n):
                i = rc.begin_tile(first + tt, False)
                for dc in range(2):
                    pb = rc.next_obank()
                    for h in range(NH):
                        P.mm(bank(pb), oT[:, h, tt * 128:(tt + 1) * 128], wo[:, h, dc * 512:(dc + 1) * 512], h == 0, h == NH - 1,
                             R=[b_oT, b_wo], W=[pbuf[pb]])
                    rc.update(i, pb, dc)
                rc.end_tile(i, first + tt, False)
        P.barrier()
        AR.release(m0)

    def dwconv_segments(with_ctx):
        return ([(0, NCT, 1)] if with_ctx else []) + [(NCT, SEQ // 128, 0)]

    def phase_conf(l, ctx_out):
        PAD = CK // 2
        m0 = AR.mark()
        uL = AR.alloc([8, SEQ + 2 * PAD], BF16)
        uC = AR.alloc([8, CTX + 2 * PAD], BF16)
        b_u = Buf()
        for c in range(8):
            P.memset("dve", uL[:, c, 0:PAD], 0.0, W=[b_u])
            P.memset("dve", uL[:, c, PAD + SEQ:], 0.0, W=[b_u])
            P.memset("dve", uC[:, c, 0:PAD], 0.0, W=[b_u])
            P.memset("dve", uC[:, c, PAD + CTX:], 0.0, W=[b_u])
        blocks = _tiles_ffn(ctx_out)
        m1 = AR.mark()
        w_in = AR.alloc([8, 2 * D], BF16)
        b_win = [Buf() for _ in range(4)]
        wsrc = conv_w_in.rearrange("(k p) n -> p k n", p=128)
        for g in range(4):
            P.dma("pool", w_in[:, :, g * 256:(g + 1) * 256], wsrc[:, :, g * 256:(g + 1) * 256], ("w", 2 * g), W=[b_win[g]])
            P.dma("pool", w_in[:, :, D + g * 256:D + (g + 1) * 256], wsrc[:, :, D + g * 256:D + (g + 1) * 256], ("w", 2 * g + 1),
                  W=[b_win[g]])
        binF = AR.alloc([16], F32)
        b_bin = Buf()
        P.dma("sp", binF, conv_b_inF, ("misc", 0), W=[b_bin])
        nctx = NormCtx(ptr_bank=6)
        hT = AR.alloc([8, 512], BF16)
        b_hT = Buf()
        sg = [AR.alloc([512], F32) for _ in range(2)]
        b_sg = [Buf() for _ in range(2)]
        for (first, n, is_ctx) in blocks:
            m = 1 if is_ctx else 0
            N = n * 128
            pb = nctx.ptr_bank
            ptv = bank_bf(pb)
            for tt in range(n):
                ys = nctx.load_norm(src_tile_ap(l, first + tt, 1, False)[:, 0, :])
                for kc in range(8):
                    P.tr(ptv[:, kc * 128:(kc + 1) * 128], nctx.y[ys][:, kc * 128:(kc + 1) * 128], ident,
                         R=[nctx.b_y[ys], b_ident], W=[pbuf[pb]])
                for kc in range(8):
                    P.act(hT[:, kc, tt * 128:(tt + 1) * 128], ptv[:, kc * 128:(kc + 1) * 128], AF.Identity,
                          bias=modF[:, 3 * 8 + kc, m:m + 1], scale=ATab[:, 1, m, kc:kc + 1],
                          R=[pbuf[pb], b_modF, b_ATab], W=[b_hT])
            ubuf = uC if is_ctx else uL
            t0 = PAD + (first * 128 if is_ctx else (first - NCT) * 128)
            for c in range(8):
                g = c // 2
                pa = (c % 2) * 2
                pg = pa + 1
                for kc in range(8):
                    P.mm(bank(pa)[:, 0:N], w_in[:, kc, c * 128:(c + 1) * 128], hT[:, kc, 0:N], kc == 0, kc == 7,
                         R=[b_win[g], b_hT], W=[pbuf[pa]])
                for kc in range(8):
                    P.mm(bank(pg)[:, 0:N], w_in[:, kc, D + c * 128:D + (c + 1) * 128], hT[:, kc, 0:N], kc == 0, kc == 7,
                         R=[b_win[g], b_hT], W=[pbuf[pg]])
                i = c % 2
                P.act(sg[i][:, 0:N], bank(pg)[:, 0:N], AF.Sigmoid, bias=binF[:, 8 + c:9 + c], R=[pbuf[pg], b_bin], W=[b_sg[i]])
                P.stt("dve", ubuf[:, c, t0:t0 + N], bank(pa)[:, 0:N], binF[:, c:c + 1], sg[i][:, 0:N], ALU.add, ALU.mult,
                      R=[pbuf[pa], b_sg[i], b_bin], W=[b_u])
        P.barrier()
        AR.release(m1)
        NB = 256
        dg = AR.alloc([8, CK, 128], BF16)
        b_dg = Buf()
        dwF = AR.alloc([CK, 8], F32)
        vecF = AR.alloc([3, 8], F32)
        b_dw = Buf()
        P.dma("sp", dwF, conv_dwF, ("misc", 0), W=[b_dw])
        P.dma("sp", vecF, conv_vecF, ("misc", 1), W=[b_dw])
        for c in range(8):
            for k in range(CK):
                P.ts("dve", dg[:, c, k, :], identf, dwF[:, k, c:c + 1], None, ALU.mult, R=[b_ident, b_dw], W=[b_dg])
        w_out = AR.alloc([8, D], BF16)
        b_wout = Buf()
        P.dma("pool", w_out, conv_w_out.rearrange("(k p) n -> p k n", p=128), ("w", 0), W=[b_wout])
        bo = AR.alloc([D], BF16)
        b_bo = Buf()
        P.dma("pool", bo[0:1, :], conv_b_out, ("w", 1), W=[b_bo])
        om = AR.alloc([128], BF16)
        b_om = Buf()
        P.memset("dve", om, 1.0 / D, W=[b_om])
        vT = AR.alloc([8, NB], F32)
        b_vT = Buf()
        vb = AR.alloc([8, NB], BF16)
        b_vb = Buf()
        v2 = AR.alloc([8, NB], BF16)
        b_v2 = Buf()
        mr = AR.alloc([3, NB], F32)
        b_mr = Buf()
        zt = [AR.alloc([NB], F32) for _ in range(2)]
        b_zt = [Buf() for _ in range(2)]
        sT = AR.alloc([8, NB], BF16)
        b_sT = Buf()
        rc = ResCtx(l, 1, o_banks=[6, 7])
        segs = ([(0, CTX, uC, 1)] if ctx_out else []) + [(NCT, SEQ, uL, 0)]
        ccnt = [0]
        for (ft, ntok, ubuf, m) in segs:
            rc.set_gate(m)
            for b0 in range(0, ntok, NB):
                for c in range(8):
                    pb = ccnt[0] % 2
                    ccnt[0] += 1
                    for k in range(CK):
                        P.mm(bank(pb)[:, 0:NB], dg[:, c, k, :], ubuf[:, c, b0 + k:b0 + k + NB], k == 0, k == CK - 1,
                             R=[b_dg, b_u], W=[pbuf[pb]])
                    P.act(vT[:, c, :], bank(pb)[:, 0:NB], AF.Identity, bias=vecF[:, 0, c:c + 1], R=[pbuf[pb], b_dw], W=[b_vT])
                    P.cp("dve", vb[:, c, :], vT[:, c, :], R=[b_vT], W=[b_vb])
                    P.tt("dve", v2[:, c, :], vT[:, c, :], vT[:, c, :], ALU.mult, R=[b_vT], W=[b_v2])
                for c in range(8):
                    P.mm(bank(2)[:, 0:NB], om, vb[:, c, :], c == 0, c == 7, R=[b_om, b_vb], W=[pbuf[2]])
                for c in range(8):
                    P.mm(bank(3)[:, 0:NB], om, v2[:, c, :], c == 0, c == 7, R=[b_om, b_v2], W=[pbuf[3]])
                P.cp("dve", mr[:, 0, :], bank(2)[:, 0:NB], R=[pbuf[2]], W=[b_mr])
                P.tt("dve", mr[:, 2, :], mr[:, 0, :], mr[:, 0, :], ALU.mult, R=[b_mr], W=[b_mr])
                P.tt("dve", mr[:, 1, :], bank(3)[:, 0:NB], mr[:, 2, :], ALU.subtract, R=[pbuf[3], b_mr], W=[b_mr])
                P.ts("dve", mr[:, 1, :], mr[:, 1, :], EPS, None, ALU.add, R=[b_mr], W=[b_mr])
                P.rsqrt(mr[:, 1, :], mr[:, 1, :], R=[b_mr], W=[b_mr])
                for c in range(8):
                    zi = c % 2
                    P.tt("dve", zt[zi], vT[:, c, :], mr[:, 0, :], ALU.subtract, R=[b_vT, b_mr], W=[b_zt[zi]])
                    P.tt("dve", zt[zi], zt[zi], mr[:, 1, :], ALU.mult, R=[b_zt[zi], b_mr], W=[b_zt[zi]])
                    P.act(sT[:, c, :], zt[zi], AF.Silu, bias=vecF[:, 2, c:c + 1], scale=vecF[:, 1, c:c + 1],
                          R=[b_zt[zi], b_dw], W=[b_sT])
                for tt in range(NB // 128):
                    tile = ft + b0 // 128 + tt
                    i = rc.begin_tile(tile, False)
                    for dc in range(2):
                        pb = rc.next_obank()
                        for c in range(8):
                            P.mm(bank(pb), sT[:, c, tt * 128:(tt + 1) * 128], w_out[:, c, dc * 512:(dc + 1) * 512], c == 0, False,
                                 R=[b_sT, b_wout], W=[pbuf[pb]])
                        P.mm(bank(pb), ones_row[0:1, :], bo[0:1, dc * 512:(dc + 1) * 512], False, True,
                             R=[b_ones, b_bo], W=[pbuf[pb]])
                        rc.update(i, pb, dc)
                    rc.end_tile(i, tile, False)
        P.barrier()
        AR.release(m0)

    def phase_sconv(l, ctx_out):
        PAD = SK // 2
        m0 = AR.mark()
        pL = AR.alloc([8, SEQ + 2 * PAD], BF16)
        pC = AR.alloc([8, CTX + 2 * PAD], BF16)
        b_p = Buf()
        for c in range(8):
            P.memset("dve", pL[:, c, 0:PAD], 0.0, W=[b_p])
            P.memset("dve", pL[:, c, PAD + SEQ:], 0.0, W=[b_p])
            P.memset("dve", pC[:, c, 0:PAD], 0.0, W=[b_p])
            P.memset("dve", pC[:, c, PAD + CTX:], 0.0, W=[b_p])
        blocks = _tiles_ffn(ctx_out)
        m1 = AR.mark()
        w_in = AR.alloc([8, 3 * D], BF16)
        b_win = [Buf() for _ in range(4)]
        wsrc = sc_w_in.rearrange("(k p) n -> p k n", p=128)
        for g in range(4):
            for part in range(3):
                P.dma("pool", w_in[:, :, part * D + g * 256:part * D + (g + 1) * 256],
                      wsrc[:, :, part * D + g * 256:part * D + (g + 1) * 256], ("w", 3 * g + part), W=[b_win[g]])
        nctx = NormCtx(ptr_bank=6)
        hT = AR.alloc([8, 512], BF16)
        b_hT = Buf()
        xh = [AR.alloc([512], F32) for _ in range(2)]
        b_xh = [Buf() for _ in range(2)]
        bT = [AR.alloc([8, 128], BF16) for _ in range(8)]
        b_bT = [Buf() for _ in range(8)]
        btc = [0]
        for (first, n, is_ctx) in blocks:
            m = 1 if is_ctx else 0
            N = n * 128
            pb = nctx.ptr_bank
            ptv = bank_bf(pb)
            for tt in range(n):
                ys = nctx.load_norm(src_tile_ap(l, first + tt, 1, False)[:, 0, :])
                for kc in range(8):
                    P.tr(ptv[:, kc * 128:(kc + 1) * 128], nctx.y[ys][:, kc * 128:(kc + 1) * 128], ident,
                         R=[nctx.b_y[ys], b_ident], W=[pbuf[pb]])
                for kc in range(8):
                    P.act(hT[:, kc, tt * 128:(tt + 1) * 128], ptv[:, kc * 128:(kc + 1) * 128], AF.Identity,
                          bias=modF[:, 3 * 8 + kc, m:m + 1], scale=ATab[:, 1, m, kc:kc + 1],
                          R=[pbuf[pb], b_modF, b_ATab], W=[b_hT])
            pbuf_ = pC if is_ctx else pL
            t0 = PAD + (first * 128 if is_ctx else (first - NCT) * 128)
            slots = []
            for tt in range(n):
                slots.append(btc[0] % 8)
                btc[0] += 1
            for c in range(8):
                g = c // 2
                base = (c % 2) * 3
                for part in range(3):
                    for kc in range(8):
                        P.mm(bank(base + part)[:, 0:N], w_in[:, kc, part * D + c * 128:part * D + (c + 1) * 128], hT[:, kc, 0:N],
                             kc == 0, kc == 7, R=[b_win[g], b_hT], W=[pbuf[base + part]])
                i = c % 2
                P.act(xh[i][:, 0:N], bank(base + 2)[:, 0:N], AF.Copy, R=[pbuf[base + 2]], W=[b_xh[i]])
                P.tt("dve", pbuf_[:, c, t0:t0 + N], bank(base + 1)[:, 0:N], xh[i][:, 0:N], ALU.mult,
                     R=[pbuf[base + 1], b_xh[i]], W=[b_p])
                for tt in range(n):
                    P.act(bT[slots[tt]][:, c, :], bank(base)[:, tt * 128:(tt + 1) * 128], AF.Copy, R=[pbuf[base]],
                          W=[b_bT[slots[tt]]])
            for tt in range(n):
                P.dma("sp", bT_d[first + tt].rearrange("p (c t) -> p c t", t=128), bT[slots[tt]], ("bst", slots[tt]),
                      R=[b_bT[slots[tt]]])
        P.barrier()
        AR.release(m1)
        dg = AR.alloc([8, SK, 128], BF16)
        b_dg = Buf()
        dwF = AR.alloc([SK, 8], F32)
        b_dw = Buf()
        P.dma("sp", dwF, sc_dwF, ("misc", 0), W=[b_dw])
        for c in range(8):
            for k in range(SK):
                P.ts("dve", dg[:, c, k, :], identf, dwF[:, k, c:c + 1], None, ALU.mult, R=[b_ident, b_dw], W=[b_dg])
        w_out = AR.alloc([8, D], BF16)
        b_wout = Buf()
        P.dma("pool", w_out, sc_w_out.rearrange("(k p) n -> p k n", p=128), ("w", 0), W=[b_wout])
        bTl = [AR.alloc([4, 8, 128], BF16) for _ in range(2)]
        b_bTl = [Buf() for _ in range(2)]
        yT = AR.alloc([8, 512], BF16)
        b_yT = Buf()
        rc = ResCtx(l, 1, o_banks=[6, 7])
        ccnt = [0]
        for bi, (first, n, is_ctx) in enumerate(blocks):
            m = 1 if is_ctx else 0
            N = n * 128
            rc.set_gate(m)
            pbuf_ = pC if is_ctx else pL
            b0 = first * 128 if is_ctx else (first - NCT) * 128
            bs = bi % 2
            P.dma("sp", bTl[bs][:, 0:n], bT_d[first:first + n].rearrange("t p (c k) -> p t c k", k=128), ("btl", bs),
                  W=[b_bTl[bs]])
            for c in range(8):
                pb = ccnt[0] % 4
                ccnt[0] += 1
                for k in range(SK):
                    P.mm(bank(pb)[:, 0:N], dg[:, c, k, :], pbuf_[:, c, b0 + k:b0 + k + N], k == 0, k == SK - 1,
                         R=[b_dg, b_p], W=[pbuf[pb]])
                P.tt("dve", yT[:, c, 0:N].rearrange("p (t k) -> p t k", k=128), bank(pb)[:, 0:N].rearrange("p (t k) -> p t k", k=128),
                     bTl[bs][:, 0:n, c, :], ALU.mult, R=[pbuf[pb], b_bTl[bs]], W=[b_yT])
            for tt in range(n):
                i = rc.begin_tile(first + tt, False)
                for dc in range(2):
                    pb = rc.next_obank()
                    for c in range(8):
                        P.mm(bank(pb), yT[:, c, tt * 128:(tt + 1) * 128], w_out[:, c, dc * 512:(dc + 1) * 512], c == 0, c == 7,
                             R=[b_yT, b_wout], W=[pbuf[pb]])
                    rc.update(i, pb, dc)
                rc.end_tile(i, first + tt, False)
        P.barrier()
        AR.release(m0)

    for l in range(n_layers):
        kind = l % 3
        j = l // 3
        last = l == DEPTH - 1
        ctx_in_needed = (not last) or kind == 0
        ctx_out_needed = not last
        phase_mods(l)
        phase_ffn(l, 0, ctx_in_needed, first_ffn=(l == 0), final=False)
        if kind == 0:
            lam_init = 0.8 - 0.6 * math.exp(-0.3 * l)
            phase_attn_qkv(l, j)
            phase_attn_core(l, j, ctx_out_needed, lam_init)
        elif kind == 1:
            phase_conf(l, ctx_out_needed)
        else:
            phase_sconv(l, ctx_out_needed)
        phase_ffn(l, 2, ctx_out_needed, first_ffn=False, final=(l == n_layers - 1))
    P.wait_all("sp", store_ops)
    P.emit()
    return nc, AR.peak, {e: len(q) for e, q in P.q.items()}


def _fm(v, nchunk):
    return np.ascontiguousarray(np.asarray(v, np.float32).reshape(nchunk, 128).T)


def _rope_tables():
    rows = SEQ // 64
    row_ids = np.repeat(np.arange(rows, dtype=np.float32), 64)
    col_ids = np.tile(np.arange(64, dtype=np.float32), rows)
    half = HD // 2
    inv_freq = (np.float32(10000.0) ** (-np.arange(0, half, 2, dtype=np.float32) / np.float32(half))).astype(np.float32)
    ang_r = row_ids[:, None] * inv_freq
    ang_c = col_ids[:, None] * inv_freq
    ang = np.concatenate([ang_r, ang_r, ang_c, ang_c], axis=-1)
    cos = np.cos(ang).astype(np.float32)
    sin = np.sin(ang).astype(np.float32)
    sgn = np.concatenate([-np.ones(16), np.ones(16), -np.ones(16), np.ones(16)]).astype(np.float32)
    sin_s = sin * sgn
    to = lambda t: np.ascontiguousarray(t.reshape(SEQ // 128, 128, 64).transpose(1, 0, 2))
    return to(cos), to(sin_s)


def _swap_idx():
    return np.concatenate([np.arange(16, 32), np.arange(0, 16), np.arange(48, 64), np.arange(32, 48)])


def make_in_maps(inputs):
    f = lambda k: np.asarray(inputs[k], np.float32)
    x, c, ctx, c_ctx = f("x"), f("c"), f("ctx"), f("c_ctx")
    B = x.shape[0]
    cos, sin_s = _rope_tables()
    norm_g = f("norm_g")
    norm_gF = np.ascontiguousarray(norm_g.reshape(DEPTH, 3, 8, 128).transpose(3, 0, 1, 2))
    qg, kg = f("attn_q_g"), f("attn_k_g")
    sw = _swap_idx()
    ag = np.stack([qg, kg, qg[:, sw], kg[:, sw]], axis=1)
    attn_g = np.ascontiguousarray(np.broadcast_to(ag[None], (128, 2, 4, 64)))
    attn_lam = np.ascontiguousarray(np.broadcast_to(f("attn_lambda")[None], (128, 2, 4, 64)))
    attn_sublnF = np.ascontiguousarray(f("attn_subln_g").T)
    conv_b_inF = _fm(f("conv_b_in")[0], 16)
    conv_dwF = np.ascontiguousarray(f("conv_dw_w")[0].reshape(CK, 8, 128).transpose(2, 0, 1))
    conv_vecF = np.ascontiguousarray(np.stack([_fm(f("conv_dw_b")[0], 8), _fm(f("conv_ln_g")[0], 8), _fm(f("conv_ln_b")[0], 8)], axis=1))
    sc_dwF = np.ascontiguousarray(f("sc_dw_w")[0].reshape(SK, 8, 128).transpose(2, 0, 1))
    shared = {
        "ada_w": f("ada_w"), "ada_b": f("ada_b").reshape(DEPTH, 1, NMOD * D), "norm_gF": norm_gF,
        "ffn_w_in": f("ffn_w_in"), "ffn_w_out": f("ffn_w_out"),
        "attn_w_qkv": f("attn_w_qkv"), "attn_w_o": f("attn_w_o"), "attn_g": attn_g, "attn_lam": attn_lam,
        "attn_sublnF": attn_sublnF, "rope_cos": cos, "rope_sin": sin_s,
        "conv_w_in": f("conv_w_in")[0], "conv_b_inF": conv_b_inF, "conv_dwF": conv_dwF, "conv_vecF": conv_vecF,
        "conv_w_out": f("conv_w_out")[0], "conv_b_out": f("conv_b_out").reshape(1, D),
        "sc_w_in": f("sc_w_in")[0], "sc_dwF": sc_dwF, "sc_w_out": f("sc_w_out")[0],
    }
    maps = []
    for b in range(B):
        cv = np.ascontiguousarray(np.stack([c[b].reshape(8, 128).T, c_ctx.reshape(8, 128).T], axis=-1))
        mp = dict(shared)
        mp.update({"x": np.ascontiguousarray(x[b]), "ctx": np.ascontiguousarray(ctx[b]), "cvec": cv})
        maps.append(mp)
    return maps


def run(inputs, n_layers=DEPTH, debug=False, trace=False):
    nc, peak, counts = build_program(n_layers=n_layers, debug=debug)
    maps = make_in_maps(inputs)
    res = run_bass_kernel_spmd(nc, maps, core_ids=list(range(len(maps))), trace=trace)
    out = np.stack([r["out"] for r in res.results], axis=0)
    if debug:
        return out, res
    return out


def kernel(**inputs):
    return run(inputs).astype(np.float32)
```

```python
import math
import os
import numpy as np
import concourse.bass as bass
import concourse.mybir as mybir
from concourse.bass_utils import run_bass_kernel_spmd

F32 = mybir.dt.float32
BF16 = mybir.dt.bfloat16
AF = mybir.ActivationFunctionType
ALU = mybir.AluOpType
AX = mybir.AxisListType

D = 1024
DEPTH = 4
SEQ = 4096
CTX = 256
NH = 8
HD = 64
FF = 2816
NMOD = 9
EPS = 1e-6
CK = 31
SK = 3
NT_ALL = (SEQ + CTX) // 128
NCT = CTX // 128
VW = 130

DSZ = {F32: 4, BF16: 2}
ATTACH_WAITS = False


class Op:
    __slots__ = ("eng", "fn", "deps", "sig", "count", "idx", "key", "order")


class Buf:
    __slots__ = ("w", "r", "pr")

    def __init__(self):
        self.w = {}
        self.r = {}
        self.pr = {}


class Prog:
    ENG = ("pe", "act", "dve", "pool", "sp")
    GAP = 4

    def __init__(self, nc):
        self.nc = nc
        self.q = {e: [] for e in self.ENG}
        self.esem = {e: nc.alloc_semaphore("sem_" + e) for e in ("pe", "act", "dve", "pool")}
        self.ksem = {}
        self.kcnt = {}
        self.klast = {}
        self.pending = {e: {} for e in self.ENG}

    @staticmethod
    def _sid(o):
        return ("k", o.key) if o.key is not None else ("e", o.eng)

    def _mk(self, eng, fn, R=(), W=(), deps=(), key=None):
        op = Op()
        op.eng = eng
        op.fn = fn
        op.sig = False
        op.key = key
        op.idx = len(self.q[eng])
        op.count = None
        d = {}

        def add(o):
            k = self._sid(o)
            if k not in d or o.order > d[k].order:
                d[k] = o

        for o in deps:
            add(o)
        for o in self.pending[eng].values():
            add(o)
        self.pending[eng] = {}
        for b in R:
            for o in b.w.values():
                add(o)
        for b in W:
            for o in b.r.values():
                add(o)
            for o in b.pr.values():
                add(o)
        if key is not None:
            if key not in self.ksem:
                self.ksem[key] = self.nc.alloc_semaphore("k_" + "_".join(str(x) for x in key))
                self.kcnt[key] = 0
            self.kcnt[key] += 16
            op.count = self.kcnt[key]
            op.order = op.count
            self.klast[key] = op
        else:
            op.order = op.idx
        for o in d.values():
            o.sig = True
        op.deps = list(d.values())
        sid = self._sid(op)
        for b in R:
            b.r[sid] = op
        for b in W:
            if b.r and not any(b is rb for rb in R):
                b.pr = b.r
                b.r = {}
                b.w = {}
            elif any(b is rb for rb in R):
                b.pr = {k: v for k, v in b.r.items() if v is not op}
                b.r = {}
                b.w = {}
            b.w[sid] = op
        self.q[eng].append(op)
        return op

    def barrier(self):
        last = {}
        for e in ("pe", "act", "dve", "pool"):
            for o in reversed(self.q[e]):
                if o.key is None and o.fn is not None:
                    last[("e", e)] = o
                    o.sig = True
                    break
        for k, o in self.klast.items():
            last[("k", k)] = o
        self.klast = {}
        for e in self.ENG:
            self.pending[e] = dict(last)

    def mm(self, out, lhsT, rhs, start, stop, R=(), W=(), skip=False):
        if skip:
            return self._mk("pe", lambda e: e.matmul(out, lhsT=lhsT, rhs=rhs, start=start, stop=stop, skip_group_check=True), R, W)
        return self._mk("pe", lambda e: e.matmul(out, lhsT=lhsT, rhs=rhs, start=start, stop=stop), R, W)

    def tr(self, out, in_, ident, R=(), W=()):
        return self._mk("pe", lambda e: e.transpose(out=out, in_=in_, identity=ident), R, W)

    def act(self, out, in_, func, bias=None, scale=None, R=(), W=()):
        kw = {}
        if bias is not None:
            kw["bias"] = bias
        if scale is not None:
            kw["scale"] = scale
        return self._mk("act", lambda e: e.activation(out=out, in_=in_, func=func, **kw), R, W)

    def ts(self, eng, out, in0, s1, s2, op0, op1=None, R=(), W=()):
        if op1 is None:
            return self._mk(eng, lambda e: e.tensor_scalar(out=out, in0=in0, scalar1=s1, scalar2=None, op0=op0), R, W)
        return self._mk(eng, lambda e: e.tensor_scalar(out=out, in0=in0, scalar1=s1, scalar2=s2, op0=op0, op1=op1), R, W)

    def tt(self, eng, out, in0, in1, op, R=(), W=()):
        return self._mk(eng, lambda e: e.tensor_tensor(out=out, in0=in0, in1=in1, op=op), R, W)

    def stt(self, eng, out, in0, scalar, in1, op0, op1, R=(), W=()):
        return self._mk(eng, lambda e: e.scalar_tensor_tensor(out=out, in0=in0, scalar=scalar, in1=in1, op0=op0, op1=op1), R, W)

    def red(self, eng, out, in_, R=(), W=()):
        return self._mk(eng, lambda e: e.reduce_sum(out=out, in_=in_, axis=AX.X), R, W)

    def cp(self, eng, out, in_, R=(), W=()):
        return self._mk(eng, lambda e: e.tensor_copy(out=out, in_=in_), R, W)

    def rsqrt(self, out, in_, R=(), W=(), lnexp=False):
        if lnexp:
            self._mk("act", lambda e: e.activation(out=out, in_=in_, func=AF.Ln), list(R), list(W))
            return self._mk("act", lambda e: e.activation(out=out, in_=out, func=AF.Exp, scale=-0.5), list(W), list(W))
        self._mk("act", lambda e: e.activation(out=out, in_=in_, func=AF.Sqrt), list(R), list(W))
        return self._mk("dve", lambda e: e.reciprocal(out=out, in_=out), list(W), list(W))

    def recip(self, out, in_, R=(), W=()):
        return self._mk("dve", lambda e: e.reciprocal(out=out, in_=in_), R, W)

    def memset(self, eng, ap, val, R=(), W=()):
        return self._mk(eng, lambda e: e.memset(ap, val), R, W)

    def dma(self, q, out, in_, key, R=(), W=()):
        return self._mk(q, lambda e: e.dma_start(out=out, in_=in_), R, W, key=key)

    def wait_all(self, eng, ops):
        return self._mk(eng, None, deps=ops)

    def emit(self):
        nc = self.nc
        for e in ("pe", "act", "dve", "pool"):
            c = 0
            for o in self.q[e]:
                if o.key is None and o.sig:
                    c += 1
                    o.count = c
        handles = {}

        def flush(ename, e):
            waited = {}
            for o in self.q[ename]:
                need = []
                for dpn in o.deps:
                    if dpn.key is not None:
                        sem = self.ksem[dpn.key]
                        sid = ("k", dpn.key)
                    else:
                        if dpn.eng == ename:
                            if ename == "pe":
                                continue
                            if ename != "pool" and o.idx - dpn.idx > self.GAP:
                                continue
                        sem = self.esem[dpn.eng]
                        sid = ("e", dpn.eng)
                    if waited.get(sid, 0) >= dpn.count:
                        continue
                    need.append((sem, dpn.count))
                    waited[sid] = dpn.count
                if o.fn is None:
                    for sem, cnt in need:
                        e.wait_ge(sem, cnt)
                    continue
                attach = ATTACH_WAITS and o.key is None and ename in ("pe", "act", "dve")
                for sem, cnt in (need[:-1] if attach else need):
                    e.wait_ge(sem, cnt)
                ins = o.fn(e)
                if need and attach:
                    ins._wait_ge(need[-1][0], need[-1][1])
                if o.key is not None:
                    ins.then_inc(self.ksem[o.key], 16)
                elif o.sig:
                    ins.then_inc(self.esem[ename], 1)

        with nc.Block() as block:
            @block.tensor
            def _(e):
                flush("pe", e)

            @block.scalar
            def _(e):
                flush("act", e)

            @block.vector
            def _(e):
                flush("dve", e)

            @block.gpsimd
            def _(e):
                flush("pool", e)

            @block.sync
            def _(e):
                flush("sp", e)


class Arena:
    def __init__(self, nc, nbytes):
        self.t = nc.alloc_sbuf_tensor("arena", [128, nbytes // 4], F32)
        self.cap = nbytes
        self.off = 0
        self.peak = 0

    def alloc(self, shape, dtype):
        n = 1
        for s in shape:
            n *= s
        nb = (n * DSZ[dtype] + 31) // 32 * 32
        assert self.off + nb <= self.cap, f"arena overflow: {self.off + nb} > {self.cap}"
        v = self.t[:, self.off // 4:(self.off + nb) // 4]
        if dtype != F32:
            v = v.bitcast(dtype)
        v = v[:, 0:n]
        self.off += nb
        self.peak = max(self.peak, self.off)
        if len(shape) == 2:
            v = v.rearrange("p (a b) -> p a b", b=shape[1])
        elif len(shape) == 3:
            v = v.rearrange("p (a b c) -> p a b c", b=shape[1], c=shape[2])
        elif len(shape) == 4:
            v = v.rearrange("p (a b c d) -> p a b c d", b=shape[1], c=shape[2], d=shape[3])
        return v

    def mark(self):
        return self.off

    def release(self, m):
        self.off = m


def _tiles_ffn(with_ctx):
    blocks = []
    for b in range(SEQ // 512):
        blocks.append((NCT + 4 * b, 4, False))
    if with_ctx:
        blocks.append((0, NCT, True))
    return blocks


def build_program(n_layers=DEPTH, debug=False):
    nc = bass.Bass("TRN2", target_bir_lowering=False)

    def din(name, shape, dt=F32):
        return nc.dram_tensor(name, list(shape), dt, kind="ExternalInput").ap()

    x_in = din("x", [SEQ, D])
    ctx_in = din("ctx", [CTX, D])
    cvec = din("cvec", [128, 8, 2])
    ada_w = din("ada_w", [DEPTH, D, NMOD * D])
    ada_b = din("ada_b", [DEPTH, 1, NMOD * D])
    norm_gF = din("norm_gF", [128, DEPTH, 3, 8])
    ffn_w_in = din("ffn_w_in", [DEPTH, 2, D, 2 * FF])
    ffn_w_out = din("ffn_w_out", [DEPTH, 2, FF, D])
    attn_w_qkv = din("attn_w_qkv", [2, D, 3 * D])
    attn_w_o = din("attn_w_o", [2, D, D])
    attn_g = din("attn_g", [128, 2, 4, 64])
    attn_lam = din("attn_lam", [128, 2, 4, 64])
    attn_sublnF = din("attn_sublnF", [128, 2])
    rope_cos = din("rope_cos", [128, SEQ // 128, 64])
    rope_sin = din("rope_sin", [128, SEQ // 128, 64])
    conv_w_in = din("conv_w_in", [D, 2 * D])
    conv_b_inF = din("conv_b_inF", [128, 16])
    conv_dwF = din("conv_dwF", [128, CK, 8])
    conv_vecF = din("conv_vecF", [128, 3, 8])
    conv_w_out = din("conv_w_out", [D, D])
    conv_b_out = din("conv_b_out", [1, D])
    sc_w_in = din("sc_w_in", [D, 3 * D])
    sc_dwF = din("sc_dwF", [128, SK, 8])
    sc_w_out = din("sc_w_out", [D, D])

    out = nc.dram_tensor("out", [SEQ, D], F32, kind="ExternalOutput").ap()
    skind = "ExternalOutput" if debug else "Internal"
    xs = nc.dram_tensor("xs", [NT_ALL * 128, D], F32, kind=skind).ap()
    QT_d = nc.dram_tensor("QT_d", [NT_ALL, 128, 1024], BF16, kind="Internal").ap()
    KT_d = nc.dram_tensor("KT_d", [NT_ALL, 128, 1024], BF16, kind="Internal").ap()
    V_d = nc.dram_tensor("V_d", [NT_ALL, 128, NH * VW], BF16, kind="Internal").ap()
    bT_d = nc.dram_tensor("bT_d", [NT_ALL, 128, 1024], BF16, kind="Internal").ap()
    gates_d = nc.dram_tensor("gates_d", [DEPTH, 3, 2, 128, D], F32, kind="Internal").ap()

    P = Prog(nc)
    AR = Arena(nc, 207 * 1024)
    ps_all = nc.alloc_psum_tensor("ps_all", [128, 8 * 512], F32)

    def bank(i, n=1):
        return ps_all[:, i * 512:(i + n) * 512]

    def bank_bf(i):
        return ps_all[:, i * 512:(i + 1) * 512].bitcast(BF16)

    pbuf = [Buf() for _ in range(8)]

    ident = AR.alloc([128], BF16)
    identf = AR.alloc([128], F32)
    modF = AR.alloc([72, 2], F32)
    ATab = AR.alloc([3, 2, 8], F32)
    gF = AR.alloc([DEPTH, 3, 8], F32)
    scT = AR.alloc([8, 2], BF16)
    scB = [AR.alloc([8, 128], BF16) for _ in range(2)]
    ones_row = AR.alloc([128], BF16)
    small = AR.alloc([64], F32)
    b_ident, b_modF, b_ATab, b_gF, b_sc, b_ones, b_small = (Buf() for _ in range(7))

    P.memset("pool", identf, 0.0, W=[b_ident])
    P._mk("pool", lambda e: e.affine_select(out=identf, in_=identf, pattern=[[-1, 128]], compare_op=ALU.not_equal,
                                             fill=1.0, base=0, channel_multiplier=1), R=[b_ident], W=[b_ident])
    P.cp("dve", ident, identf, R=[b_ident], W=[b_ident])
    P.memset("dve", ones_row, 1.0, W=[b_ones])
    m0 = AR.mark()
    cv = AR.alloc([8, 2], F32)
    b_cv = Buf()
    P.dma("sp", cv, cvec, ("misc", 0), W=[b_cv])
    P.dma("sp", gF, norm_gF, ("misc", 1), W=[b_gF])
    cvs = AR.alloc([8, 2], F32)
    P.act(cvs, cv, AF.Silu, R=[b_cv], W=[b_cv])
    P.cp("dve", scT, cvs, R=[b_cv], W=[b_sc])
    for m in range(2):
        P.cp("dve", scB[m], cvs[:, :, m:m + 1].to_broadcast([128, 8, 128]), R=[b_cv], W=[b_sc])
    P.barrier()
    AR.release(m0)
    PERSIST = AR.mark()

    def src_tile_ap(layer, first, n, first_ffn):
        if first_ffn:
            if first < NCT:
                return ctx_in[first * 128:(first + n) * 128, :].rearrange("(t p) d -> p t d", p=128)
            f = first - NCT
            return x_in[f * 128:(f + n) * 128, :].rearrange("(t p) d -> p t d", p=128)
        return xs[first * 128:(first + n) * 128, :].rearrange("(t p) d -> p t d", p=128)

    def dst_tile_ap(first, n, final):
        if final:
            f = first - NCT
            return out[f * 128:(f + n) * 128, :].rearrange("(t p) d -> p t d", p=128)
        return xs[first * 128:(first + n) * 128, :].rearrange("(t p) d -> p t d", p=128)

    store_ops = []

    def phase_mods(l):
        m0 = AR.mark()
        NCH = 18
        adw = [AR.alloc([8, 512], BF16) for _ in range(3)]
        b_adw = [Buf() for _ in range(3)]
        brow = AR.alloc([NMOD * D], BF16)
        b_brow = Buf()
        ones2 = AR.alloc([2], BF16)
        b_o2 = Buf()
        gsb = [AR.alloc([512], F32) for _ in range(2)]
        b_gsb = [Buf() for _ in range(2)]
        P.memset("dve", ones2, 1.0, W=[b_o2])
        P.dma("pool", brow[0:1, :], ada_b[l], ("w", 0), W=[b_brow])
        psF = bank(0)[:, 0:144].rearrange("p (f m) -> p f m", m=2)
        gi = 0
        for c in range(NCH):
            s_ = c % 3
            src = ada_w[l, :, c * 512:(c + 1) * 512].rearrange("(k p) n -> p k n", p=128)
            P.dma("pool", adw[s_], src, ("ada", s_), W=[b_adw[s_]])
            for fi in range(4):
                f = 4 * c + fi
                for kc in range(8):
                    P.mm(psF[:, f, :], adw[s_][:, kc, fi * 128:(fi + 1) * 128], scT[:, kc, :], kc == 0, False,
                         R=[b_adw[s_], b_sc], W=[pbuf[0]])
                P.mm(psF[:, f, :], brow[0:1, f * 128:(f + 1) * 128], ones2[0:1, :], False, True,
                     R=[b_brow, b_o2], W=[pbuf[0]])
            n = c // 2
            if n % 3 == 2:
                s = n // 3
                half = c % 2
                for m in range(2):
                    pb = 1 + m
                    for kc in range(8):
                        P.mm(bank(pb), scB[m][:, kc, :], adw[s_][:, kc, :], kc == 0, False,
                             R=[b_adw[s_], b_sc], W=[pbuf[pb]])
                    P.mm(bank(pb), ones_row[0:1, :], brow[0:1, c * 512:(c + 1) * 512], False, True,
                         R=[b_brow, b_ones], W=[pbuf[pb]])
                    g_ = gi % 2
                    gi += 1
                    P.act(gsb[g_], bank(pb), AF.Copy, scale=(1.0 if s == 1 else 0.5), R=[pbuf[pb]], W=[b_gsb[g_]])
                    P.dma("sp", gates_d[l, s, m, :, half * 512:(half + 1) * 512], gsb[g_], ("gst", g_), R=[b_gsb[g_]])
        P.cp("dve", modF, psF, R=[pbuf[0]], W=[b_modF])
        tmp = AR.alloc([8], F32)
        b_tmp = Buf()
        for s in range(3):
            for m in range(2):
                P.ts("dve", tmp, modF[:, (3 * s + 1) * 8:(3 * s + 2) * 8, m], 1.0, None, ALU.add, R=[b_modF], W=[b_tmp])
                P.tt("dve", ATab[:, s, m, :], tmp, gF[:, l, s, :], ALU.mult, R=[b_tmp, b_gF], W=[b_ATab])
        P.barrier()
        AR.release(m0)

    class NormCtx:
        def __init__(self, ptr_bank):
            self.xn = [AR.alloc([D], F32) for _ in range(2)]
            self.b_xn = [Buf() for _ in range(2)]
            self.sq = AR.alloc([D], F32)
            self.b_sq = Buf()
            self.y = [AR.alloc([D], BF16) for _ in range(2)]
            self.b_y = [Buf() for _ in range(2)]
            self.st = AR.alloc([2, 4], F32)
            self.b_st = [Buf() for _ in range(2)]
            self.ptr_bank = ptr_bank
            self.cnt = 0

        def load_norm(self, tile_ap):
            i = self.cnt % 2
            self.cnt += 1
            P.dma("sp", self.xn[i], tile_ap, ("xn", i), W=[self.b_xn[i]])
            P.act(self.sq, self.xn[i], AF.Square, R=[self.b_xn[i]], W=[self.b_sq])
            st = self.st[:, i, :]
            P.red("dve", st[:, 0:1], self.sq, R=[self.b_sq], W=[self.b_st[i]])
            P.ts("dve", st[:, 1:2], st[:, 0:1], 1.0 / D, EPS, ALU.mult, ALU.add, R=[self.b_st[i]], W=[self.b_st[i]])
            P.rsqrt(st[:, 2:3], st[:, 1:2], R=[self.b_st[i]], W=[self.b_st[i]])
            P.act(self.y[i], self.xn[i], AF.Identity, scale=st[:, 2:3], R=[self.b_xn[i], self.b_st[i]], W=[self.b_y[i]])
            return i

    def transpose_mod(nctx, yslots, hT, b_hT, s, m, l):
        n = len(yslots)
        pb = nctx.ptr_bank
        ptv = bank_bf(pb)
        for kc in range(8):
            half = kc % 2
            for tt, ys in enumerate(yslots):
                P.tr(ptv[:, half * 512 + tt * 128: half * 512 + (tt + 1) * 128], nctx.y[ys][:, kc * 128:(kc + 1) * 128], ident,
                     R=[nctx.b_y[ys], b_ident], W=[pbuf[pb]])
            P.act(hT[:, kc, 0:n * 128], ptv[:, half * 512: half * 512 + n * 128], AF.Identity,
                  bias=modF[:, 3 * s * 8 + kc, m:m + 1], scale=ATab[:, s, m, kc:kc + 1],
                  R=[pbuf[pb], b_modF, b_ATab], W=[b_hT])


    def residual_out(nctx_x, o_bank, pb, first_tile, tt, dc, gate, b_gate, tmpb, b_tmpb, xr, b_xr, final, idx):
        i = idx % 2
        P.tt("dve", tmpb[i], o_bank, gate[:, dc * 512:(dc + 1) * 512], ALU.mult, R=[pbuf[pb], b_gate], W=[b_tmpb[i]])
        P.tt("dve", xr[:, dc * 512:(dc + 1) * 512], xr[:, dc * 512:(dc + 1) * 512], tmpb[i], ALU.add,
             R=[b_tmpb[i], b_xr], W=[b_xr])

    class ResCtx:
        def __init__(self, l, s, o_banks):
            self.l, self.s = l, s
            self.gate = AR.alloc([D], F32)
            self.b_gate = Buf()
            self.gate_m = None
            self.xr = [AR.alloc([D], F32) for _ in range(2)]
            self.b_xr = [Buf() for _ in range(2)]
            self.tmp = [AR.alloc([512], F32) for _ in range(2)]
            self.b_tmp = [Buf() for _ in range(2)]
            self.o_banks = o_banks
            self.ocnt = 0
            self.xcnt = 0

        def set_gate(self, m):
            if self.gate_m != m:
                P.dma("sp", self.gate, gates_d[self.l, self.s, m], ("gate", 0), W=[self.b_gate])
                self.gate_m = m

        def begin_tile(self, tile, first_ffn):
            i = self.xcnt % 2
            self.xcnt += 1
            P.dma("sp", self.xr[i], src_tile_ap(self.l, tile, 1, first_ffn)[:, 0, :], ("xr", i), W=[self.b_xr[i]])
            return i

        def next_obank(self):
            pb = self.o_banks[self.ocnt % len(self.o_banks)]
            self.ocnt += 1
            return pb

        def update(self, i, pb, dc):
            j = self.ocnt % 2
            P.tt("dve", self.tmp[j], bank(pb), self.gate[:, dc * 512:(dc + 1) * 512], ALU.mult,
                 R=[pbuf[pb], self.b_gate], W=[self.b_tmp[j]])
            P.tt("dve", self.xr[i][:, dc * 512:(dc + 1) * 512], self.xr[i][:, dc * 512:(dc + 1) * 512], self.tmp[j], ALU.add,
                 R=[self.b_tmp[j], self.b_xr[i]], W=[self.b_xr[i]])

        def end_tile(self, i, tile, final):
            final = final and tile >= NCT
            op = P.dma("sp", dst_tile_ap(tile, 1, final)[:, 0, :], self.xr[i], ("xst", i), R=[self.b_xr[i]])
            if final:
                store_ops.append(op)

    def phase_ffn(l, s, with_ctx, first_ffn, final):
        wi = 0 if s == 0 else 1
        m0 = AR.mark()
        w_in = AR.alloc([8, 2 * FF], BF16)
        w_out = AR.alloc([22, D], BF16)
        NG = 11
        b_win = [Buf() for _ in range(NG)]
        b_wout = [Buf() for _ in range(2)]
        wsrc = ffn_w_in[l, wi].rearrange("(k p) n -> p k n", p=128)
        for g in range(NG):
            P.dma("pool", w_in[:, :, g * 256:(g + 1) * 256], wsrc[:, :, g * 256:(g + 1) * 256], ("w", 2 * g), W=[b_win[g]])
            P.dma("pool", w_in[:, :, FF + g * 256:FF + (g + 1) * 256], wsrc[:, :, FF + g * 256:FF + (g + 1) * 256],
                  ("w", 2 * g + 1), W=[b_win[g]])
        osrc = ffn_w_out[l, wi].rearrange("(j p) d -> p j d", p=128)
        for h in range(2):
            P.dma("pool", w_out[:, h * 11:(h + 1) * 11, :], osrc[:, h * 11:(h + 1) * 11, :], ("w", 22 + h), W=[b_wout[h]])
        nctx = NormCtx(ptr_bank=6)
        rc = ResCtx(l, s, o_banks=[4, 5])
        hT = AR.alloc([8, 512], BF16)
        b_hT = Buf()
        uT = AR.alloc([22, 512], BF16)
        b_uT = Buf()
        sg = [AR.alloc([512], F32) for _ in range(2)]
        b_sg = [Buf() for _ in range(2)]
        blocks = _tiles_ffn(with_ctx)

        def stage_T(blk):
            first, n, is_ctx = blk
            ys = []
            for tt in range(n):
                ys.append(nctx.load_norm(src_tile_ap(l, first + tt, 1, first_ffn)[:, 0, :]))
                if len(ys) == 2 or tt == n - 1:
                    pass
            return ys

        def stage_T_full(blk):
            first, n, is_ctx = blk
            m = 1 if is_ctx else 0
            pb = nctx.ptr_bank
            ptv = bank_bf(pb)
            for tt in range(n):
                ys = nctx.load_norm(src_tile_ap(l, first + tt, 1, first_ffn)[:, 0, :])
                for kc in range(8):
                    P.tr(ptv[:, kc * 128:(kc + 1) * 128], nctx.y[ys][:, kc * 128:(kc + 1) * 128], ident,
                         R=[nctx.b_y[ys], b_ident], W=[pbuf[pb]])
                for kc in range(8):
                    P.act(hT[:, kc, tt * 128:(tt + 1) * 128], ptv[:, kc * 128:(kc + 1) * 128], AF.Identity,
                          bias=modF[:, 3 * s * 8 + kc, m:m + 1], scale=ATab[:, s, m, kc:kc + 1],
                          R=[pbuf[pb], b_modF, b_ATab], W=[b_hT])

        def stage_IN(blk):
            first, n, is_ctx = blk
            N = n * 128
            for j in range(22):
                g = j // 2
                pa = (j % 2) * 2
                pg = pa + 1
                for kc in range(8):
                    P.mm(bank(pa)[:, 0:N], w_in[:, kc, j * 128:(j + 1) * 128], hT[:, kc, 0:N], kc == 0, kc == 7,
                         R=[b_win[g], b_hT], W=[pbuf[pa]])
                for kc in range(8):
                    P.mm(bank(pg)[:, 0:N], w_in[:, kc, FF + j * 128:FF + (j + 1) * 128], hT[:, kc, 0:N], kc == 0, kc == 7,
                         R=[b_win[g], b_hT], W=[pbuf[pg]])
                i = j % 2
                P.act(sg[i][:, 0:N], bank(pg)[:, 0:N], AF.Silu, R=[pbuf[pg]], W=[b_sg[i]])
                P.tt("dve", uT[:, j, 0:N], bank(pa)[:, 0:N], sg[i][:, 0:N], ALU.mult, R=[pbuf[pa], b_sg[i]], W=[b_uT])

        def stage_OUT(blk):
            first, n, is_ctx = blk
            rc.set_gate(1 if is_ctx else 0)
            for tt in range(n):
                i = rc.begin_tile(first + tt, first_ffn)
                for dc in range(2):
                    pb = rc.next_obank()
                    for j in range(22):
                        P.mm(bank(pb), uT[:, j, tt * 128:(tt + 1) * 128], w_out[:, j, dc * 512:(dc + 1) * 512], j == 0, j == 21,
                             R=[b_uT, b_wout[j // 11]], W=[pbuf[pb]])
                    rc.update(i, pb, dc)
                rc.end_tile(i, first + tt, final)

        stage_T_full(blocks[0])
        for bi, blk in enumerate(blocks):
            stage_IN(blk)
            if bi + 1 < len(blocks):
                stage_T_full(blocks[bi + 1])
            stage_OUT(blk)
        P.barrier()
        AR.release(m0)

    def phase_attn_qkv(l, j_attn):
        m0 = AR.mark()
        wq = AR.alloc([8, 3 * D], BF16)
        b_wq = [Buf() for _ in range(6)]
        wsrc = attn_w_qkv[j_attn].rearrange("(k p) n -> p k n", p=128)
        for n in range(6):
            P.dma("pool", wq[:, :, n * 512:(n + 1) * 512], wsrc[:, :, n * 512:(n + 1) * 512], ("w", n), W=[b_wq[n]])
        cosT = AR.alloc([SEQ // 128, 64], F32)
        sinT = AR.alloc([SEQ // 128, 64], F32)
        gq = AR.alloc([4, 64], F32)
        b_tab = Buf()
        P.dma("sp", cosT, rope_cos, ("misc", 0), W=[b_tab])
        P.dma("sp", sinT, rope_sin, ("misc", 1), W=[b_tab])
        P.dma("sp", gq, attn_g[:, j_attn], ("misc", 2), W=[b_tab])
        nctx = NormCtx(ptr_bank=6)
        hT = [AR.alloc([8, 128], BF16) for _ in range(2)]
        b_hT = [Buf() for _ in range(2)]
        sqb = AR.alloc([2 * D], F32)
        b_sqb = Buf()
        qks = AR.alloc([2 * D], F32)
        b_qks = Buf()
        stg = AR.alloc([64], F32)
        b_stg = Buf()
        csk = AR.alloc([2, 2, 64], F32)
        b_csk = Buf()
        t1 = AR.alloc([2 * D], F32)
        b_t1 = Buf()
        t2 = AR.alloc([2 * D], F32)
        b_t2 = Buf()
        qkh = [AR.alloc([2 * D], BF16) for _ in range(2)]
        b_qkh = [Buf() for _ in range(2)]
        qkT = [AR.alloc([2, NH, 128], BF16) for _ in range(2)]
        b_qkT = [Buf() for _ in range(2)]
        vaug = [AR.alloc([NH, VW], BF16) for _ in range(2)]
        b_vaug = [Buf() for _ in range(2)]
        for i in range(2):
            P.memset("dve", vaug[i], 1.0, W=[b_vaug[i]])
        pending_qkt = [None]

        def qkt(jt, qi):
            ptq = bank_bf(7)
            for w in range(2):
                for h in range(NH):
                    P.tr(ptq[:, h * 128:(h + 1) * 128], qkh[qi][:, w * D + h * 128: w * D + (h + 1) * 128], ident,
                         R=[b_qkh[qi], b_ident], W=[pbuf[7]])
                P.cp("dve", qkT[qi][:, w, :, :], ptq.rearrange("p (h t) -> p h t", t=128), R=[pbuf[7]], W=[b_qkT[qi]])
                dst = (QT_d if w == 0 else KT_d)[jt].rearrange("p (h t) -> p h t", t=128)
                P.dma("sp", dst, qkT[qi][:, w, :, :], ("qkst", qi * 2 + w), R=[b_qkT[qi]])

        def prep(jt):
            m = 1 if jt < NCT else 0
            ys = nctx.load_norm(src_tile_ap(l, jt, 1, False)[:, 0, :])
            hs = jt % 2
            pb = nctx.ptr_bank
            ptv = bank_bf(pb)
            for kc in range(8):
                P.tr(ptv[:, kc * 128:(kc + 1) * 128], nctx.y[ys][:, kc * 128:(kc + 1) * 128], ident,
                     R=[nctx.b_y[ys], b_ident], W=[pbuf[pb]])
            for kc in range(8):
                P.act(hT[hs][:, kc, :], ptv[:, kc * 128:(kc + 1) * 128], AF.Identity,
                      bias=modF[:, 3 * 8 + kc, m:m + 1], scale=ATab[:, 1, m, kc:kc + 1],
                      R=[pbuf[pb], b_modF, b_ATab], W=[b_hT[hs]])

        prep(0)
        for jt in range(NT_ALL):
            is_ctx = jt < NCT
            m = 1 if is_ctx else 0
            if jt + 1 < NT_ALL:
                prep(jt + 1)
            hs = jt % 2
            for n in range(6):
                for kc in range(8):
                    P.mm(bank(n), hT[hs][:, kc, :], wq[:, kc, n * 512:(n + 1) * 512], kc == 0, kc == 7,
                         R=[b_hT[hs], b_wq[n]], W=[pbuf[n]])
            P.act(qks, bank(0, 4), AF.Copy, R=[pbuf[0], pbuf[1], pbuf[2], pbuf[3]], W=[b_qks])
            qi = jt % 2
            P.act(vaug[qi][:, :, 0:128], bank(4, 2).rearrange("p (h d) -> p h d", d=128), AF.Copy,
                  R=[pbuf[4], pbuf[5]], W=[b_vaug[qi]])
            P.dma("sp", V_d[jt].rearrange("p (h d) -> p h d", d=VW), vaug[qi], ("vst", qi), R=[b_vaug[qi]])
            if pending_qkt[0] is not None:
                qkt(*pending_qkt[0])
            qk_ps = qks
            qb = [b_qks]
            P.act(sqb, qk_ps, AF.Square, R=qb, W=[b_sqb])
            P.red("dve", stg[:, 0:32], sqb.rearrange("p (g d) -> p g d", d=64), R=[b_sqb], W=[b_stg])
            P.ts("dve", stg[:, 0:32], stg[:, 0:32], 1.0 / HD, EPS, ALU.mult, ALU.add, R=[b_stg], W=[b_stg])
            P.rsqrt(stg[:, 32:64], stg[:, 0:32], R=[b_stg], W=[b_stg])
            rs_b = stg[:, 32:64].unsqueeze(2).to_broadcast([128, 32, 64])
            qi = jt % 2
            if is_ctx:
                for w in range(2):
                    P.tt("dve", t1[:, w * D:(w + 1) * D].rearrange("p (g d) -> p g d", d=64),
                         qk_ps[:, w * D:(w + 1) * D].rearrange("p (g d) -> p g d", d=64),
                         gq[:, w:w + 1, :].to_broadcast([128, 16, 64]), ALU.mult, R=qb + [b_tab], W=[b_t1])
            else:
                jl = jt - NCT
                for w in range(2):
                    P.tt("dve", csk[:, w, 0, :], cosT[:, jl, :], gq[:, w, :], ALU.mult, R=[b_tab], W=[b_csk])
                    P.tt("dve", csk[:, w, 1, :], sinT[:, jl, :], gq[:, 2 + w, :], ALU.mult, R=[b_tab], W=[b_csk])
                for w in range(2):
                    xv = qk_ps[:, w * D:(w + 1) * D]
                    P.tt("dve", t1[:, w * D:(w + 1) * D].rearrange("p (g d) -> p g d", d=64),
                         xv.rearrange("p (g d) -> p g d", d=64),
                         csk[:, w, 0:1, :].to_broadcast([128, 16, 64]), ALU.mult, R=qb + [b_csk], W=[b_t1])
                    x5 = xv.rearrange("p (g a h d) -> p g a h d", a=2, h=2, d=16)
                    o5 = t2[:, w * D:(w + 1) * D].rearrange("p (g a h d) -> p g a h d", a=2, h=2, d=16)
                    s4 = csk[:, w, 1, :].rearrange("p (a h d) -> p a h d", a=2, h=2)
                    for hh in range(2):
                        P.tt("dve", o5[:, :, :, hh, :], x5[:, :, :, 1 - hh, :],
                             s4[:, :, hh, :].unsqueeze(1).to_broadcast([128, 16, 2, 16]), ALU.mult,
                             R=qb + [b_csk], W=[b_t2])
                P.tt("dve", t1, t1, t2, ALU.add, R=[b_t1, b_t2], W=[b_t1])
            P.tt("dve", qkh[qi].rearrange("p (g d) -> p g d", d=64), t1.rearrange("p (g d) -> p g d", d=64), rs_b, ALU.mult,
                 R=[b_t1, b_stg], W=[b_qkh[qi]])
            pending_qkt[0] = (jt, qi)
        qkt(*pending_qkt[0])
        P.barrier()
        AR.release(m0)

    def phase_attn_core(l, j_attn, ctx_out, lam_init):
        m0 = AR.mark()
        KT = AR.alloc([NT_ALL, NH, 128], BF16)
        VA = AR.alloc([NT_ALL, NH, VW], BF16)
        b_KT = [Buf() for _ in range(NT_ALL)]
        b_VA = [Buf() for _ in range(NT_ALL)]
        wo = AR.alloc([NH, D], BF16)
        b_wo = Buf()
        grp = [(0, 2)] + [(2 + 4 * i, 4) for i in range(8)]
        for gi_, (f, n) in enumerate(grp):
            o1 = P.dma("sp", KT[:, f:f + n], KT_d[f:f + n].rearrange("t p (h k) -> p t h k", k=128), ("kv", 2 * gi_),
                       W=[b_KT[t] for t in range(f, f + n)])
            o2 = P.dma("sp", VA[:, f:f + n], V_d[f:f + n].rearrange("t p (h k) -> p t h k", k=VW), ("kv", 2 * gi_ + 1),
                       W=[b_VA[t] for t in range(f, f + n)])
        P.dma("pool", wo, attn_w_o[j_attn].rearrange("(h p) d -> p h d", p=128), ("w", 0), W=[b_wo])
        lamb = AR.alloc([4, 64], F32)
        b_lam = Buf()
        P.dma("sp", lamb, attn_lam[:, j_attn], ("misc", 0), W=[b_lam])
        lw = AR.alloc([2, 64], F32)
        b_lw = Buf()
        lv = AR.alloc([8], F32)
        b_lv = Buf()
        for i in range(2):
            P.tt("dve", lw[:, i, :], lamb[:, 2 * i, :], lamb[:, 2 * i + 1, :], ALU.mult, R=[b_lam], W=[b_lw])
        P.red("dve", lv[:, 0:2], lw, R=[b_lw], W=[b_lv])
        P.act(lv[:, 2:4], lv[:, 0:2], AF.Exp, R=[b_lv], W=[b_lv])
        P.tt("dve", lv[:, 4:5], lv[:, 2:3], lv[:, 3:4], ALU.subtract, R=[b_lv], W=[b_lv])
        P.ts("dve", lv[:, 5:6], lv[:, 4:5], lam_init, -1.0, ALU.add, ALU.mult, R=[b_lv], W=[b_lv])
        sub = AR.alloc([2], F32)
        b_sub = Buf()
        P.dma("sp", sub, attn_sublnF, ("misc", 1), W=[b_sub])
        P.ts("dve", lv[:, 6:7], sub[:, j_attn:j_attn + 1], 1.0 - lam_init, None, ALU.mult, R=[b_sub, b_lv], W=[b_lv])

        QT = [AR.alloc([4, NH, 128], BF16) for _ in range(1)]
        b_QT = [Buf() for _ in range(1)]
        NPT = 4
        PT = [AR.alloc([512], BF16) for _ in range(NPT)]
        b_PT = [Buf() for _ in range(NPT)]
        ow = AR.alloc([8, 128], F32)
        b_ow = [Buf() for _ in range(8)]
        rr = AR.alloc([8, 4], F32)
        b_rr = [Buf() for _ in range(8)]
        on = [AR.alloc([128], BF16) for _ in range(4)]
        b_on = [Buf() for _ in range(4)]
        oT = AR.alloc([NH, 512], BF16)
        b_oT = Buf()
        rc = ResCtx(l, 1, o_banks=[7])
        acc_ps = bank(3, 3)
        accv = []
        b_acc = []
        for tt in range(4):
            row = []
            for c in range(2):
                idx = tt * 2 + c
                bnk = idx // 3
                off = bnk * 512 + (idx % 3) * 132
                row.append(acc_ps[:, off:off + 129])
            accv.append(row)
            b_acc.append([Buf(), Buf()])
        pto = bank_bf(6)
        scnt = [0]
        pcnt = [0]
        owc = [0]

        qblocks = []
        if ctx_out:
            qblocks.append((0, NCT, list(range(NCT)), 1))
        for b in range(SEQ // 512):
            qblocks.append((NCT + 4 * b, 4, list(range(NT_ALL)), 0))

        for qb_i, (first, n, kcs, m) in enumerate(qblocks):
            N = n * 128
            qs = 0
            P.dma("sp", QT[qs][:, 0:n], QT_d[first:first + n].rearrange("t p (h k) -> p t h k", k=128), ("qt", qs),
                  W=[b_QT[qs]])
            rc.set_gate(m)
            deferred = [None]

            def finish_pe(h_):
                for tt in range(n):
                    P.tr(pto[:, tt * 128:(tt + 1) * 128], on[tt], ident, R=[b_on[tt], b_ident], W=[pbuf[6]])
                P.act(oT[:, h_, 0:N], pto[:, 0:N], AF.Identity, scale=lv[:, 6:7], R=[pbuf[6], b_lv], W=[b_oT])

            its = [(h, ki, c) for h in range(NH) for ki in range(len(kcs)) for c in range(2)]
            info = {}
            seen = [set()]

            def S_stage(i):
                h, ki, c = its[i]
                kc = kcs[ki]
                sb = scnt[0] % 3
                scnt[0] += 1
                P.mm(bank(sb)[:, 0:N].rearrange("p (t k) -> p t k", k=128), KT[64 * c:64 * (c + 1), kc, h, :],
                     QT[qs][64 * c:64 * (c + 1), 0:n, h, :], True, True,
                     R=[b_KT[kc], b_QT[qs]], W=[pbuf[sb]])
                pi = pcnt[0] % NPT
                pcnt[0] += 1
                P.act(PT[pi][:, 0:N], bank(sb)[:, 0:N], AF.Exp, scale=HD ** -0.5, R=[pbuf[sb]], W=[b_PT[pi]])
                info[i] = pi

            def AV_stage(i):
                h, ki, c = its[i]
                kc = kcs[ki]
                pi = info.pop(i)
                if c == 0 and ki == min(2, len(kcs) - 1) and deferred[0] is not None:
                    finish_pe(deferred[0])
                    deferred[0] = None
                for tt in range(n):
                    idx = tt * 2 + c
                    first_in_bank = False
                    if ki == 0:
                        if c == 0 and tt == 0:
                            seen[0] = set()
                        if idx // 3 not in seen[0]:
                            seen[0].add(idx // 3)
                            first_in_bank = True
                    P.mm(accv[tt][c], PT[pi][:, tt * 128:(tt + 1) * 128], VA[:, kc, h, 0:129],
                         first_in_bank, ki == len(kcs) - 1, R=[b_PT[pi], b_VA[kc]], W=[b_acc[tt][c]], skip=True)
                if not (ki == len(kcs) - 1 and c == 1):
                    return
                o1s = []
                for tt in range(n):
                    ri = (h * 4 + tt) % 8
                    r = rr[:, ri, :]
                    wi_ = owc[0] % 8
                    owc[0] += 1
                    o1 = ow[:, wi_, :]
                    o1s.append((o1, wi_, r, ri))
                    for c in range(2):
                        P.recip(r[:, c:c + 1], accv[tt][c][:, 128:129], R=[b_acc[tt][c]], W=[b_rr[ri]])
                    P.ts("dve", o1, accv[tt][0][:, 0:128], r[:, 0:1], None, ALU.mult, R=[b_acc[tt][0], b_rr[ri]], W=[b_ow[wi_]])
                    P.tt("dve", r[:, 1:2], r[:, 1:2], lv[:, 5:6], ALU.mult, R=[b_rr[ri], b_lv], W=[b_rr[ri]])
                    P.stt("dve", o1, accv[tt][1][:, 0:128], r[:, 1:2], o1, ALU.mult, ALU.add,
                          R=[b_acc[tt][1], b_rr[ri], b_ow[wi_]], W=[b_ow[wi_]])
                for tt in range(n):
                    o1, wi_, r, ri = o1s[tt]
                    wj = owc[0] % 8
                    owc[0] += 1
                    P.act(ow[:, wj, :], o1, AF.Square, R=[b_ow[wi_]], W=[b_ow[wj]])
                    P.red("dve", r[:, 2:3], ow[:, wj, :], R=[b_ow[wj]], W=[b_rr[ri]])
                    P.ts("dve", r[:, 2:3], r[:, 2:3], 1.0 / 128, EPS, ALU.mult, ALU.add, R=[b_rr[ri]], W=[b_rr[ri]])
                    P.rsqrt(r[:, 3:4], r[:, 2:3], R=[b_rr[ri]], W=[b_rr[ri]], lnexp=True)
                    P.ts("dve", on[tt], o1, r[:, 3:4], None, ALU.mult, R=[b_ow[wi_], b_rr[ri]], W=[b_on[tt]])
                deferred[0] = h

            S_stage(0)
            P.mm(bank(7)[:, 0:128], ident, ident, True, True, R=[b_ident], W=[pbuf[7]])
            for i in range(len(its)):
                if i + 1 < len(its):
                    S_stage(i + 1)
                AV_stage(i)
            finish_pe(deferred[0])
            for tt in range(n):
                i = rc.begin_tile(first + tt, False)
                for dc in range(2):
                    pb = rc.next_obank()
                    for h in range(NH):
                        P.mm(bank(pb), oT[:, h, tt * 128:(tt + 1) * 128], wo[:, h, dc * 512:(dc + 1) * 512], h == 0, h == NH - 1,
                             R=[b_oT, b_wo], W=[pbuf[pb]])
                    rc.update(i, pb, dc)
                rc.end_tile(i, first + tt, False)
        P.barrier()
        AR.release(m0)

    def dwconv_segments(with_ctx):
        return ([(0, NCT, 1)] if with_ctx else []) + [(NCT, SEQ // 128, 0)]

    def phase_conf(l, ctx_out):
        PAD = CK // 2
        m0 = AR.mark()
        uL = AR.alloc([8, SEQ + 2 * PAD], BF16)
        uC = AR.alloc([8, CTX + 2 * PAD], BF16)
        b_u = Buf()
        for c in range(8):
            P.memset("dve", uL[:, c, 0:PAD], 0.0, W=[b_u])
            P.memset("dve", uL[:, c, PAD + SEQ:], 0.0, W=[b_u])
            P.memset("dve", uC[:, c, 0:PAD], 0.0, W=[b_u])
            P.memset("dve", uC[:, c, PAD + CTX:], 0.0, W=[b_u])
        blocks = _tiles_ffn(ctx_out)
        m1 = AR.mark()
        w_in = AR.alloc([8, 2 * D], BF16)
        b_win = [Buf() for _ in range(4)]
        wsrc = conv_w_in.rearrange("(k p) n -> p k n", p=128)
        for g in range(4):
            P.dma("pool", w_in[:, :, g * 256:(g + 1) * 256], wsrc[:, :, g * 256:(g + 1) * 256], ("w", 2 * g), W=[b_win[g]])
            P.dma("pool", w_in[:, :, D + g * 256:D + (g + 1) * 256], wsrc[:, :, D + g * 256:D + (g + 1) * 256], ("w", 2 * g + 1),
                  W=[b_win[g]])
        binF = AR.alloc([16], F32)
        b_bin = Buf()
        P.dma("sp", binF, conv_b_inF, ("misc", 0), W=[b_bin])
        nctx = NormCtx(ptr_bank=6)
        hT = AR.alloc([8, 512], BF16)
        b_hT = Buf()
        sg = [AR.alloc([512], F32) for _ in range(2)]
        b_sg = [Buf() for _ in range(2)]
        for (first, n, is_ctx) in blocks:
            m = 1 if is_ctx else 0
            N = n * 128
            pb = nctx.ptr_bank
            ptv = bank_bf(pb)
            for tt in range(n):
                ys = nctx.load_norm(src_tile_ap(l, first + tt, 1, False)[:, 0, :])
                for kc in range(8):
                    P.tr(ptv[:, kc * 128:(kc + 1) * 128], nctx.y[ys][:, kc * 128:(kc + 1) * 128], ident,
                         R=[nctx.b_y[ys], b_ident], W=[pbuf[pb]])
                for kc in range(8):
                    P.act(hT[:, kc, tt * 128:(tt + 1) * 128], ptv[:, kc * 128:(kc + 1) * 128], AF.Identity,
                          bias=modF[:, 3 * 8 + kc, m:m + 1], scale=ATab[:, 1, m, kc:kc + 1],
                          R=[pbuf[pb], b_modF, b_ATab], W=[b_hT])
            ubuf = uC if is_ctx else uL
            t0 = PAD + (first * 128 if is_ctx else (first - NCT) * 128)
            for c in range(8):
                g = c // 2
                pa = (c % 2) * 2
                pg = pa + 1
                for kc in range(8):
                    P.mm(bank(pa)[:, 0:N], w_in[:, kc, c * 128:(c + 1) * 128], hT[:, kc, 0:N], kc == 0, kc == 7,
                         R=[b_win[g], b_hT], W=[pbuf[pa]])
                for kc in range(8):
                    P.mm(bank(pg)[:, 0:N], w_in[:, kc, D + c * 128:D + (c + 1) * 128], hT[:, kc, 0:N], kc == 0, kc == 7,
                         R=[b_win[g], b_hT], W=[pbuf[pg]])
                i = c % 2
                P.act(sg[i][:, 0:N], bank(pg)[:, 0:N], AF.Sigmoid, bias=binF[:, 8 + c:9 + c], R=[pbuf[pg], b_bin], W=[b_sg[i]])
                P.stt("dve", ubuf[:, c, t0:t0 + N], bank(pa)[:, 0:N], binF[:, c:c + 1], sg[i][:, 0:N], ALU.add, ALU.mult,
                      R=[pbuf[pa], b_sg[i], b_bin], W=[b_u])
        P.barrier()
        AR.release(m1)
        NB = 256
        dg = AR.alloc([8, CK, 128], BF16)
        b_dg = Buf()
        dwF = AR.alloc([CK, 8], F32)
        vecF = AR.alloc([3, 8], F32)
        b_dw = Buf()
        P.dma("sp", dwF, conv_dwF, ("misc", 0), W=[b_dw])
        P.dma("sp", vecF, conv_vecF, ("misc", 1), W=[b_dw])
        for c in range(8):
            for k in range(CK):
                P.ts("dve", dg[:, c, k, :], identf, dwF[:, k, c:c + 1], None, ALU.mult, R=[b_ident, b_dw], W=[b_dg])
        w_out = AR.alloc([8, D], BF16)
        b_wout = Buf()
        P.dma("pool", w_out, conv_w_out.rearrange("(k p) n -> p k n", p=128), ("w", 0), W=[b_wout])
        bo = AR.alloc([D], BF16)
        b_bo = Buf()
        P.dma("pool", bo[0:1, :], conv_b_out, ("w", 1), W=[b_bo])
        om = AR.alloc([128], BF16)
        b_om = Buf()
        P.memset("dve", om, 1.0 / D, W=[b_om])
        vT = AR.alloc([8, NB], F32)
        b_vT = Buf()
        vb = AR.alloc([8, NB], BF16)
        b_vb = Buf()
        v2 = AR.alloc([8, NB], BF16)
        b_v2 = Buf()
        mr = AR.alloc([3, NB], F32)
        b_mr = Buf()
        zt = [AR.alloc([NB], F32) for _ in range(2)]
        b_zt = [Buf() for _ in range(2)]
        sT = AR.alloc([8, NB], BF16)
        b_sT = Buf()
        rc = ResCtx(l, 1, o_banks=[6, 7])
        segs = ([(0, CTX, uC, 1)] if ctx_out else []) + [(NCT, SEQ, uL, 0)]
        ccnt = [0]
        for (ft, ntok, ubuf, m) in segs:
            rc.set_gate(m)
            for b0 in range(0, ntok, NB):
                for c in range(8):
                    pb = ccnt[0] % 2
                    ccnt[0] += 1
                    for k in range(CK):
                        P.mm(bank(pb)[:, 0:NB], dg[:, c, k, :], ubuf[:, c, b0 + k:b0 + k + NB], k == 0, k == CK - 1,
                             R=[b_dg, b_u], W=[pbuf[pb]])
                    P.act(vT[:, c, :], bank(pb)[:, 0:NB], AF.Identity, bias=vecF[:, 0, c:c + 1], R=[pbuf[pb], b_dw], W=[b_vT])
                    P.cp("dve", vb[:, c, :], vT[:, c, :], R=[b_vT], W=[b_vb])
                    P.tt("dve", v2[:, c, :], vT[:, c, :], vT[:, c, :], ALU.mult, R=[b_vT], W=[b_v2])
                for c in range(8):
                    P.mm(bank(2)[:, 0:NB], om, vb[:, c, :], c == 0, c == 7, R=[b_om, b_vb], W=[pbuf[2]])
                for c in range(8):
                    P.mm(bank(3)[:, 0:NB], om, v2[:, c, :], c == 0, c == 7, R=[b_om, b_v2], W=[pbuf[3]])
                P.cp("dve", mr[:, 0, :], bank(2)[:, 0:NB], R=[pbuf[2]], W=[b_mr])
                P.tt("dve", mr[:, 2, :], mr[:, 0, :], mr[:, 0, :], ALU.mult, R=[b_mr], W=[b_mr])
                P.tt("dve", mr[:, 1, :], bank(3)[:, 0:NB], mr[:, 2, :], ALU.subtract, R=[pbuf[3], b_mr], W=[b_mr])
                P.ts("dve", mr[:, 1, :], mr[:, 1, :], EPS, None, ALU.add, R=[b_mr], W=[b_mr])
                P.rsqrt(mr[:, 1, :], mr[:, 1, :], R=[b_mr], W=[b_mr])
                for c in range(8):
                    zi = c % 2
                    P.tt("dve", zt[zi], vT[:, c, :], mr[:, 0, :], ALU.subtract, R=[b_vT, b_mr], W=[b_zt[zi]])
                    P.tt("dve", zt[zi], zt[zi], mr[:, 1, :], ALU.mult, R=[b_zt[zi], b_mr], W=[b_zt[zi]])
                    P.act(sT[:, c, :], zt[zi], AF.Silu, bias=vecF[:, 2, c:c + 1], scale=vecF[:, 1, c:c + 1],
                          R=[b_zt[zi], b_dw], W=[b_sT])
                for tt in range(NB // 128):
                    tile = ft + b0 // 128 + tt
                    i = rc.begin_tile(tile, False)
                    for dc in range(2):
                        pb = rc.next_obank()
                        for c in range(8):
                            P.mm(bank(pb), sT[:, c, tt * 128:(tt + 1) * 128], w_out[:, c, dc * 512:(dc + 1) * 512], c == 0, False,
                                 R=[b_sT, b_wout], W=[pbuf[pb]])
                        P.mm(bank(pb), ones_row[0:1, :], bo[0:1, dc * 512:(dc + 1) * 512], False, True,
                             R=[b_ones, b_bo], W=[pbuf[pb]])
                        rc.update(i, pb, dc)
                    rc.end_tile(i, tile, False)
        P.barrier()
        AR.release(m0)

    def phase_sconv(l, ctx_out):
        PAD = SK // 2
        m0 = AR.mark()
        pL = AR.alloc([8, SEQ + 2 * PAD], BF16)
        pC = AR.alloc([8, CTX + 2 * PAD], BF16)
        b_p = Buf()
        for c in range(8):
            P.memset("dve", pL[:, c, 0:PAD], 0.0, W=[b_p])
            P.memset("dve", pL[:, c, PAD + SEQ:], 0.0, W=[b_p])
            P.memset("dve", pC[:, c, 0:PAD], 0.0, W=[b_p])
            P.memset("dve", pC[:, c, PAD + CTX:], 0.0, W=[b_p])
        blocks = _tiles_ffn(ctx_out)
        m1 = AR.mark()
        w_in = AR.alloc([8, 3 * D], BF16)
        b_win = [Buf() for _ in range(4)]
        wsrc = sc_w_in.rearrange("(k p) n -> p k n", p=128)
        for g in range(4):
            for part in range(3):
                P.dma("pool", w_in[:, :, part * D + g * 256:part * D + (g + 1) * 256],
                      wsrc[:, :, part * D + g * 256:part * D + (g + 1) * 256], ("w", 3 * g + part), W=[b_win[g]])
        nctx = NormCtx(ptr_bank=6)
        hT = AR.alloc([8, 512], BF16)
        b_hT = Buf()
        xh = [AR.alloc([512], F32) for _ in range(2)]
        b_xh = [Buf() for _ in range(2)]
        bT = [AR.alloc([8, 128], BF16) for _ in range(8)]
        b_bT = [Buf() for _ in range(8)]
        btc = [0]
        for (first, n, is_ctx) in blocks:
            m = 1 if is_ctx else 0
            N = n * 128
            pb = nctx.ptr_bank
            ptv = bank_bf(pb)
            for tt in range(n):
                ys = nctx.load_norm(src_tile_ap(l, first + tt, 1, False)[:, 0, :])
                for kc in range(8):
                    P.tr(ptv[:, kc * 128:(kc + 1) * 128], nctx.y[ys][:, kc * 128:(kc + 1) * 128], ident,
                         R=[nctx.b_y[ys], b_ident], W=[pbuf[pb]])
                for kc in range(8):
                    P.act(hT[:, kc, tt * 128:(tt + 1) * 128], ptv[:, kc * 128:(kc + 1) * 128], AF.Identity,
                          bias=modF[:, 3 * 8 + kc, m:m + 1], scale=ATab[:, 1, m, kc:kc + 1],
                          R=[pbuf[pb], b_modF, b_ATab], W=[b_hT])
            pbuf_ = pC if is_ctx else pL
            t0 = PAD + (first * 128 if is_ctx else (first - NCT) * 128)
            slots = []
            for tt in range(n):
                slots.append(btc[0] % 8)
                btc[0] += 1
            for c in range(8):
                g = c // 2
                base = (c % 2) * 3
                for part in range(3):
                    for kc in range(8):
                        P.mm(bank(base + part)[:, 0:N], w_in[:, kc, part * D + c * 128:part * D + (c + 1) * 128], hT[:, kc, 0:N],
                             kc == 0, kc == 7, R=[b_win[g], b_hT], W=[pbuf[base + part]])
                i = c % 2
                P.act(xh[i][:, 0:N], bank(base + 2)[:, 0:N], AF.Copy, R=[pbuf[base + 2]], W=[b_xh[i]])
                P.tt("dve", pbuf_[:, c, t0:t0 + N], bank(base + 1)[:, 0:N], xh[i][:, 0:N], ALU.mult,
                     R=[pbuf[base + 1], b_xh[i]], W=[b_p])
                for tt in range(n):
                    P.act(bT[slots[tt]][:, c, :], bank(base)[:, tt * 128:(tt + 1) * 128], AF.Copy, R=[pbuf[base]],
                          W=[b_bT[slots[tt]]])
            for tt in range(n):
                P.dma("sp", bT_d[first + tt].rearrange("p (c t) -> p c t", t=128), bT[slots[tt]], ("bst", slots[tt]),
                      R=[b_bT[slots[tt]]])
        P.barrier()
        AR.release(m1)
        dg = AR.alloc([8, SK, 128], BF16)
        b_dg = Buf()
        dwF = AR.alloc([SK, 8], F32)
        b_dw = Buf()
        P.dma("sp", dwF, sc_dwF, ("misc", 0), W=[b_dw])
        for c in range(8):
            for k in range(SK):
                P.ts("dve", dg[:, c, k, :], identf, dwF[:, k, c:c + 1], None, ALU.mult, R=[b_ident, b_dw], W=[b_dg])
        w_out = AR.alloc([8, D], BF16)
        b_wout = Buf()
        P.dma("pool", w_out, sc_w_out.rearrange("(k p) n -> p k n", p=128), ("w", 0), W=[b_wout])
        bTl = [AR.alloc([4, 8, 128], BF16) for _ in range(2)]
        b_bTl = [Buf() for _ in range(2)]
        yT = AR.alloc([8, 512], BF16)
        b_yT = Buf()
        rc = ResCtx(l, 1, o_banks=[6, 7])
        ccnt = [0]
        for bi, (first, n, is_ctx) in enumerate(blocks):
            m = 1 if is_ctx else 0
            N = n * 128
            rc.set_gate(m)
            pbuf_ = pC if is_ctx else pL
            b0 = first * 128 if is_ctx else (first - NCT) * 128
            bs = bi % 2
            P.dma("sp", bTl[bs][:, 0:n], bT_d[first:first + n].rearrange("t p (c k) -> p t c k", k=128), ("btl", bs),
                  W=[b_bTl[bs]])
            for c in range(8):
                pb = ccnt[0] % 4
                ccnt[0] += 1
                for k in range(SK):
                    P.mm(bank(pb)[:, 0:N], dg[:, c, k, :], pbuf_[:, c, b0 + k:b0 + k + N], k == 0, k == SK - 1,
                         R=[b_dg, b_p], W=[pbuf[pb]])
                P.tt("dve", yT[:, c, 0:N].rearrange("p (t k) -> p t k", k=128), bank(pb)[:, 0:N].rearrange("p (t k) -> p t k", k=128),
                     bTl[bs][:, 0:n, c, :], ALU.mult, R=[pbuf[pb], b_bTl[bs]], W=[b_yT])
            for tt in range(n):
                i = rc.begin_tile(first + tt, False)
                for dc in range(2):
                    pb = rc.next_obank()
                    for c in range(8):
                        P.mm(bank(pb), yT[:, c, tt * 128:(tt + 1) * 128], w_out[:, c, dc * 512:(dc + 1) * 512], c == 0, c == 7,
                             R=[b_yT, b_wout], W=[pbuf[pb]])
                    rc.update(i, pb, dc)
                rc.end_tile(i, first + tt, False)
        P.barrier()
        AR.release(m0)

    for l in range(n_layers):
        kind = l % 3
        j = l // 3
        last = l == DEPTH - 1
        ctx_in_needed = (not last) or kind == 0
        ctx_out_needed = not last
        phase_mods(l)
        phase_ffn(l, 0, ctx_in_needed, first_ffn=(l == 0), final=False)
        if kind == 0:
            lam_init = 0.8 - 0.6 * math.exp(-0.3 * l)
            phase_attn_qkv(l, j)
            phase_attn_core(l, j, ctx_out_needed, lam_init)
        elif kind == 1:
            phase_conf(l, ctx_out_needed)
        else:
            phase_sconv(l, ctx_out_needed)
        phase_ffn(l, 2, ctx_out_needed, first_ffn=False, final=(l == n_layers - 1))
    P.wait_all("sp", store_ops)
    P.emit()
    return nc, AR.peak, {e: len(q) for e, q in P.q.items()}


def _fm(v, nchunk):
    return np.ascontiguousarray(np.asarray(v, np.float32).reshape(nchunk, 128).T)


def _rope_tables():
    rows = SEQ // 64
    row_ids = np.repeat(np.arange(rows, dtype=np.float32), 64)
    col_ids = np.tile(np.arange(64, dtype=np.float32), rows)
    half = HD // 2
    inv_freq = (np.float32(10000.0) ** (-np.arange(0, half, 2, dtype=np.float32) / np.float32(half))).astype(np.float32)
    ang_r = row_ids[:, None] * inv_freq
    ang_c = col_ids[:, None] * inv_freq
    ang = np.concatenate([ang_r, ang_r, ang_c, ang_c], axis=-1)
    cos = np.cos(ang).astype(np.float32)
    sin = np.sin(ang).astype(np.float32)
    sgn = np.concatenate([-np.ones(16), np.ones(16), -np.ones(16), np.ones(16)]).astype(np.float32)
    sin_s = sin * sgn
    to = lambda t: np.ascontiguousarray(t.reshape(SEQ // 128, 128, 64).transpose(1, 0, 2))
    return to(cos), to(sin_s)


def _swap_idx():
    return np.concatenate([np.arange(16, 32), np.arange(0, 16), np.arange(48, 64), np.arange(32, 48)])


def make_in_maps(inputs):
    f = lambda k: np.asarray(inputs[k], np.float32)
    x, c, ctx, c_ctx = f("x"), f("c"), f("ctx"), f("c_ctx")
    B = x.shape[0]
    cos, sin_s = _rope_tables()
    norm_g = f("norm_g")
    norm_gF = np.ascontiguousarray(norm_g.reshape(DEPTH, 3, 8, 128).transpose(3, 0, 1, 2))
    qg, kg = f("attn_q_g"), f("attn_k_g")
    sw = _swap_idx()
    ag = np.stack([qg, kg, qg[:, sw], kg[:, sw]], axis=1)
    attn_g = np.ascontiguousarray(np.broadcast_to(ag[None], (128, 2, 4, 64)))
    attn_lam = np.ascontiguousarray(np.broadcast_to(f("attn_lambda")[None], (128, 2, 4, 64)))
    attn_sublnF = np.ascontiguousarray(f("attn_subln_g").T)
    conv_b_inF = _fm(f("conv_b_in")[0], 16)
    conv_dwF = np.ascontiguousarray(f("conv_dw_w")[0].reshape(CK, 8, 128).transpose(2, 0, 1))
    conv_vecF = np.ascontiguousarray(np.stack([_fm(f("conv_dw_b")[0], 8), _fm(f("conv_ln_g")[0], 8), _fm(f("conv_ln_b")[0], 8)], axis=1))
    sc_dwF = np.ascontiguousarray(f("sc_dw_w")[0].reshape(SK, 8, 128).transpose(2, 0, 1))
    shared = {
        "ada_w": f("ada_w"), "ada_b": f("ada_b").reshape(DEPTH, 1, NMOD * D), "norm_gF": norm_gF,
        "ffn_w_in": f("ffn_w_in"), "ffn_w_out": f("ffn_w_out"),
        "attn_w_qkv": f("attn_w_qkv"), "attn_w_o": f("attn_w_o"), "attn_g": attn_g, "attn_lam": attn_lam,
        "attn_sublnF": attn_sublnF, "rope_cos": cos, "rope_sin": sin_s,
        "conv_w_in": f("conv_w_in")[0], "conv_b_inF": conv_b_inF, "conv_dwF": conv_dwF, "conv_vecF": conv_vecF,
        "conv_w_out": f("conv_w_out")[0], "conv_b_out": f("conv_b_out").reshape(1, D),
        "sc_w_in": f("sc_w_in")[0], "sc_dwF": sc_dwF, "sc_w_out": f("sc_w_out")[0],
    }
    maps = []
    for b in range(B):
        cv = np.ascontiguousarray(np.stack([c[b].reshape(8, 128).T, c_ctx.reshape(8, 128).T], axis=-1))
        mp = dict(shared)
        mp.update({"x": np.ascontiguousarray(x[b]), "ctx": np.ascontiguousarray(ctx[b]), "cvec": cv})
        maps.append(mp)
    return maps


def run(inputs, n_layers=DEPTH, debug=False, trace=False):
    nc, peak, counts = build_program(n_layers=n_layers, debug=debug)
    maps = make_in_maps(inputs)
    res = run_bass_kernel_spmd(nc, maps, core_ids=list(range(len(maps))), trace=trace)
    out = np.stack([r["out"] for r in res.results], axis=0)
    if debug:
        return out, res
    return out


def kernel(**inputs):
    return run(inputs).astype(np.float32)
```

```python
import math
import os
import numpy as np
import concourse.bass as bass
import concourse.mybir as mybir
from concourse.bass_utils import run_bass_kernel_spmd

F32 = mybir.dt.float32
BF16 = mybir.dt.bfloat16
AF = mybir.ActivationFunctionType
ALU = mybir.AluOpType
AX = mybir.AxisListType

D = 1024
DEPTH = 4
SEQ = 4096
CTX = 256
NH = 8
HD = 64
FF = 2816
NMOD = 9
EPS = 1e-6
CK = 31
SK = 3
NT_ALL = (SEQ + CTX) // 128
NCT = CTX // 128
VW = 130

DSZ = {F32: 4, BF16: 2}
ATTACH_WAITS = False


class Op:
    __slots__ = ("eng", "fn", "deps", "sig", "count", "idx", "key", "order")


class Buf:
    __slots__ = ("w", "r", "pr")

    def __init__(self):
        self.w = {}
        self.r = {}
        self.pr = {}


class Prog:
    ENG = ("pe", "act", "dve", "pool", "sp")
    GAP = 4

    def __init__(self, nc):
        self.nc = nc
        self.q = {e: [] for e in self.ENG}
        self.esem = {e: nc.alloc_semaphore("sem_" + e) for e in ("pe", "act", "dve", "pool")}
        self.ksem = {}
        self.kcnt = {}
        self.klast = {}
        self.pending = {e: {} for e in self.ENG}

    @staticmethod
    def _sid(o):
        return ("k", o.key) if o.key is not None else ("e", o.eng)

    def _mk(self, eng, fn, R=(), W=(), deps=(), key=None):
        op = Op()
        op.eng = eng
        op.fn = fn
        op.sig = False
        op.key = key
        op.idx = len(self.q[eng])
        op.count = None
        d = {}

        def add(o):
            k = self._sid(o)
            if k not in d or o.order > d[k].order:
                d[k] = o

        for o in deps:
            add(o)
        for o in self.pending[eng].values():
            add(o)
        self.pending[eng] = {}
        for b in R:
            for o in b.w.values():
                add(o)
        for b in W:
            for o in b.r.values():
                add(o)
            for o in b.pr.values():
                add(o)
        if key is not None:
            if key not in self.ksem:
                self.ksem[key] = self.nc.alloc_semaphore("k_" + "_".join(str(x) for x in key))
                self.kcnt[key] = 0
            self.kcnt[key] += 16
            op.count = self.kcnt[key]
            op.order = op.count
            self.klast[key] = op
        else:
            op.order = op.idx
        for o in d.values():
            o.sig = True
        op.deps = list(d.values())
        sid = self._sid(op)
        for b in R:
            b.r[sid] = op
        for b in W:
            if b.r and not any(b is rb for rb in R):
                b.pr = b.r
                b.r = {}
                b.w = {}
            elif any(b is rb for rb in R):
                b.pr = {k: v for k, v in b.r.items() if v is not op}
                b.r = {}
                b.w = {}
            b.w[sid] = op
        self.q[eng].append(op)
        return op

    def barrier(self):
        last = {}
        for e in ("pe", "act", "dve", "pool"):
            for o in reversed(self.q[e]):
                if o.key is None and o.fn is not None:
                    last[("e", e)] = o
                    o.sig = True
                    break
        for k, o in self.klast.items():
            last[("k", k)] = o
        self.klast = {}
        for e in self.ENG:
            self.pending[e] = dict(last)

    def mm(self, out, lhsT, rhs, start, stop, R=(), W=(), skip=False):
        if skip:
            return self._mk("pe", lambda e: e.matmul(out, lhsT=lhsT, rhs=rhs, start=start, stop=stop, skip_group_check=True), R, W)
        return self._mk("pe", lambda e: e.matmul(out, lhsT=lhsT, rhs=rhs, start=start, stop=stop), R, W)

    def tr(self, out, in_, ident, R=(), W=()):
        return self._mk("pe", lambda e: e.transpose(out=out, in_=in_, identity=ident), R, W)

    def act(self, out, in_, func, bias=None, scale=None, R=(), W=()):
        kw = {}
        if bias is not None:
            kw["bias"] = bias
        if scale is not None:
            kw["scale"] = scale
        return self._mk("act", lambda e: e.activation(out=out, in_=in_, func=func, **kw), R, W)

    def ts(self, eng, out, in0, s1, s2, op0, op1=None, R=(), W=()):
        if op1 is None:
            return self._mk(eng, lambda e: e.tensor_scalar(out=out, in0=in0, scalar1=s1, scalar2=None, op0=op0), R, W)
        return self._mk(eng, lambda e: e.tensor_scalar(out=out, in0=in0, scalar1=s1, scalar2=s2, op0=op0, op1=op1), R, W)

    def tt(self, eng, out, in0, in1, op, R=(), W=()):
        return self._mk(eng, lambda e: e.tensor_tensor(out=out, in0=in0, in1=in1, op=op), R, W)

    def stt(self, eng, out, in0, scalar, in1, op0, op1, R=(), W=()):
        return self._mk(eng, lambda e: e.scalar_tensor_tensor(out=out, in0=in0, scalar=scalar, in1=in1, op0=op0, op1=op1), R, W)

    def red(self, eng, out, in_, R=(), W=()):
        return self._mk(eng, lambda e: e.reduce_sum(out=out, in_=in_, axis=AX.X), R, W)

    def cp(self, eng, out, in_, R=(), W=()):
        return self._mk(eng, lambda e: e.tensor_copy(out=out, in_=in_), R, W)

    def rsqrt(self, out, in_, R=(), W=(), lnexp=False):
        if lnexp:
            self._mk("act", lambda e: e.activation(out=out, in_=in_, func=AF.Ln), list(R), list(W))
            return self._mk("act", lambda e: e.activation(out=out, in_=out, func=AF.Exp, scale=-0.5), list(W), list(W))
        self._mk("act", lambda e: e.activation(out=out, in_=in_, func=AF.Sqrt), list(R), list(W))
        return self._mk("dve", lambda e: e.reciprocal(out=out, in_=out), list(W), list(W))

    def recip(self, out, in_, R=(), W=()):
        return self._mk("dve", lambda e: e.reciprocal(out=out, in_=in_), R, W)

    def memset(self, eng, ap, val, R=(), W=()):
        return self._mk(eng, lambda e: e.memset(ap, val), R, W)

    def dma(self, q, out, in_, key, R=(), W=()):
        return self._mk(q, lambda e: e.dma_start(out=out, in_=in_), R, W, key=key)

    def wait_all(self, eng, ops):
        return self._mk(eng, None, deps=ops)

    def emit(self):
        nc = self.nc
        for e in ("pe", "act", "dve", "pool"):
            c = 0
            for o in self.q[e]:
                if o.key is None and o.sig:
                    c += 1
                    o.count = c
        handles = {}

        def flush(ename, e):
            waited = {}
            for o in self.q[ename]:
                need = []
                for dpn in o.deps:
                    if dpn.key is not None:
                        sem = self.ksem[dpn.key]
                        sid = ("k", dpn.key)
                    else:
                        if dpn.eng == ename:
                            if ename == "pe":
                                continue
                            if ename != "pool" and o.idx - dpn.idx > self.GAP:
                                continue
                        sem = self.esem[dpn.eng]
                        sid = ("e", dpn.eng)
                    if waited.get(sid, 0) >= dpn.count:
                        continue
                    need.append((sem, dpn.count))
                    waited[sid] = dpn.count
                if o.fn is None:
                    for sem, cnt in need:
                        e.wait_ge(sem, cnt)
                    continue
                attach = ATTACH_WAITS and o.key is None and ename in ("pe", "act", "dve")
                for sem, cnt in (need[:-1] if attach else need):
                    e.wait_ge(sem, cnt)
                ins = o.fn(e)
                if need and attach:
                    ins._wait_ge(need[-1][0], need[-1][1])
                if o.key is not None:
                    ins.then_inc(self.ksem[o.key], 16)
                elif o.sig:
                    ins.then_inc(self.esem[ename], 1)

        with nc.Block() as block:
            @block.tensor
            def _(e):
                flush("pe", e)

            @block.scalar
            def _(e):
                flush("act", e)

            @block.vector
            def _(e):
                flush("dve", e)

            @block.gpsimd
            def _(e):
                flush("pool", e)

            @block.sync
            def _(e):
                flush("sp", e)


class Arena:
    def __init__(self, nc, nbytes):
        self.t = nc.alloc_sbuf_tensor("arena", [128, nbytes // 4], F32)
        self.cap = nbytes
        self.off = 0
        self.peak = 0

    def alloc(self, shape, dtype):
        n = 1
        for s in shape:
            n *= s
        nb = (n * DSZ[dtype] + 31) // 32 * 32
        assert self.off + nb <= self.cap, f"arena overflow: {self.off + nb} > {self.cap}"
        v = self.t[:, self.off // 4:(self.off + nb) // 4]
        if dtype != F32:
            v = v.bitcast(dtype)
        v = v[:, 0:n]
        self.off += nb
        self.peak = max(self.peak, self.off)
        if len(shape) == 2:
            v = v.rearrange("p (a b) -> p a b", b=shape[1])
        elif len(shape) == 3:
            v = v.rearrange("p (a b c) -> p a b c", b=shape[1], c=shape[2])
        elif len(shape) == 4:
            v = v.rearrange("p (a b c d) -> p a b c d", b=shape[1], c=shape[2], d=shape[3])
        return v

    def mark(self):
        return self.off

    def release(self, m):
        self.off = m


def _tiles_ffn(with_ctx):
    blocks = []
    for b in range(SEQ // 512):
        blocks.append((NCT + 4 * b, 4, False))
    if with_ctx:
        blocks.append((0, NCT, True))
    return blocks


def build_program(n_layers=DEPTH, debug=False):
    nc = bass.Bass("TRN2", target_bir_lowering=False)

    def din(name, shape, dt=F32):
        return nc.dram_tensor(name, list(shape), dt, kind="ExternalInput").ap()

    x_in = din("x", [SEQ, D])
    ctx_in = din("ctx", [CTX, D])
    cvec = din("cvec", [128, 8, 2])
    ada_w = din("ada_w", [DEPTH, D, NMOD * D])
    ada_b = din("ada_b", [DEPTH, 1, NMOD * D])
    norm_gF = din("norm_gF", [128, DEPTH, 3, 8])
    ffn_w_in = din("ffn_w_in", [DEPTH, 2, D, 2 * FF])
    ffn_w_out = din("ffn_w_out", [DEPTH, 2, FF, D])
    attn_w_qkv = din("attn_w_qkv", [2, D, 3 * D])
    attn_w_o = din("attn_w_o", [2, D, D])
    attn_g = din("attn_g", [128, 2, 4, 64])
    attn_lam = din("attn_lam", [128, 2, 4, 64])
    attn_sublnF = din("attn_sublnF", [128, 2])
    rope_cos = din("rope_cos", [128, SEQ // 128, 64])
    rope_sin = din("rope_sin", [128, SEQ // 128, 64])
    conv_w_in = din("conv_w_in", [D, 2 * D])
    conv_b_inF = din("conv_b_inF", [128, 16])
    conv_dwF = din("conv_dwF", [128, CK, 8])
    conv_vecF = din("conv_vecF", [128, 3, 8])
    conv_w_out = din("conv_w_out", [D, D])
    conv_b_out = din("conv_b_out", [1, D])
    sc_w_in = din("sc_w_in", [D, 3 * D])
    sc_dwF = din("sc_dwF", [128, SK, 8])
    sc_w_out = din("sc_w_out", [D, D])

    out = nc.dram_tensor("out", [SEQ, D], F32, kind="ExternalOutput").ap()
    skind = "ExternalOutput" if debug else "Internal"
    xs = nc.dram_tensor("xs", [NT_ALL * 128, D], F32, kind=skind).ap()
    QT_d = nc.dram_tensor("QT_d", [NT_ALL, 128, 1024], BF16, kind="Internal").ap()
    KT_d = nc.dram_tensor("KT_d", [NT_ALL, 128, 1024], BF16, kind="Internal").ap()
    V_d = nc.dram_tensor("V_d", [NT_ALL, 128, NH * VW], BF16, kind="Internal").ap()
    bT_d = nc.dram_tensor("bT_d", [NT_ALL, 128, 1024], BF16, kind="Internal").ap()
    gates_d = nc.dram_tensor("gates_d", [DEPTH, 3, 2, 128, D], F32, kind="Internal").ap()

    P = Prog(nc)
    AR = Arena(nc, 207 * 1024)
    ps_all = nc.alloc_psum_tensor("ps_all", [128, 8 * 512], F32)

    def bank(i, n=1):
        return ps_all[:, i * 512:(i + n) * 512]

    def bank_bf(i):
        return ps_all[:, i * 512:(i + 1) * 512].bitcast(BF16)

    pbuf = [Buf() for _ in range(8)]

    ident = AR.alloc([128], BF16)
    identf = AR.alloc([128], F32)
    modF = AR.alloc([72, 2], F32)
    ATab = AR.alloc([3, 2, 8], F32)
    gF = AR.alloc([DEPTH, 3, 8], F32)
    scT = AR.alloc([8, 2], BF16)
    scB = [AR.alloc([8, 128], BF16) for _ in range(2)]
    ones_row = AR.alloc([128], BF16)
    small = AR.alloc([64], F32)
    b_ident, b_modF, b_ATab, b_gF, b_sc, b_ones, b_small = (Buf() for _ in range(7))

    P.memset("pool", identf, 0.0, W=[b_ident])
    P._mk("pool", lambda e: e.affine_select(out=identf, in_=identf, pattern=[[-1, 128]], compare_op=ALU.not_equal,
                                             fill=1.0, base=0, channel_multiplier=1), R=[b_ident], W=[b_ident])
    P.cp("dve", ident, identf, R=[b_ident], W=[b_ident])
    P.memset("dve", ones_row, 1.0, W=[b_ones])
    m0 = AR.mark()
    cv = AR.alloc([8, 2], F32)
    b_cv = Buf()
    P.dma("sp", cv, cvec, ("misc", 0), W=[b_cv])
    P.dma("sp", gF, norm_gF, ("misc", 1), W=[b_gF])
    cvs = AR.alloc([8, 2], F32)
    P.act(cvs, cv, AF.Silu, R=[b_cv], W=[b_cv])
    P.cp("dve", scT, cvs, R=[b_cv], W=[b_sc])
    for m in range(2):
        P.cp("dve", scB[m], cvs[:, :, m:m + 1].to_broadcast([128, 8, 128]), R=[b_cv], W=[b_sc])
    P.barrier()
    AR.release(m0)
    PERSIST = AR.mark()

    def src_tile_ap(layer, first, n, first_ffn):
        if first_ffn:
            if first < NCT:
                return ctx_in[first * 128:(first + n) * 128, :].rearrange("(t p) d -> p t d", p=128)
            f = first - NCT
            return x_in[f * 128:(f + n) * 128, :].rearrange("(t p) d -> p t d", p=128)
        return xs[first * 128:(first + n) * 128, :].rearrange("(t p) d -> p t d", p=128)

    def dst_tile_ap(first, n, final):
        if final:
            f = first - NCT
            return out[f * 128:(f + n) * 128, :].rearrange("(t p) d -> p t d", p=128)
        return xs[first * 128:(first + n) * 128, :].rearrange("(t p) d -> p t d", p=128)

    store_ops = []

    def phase_mods(l):
        m0 = AR.mark()
        NCH = 18
        adw = [AR.alloc([8, 512], BF16) for _ in range(3)]
        b_adw = [Buf() for _ in range(3)]
        brow = AR.alloc([NMOD * D], BF16)
        b_brow = Buf()
        ones2 = AR.alloc([2], BF16)
        b_o2 = Buf()
        gsb = [AR.alloc([512], F32) for _ in range(2)]
        b_gsb = [Buf() for _ in range(2)]
        P.memset("dve", ones2, 1.0, W=[b_o2])
        P.dma("pool", brow[0:1, :], ada_b[l], ("w", 0), W=[b_brow])
        psF = bank(0)[:, 0:144].rearrange("p (f m) -> p f m", m=2)
        gi = 0
        for c in range(NCH):
            s_ = c % 3
            src = ada_w[l, :, c * 512:(c + 1) * 512].rearrange("(k p) n -> p k n", p=128)
            P.dma("pool", adw[s_], src, ("ada", s_), W=[b_adw[s_]])
            for fi in range(4):
                f = 4 * c + fi
                for kc in range(8):
                    P.mm(psF[:, f, :], adw[s_][:, kc, fi * 128:(fi + 1) * 128], scT[:, kc, :], kc == 0, False,
                         R=[b_adw[s_], b_sc], W=[pbuf[0]])
                P.mm(psF[:, f, :], brow[0:1, f * 128:(f + 1) * 128], ones2[0:1, :], False, True,
                     R=[b_brow, b_o2], W=[pbuf[0]])
            n = c // 2
            if n % 3 == 2:
                s = n // 3
                half = c % 2
                for m in range(2):
                    pb = 1 + m
                    for kc in range(8):
                        P.mm(bank(pb), scB[m][:, kc, :], adw[s_][:, kc, :], kc == 0, False,
                             R=[b_adw[s_], b_sc], W=[pbuf[pb]])
                    P.mm(bank(pb), ones_row[0:1, :], brow[0:1, c * 512:(c + 1) * 512], False, True,
                         R=[b_brow, b_ones], W=[pbuf[pb]])
                    g_ = gi % 2
                    gi += 1
                    P.act(gsb[g_], bank(pb), AF.Copy, scale=(1.0 if s == 1 else 0.5), R=[pbuf[pb]], W=[b_gsb[g_]])
                    P.dma("sp", gates_d[l, s, m, :, half * 512:(half + 1) * 512], gsb[g_], ("gst", g_), R=[b_gsb[g_]])
        P.cp("dve", modF, psF, R=[pbuf[0]], W=[b_modF])
        tmp = AR.alloc([8], F32)
        b_tmp = Buf()
        for s in range(3):
            for m in range(2):
                P.ts("dve", tmp, modF[:, (3 * s + 1) * 8:(3 * s + 2) * 8, m], 1.0, None, ALU.add, R=[b_modF], W=[b_tmp])
                P.tt("dve", ATab[:, s, m, :], tmp, gF[:, l, s, :], ALU.mult, R=[b_tmp, b_gF], W=[b_ATab])
        P.barrier()
        AR.release(m0)

    class NormCtx:
        def __init__(self, ptr_bank):
            self.xn = [AR.alloc([D], F32) for _ in range(2)]
            self.b_xn = [Buf() for _ in range(2)]
            self.sq = AR.alloc([D], F32)
            self.b_sq = Buf()
            self.y = [AR.alloc([D], BF16) for _ in range(2)]
            self.b_y = [Buf() for _ in range(2)]
            self.st = AR.alloc([2, 4], F32)
            self.b_st = [Buf() for _ in range(2)]
            self.ptr_bank = ptr_bank
            self.cnt = 0

        def load_norm(self, tile_ap):
            i = self.cnt % 2
            self.cnt += 1
            P.dma("sp", self.xn[i], tile_ap, ("xn", i), W=[self.b_xn[i]])
            P.act(self.sq, self.xn[i], AF.Square, R=[self.b_xn[i]], W=[self.b_sq])
            st = self.st[:, i, :]
            P.red("dve", st[:, 0:1], self.sq, R=[self.b_sq], W=[self.b_st[i]])
            P.ts("dve", st[:, 1:2], st[:, 0:1], 1.0 / D, EPS, ALU.mult, ALU.add, R=[self.b_st[i]], W=[self.b_st[i]])
            P.rsqrt(st[:, 2:3], st[:, 1:2], R=[self.b_st[i]], W=[self.b_st[i]])
            P.act(self.y[i], self.xn[i], AF.Identity, scale=st[:, 2:3], R=[self.b_xn[i], self.b_st[i]], W=[self.b_y[i]])
            return i

    def transpose_mod(nctx, yslots, hT, b_hT, s, m, l):
        n = len(yslots)
        pb = nctx.ptr_bank
        ptv = bank_bf(pb)
        for kc in range(8):
            half = kc % 2
            for tt, ys in enumerate(yslots):
                P.tr(ptv[:, half * 512 + tt * 128: half * 512 + (tt + 1) * 128], nctx.y[ys][:, kc * 128:(kc + 1) * 128], ident,
                     R=[nctx.b_y[ys], b_ident], W=[pbuf[pb]])
            P.act(hT[:, kc, 0:n * 128], ptv[:, half * 512: half * 512 + n * 128], AF.Identity,
                  bias=modF[:, 3 * s * 8 + kc, m:m + 1], scale=ATab[:, s, m, kc:kc + 1],
                  R=[pbuf[pb], b_modF, b_ATab], W=[b_hT])


    def residual_out(nctx_x, o_bank, pb, first_tile, tt, dc, gate, b_gate, tmpb, b_tmpb, xr, b_xr, final, idx):
        i = idx % 2
        P.tt("dve", tmpb[i], o_bank, gate[:, dc * 512:(dc + 1) * 512], ALU.mult, R=[pbuf[pb], b_gate], W=[b_tmpb[i]])
        P.tt("dve", xr[:, dc * 512:(dc + 1) * 512], xr[:, dc * 512:(dc + 1) * 512], tmpb[i], ALU.add,
             R=[b_tmpb[i], b_xr], W=[b_xr])

    class ResCtx:
        def __init__(self, l, s, o_banks):
            self.l, self.s = l, s
            self.gate = AR.alloc([D], F32)
            self.b_gate = Buf()
            self.gate_m = None
            self.xr = [AR.alloc([D], F32) for _ in range(2)]
            self.b_xr = [Buf() for _ in range(2)]
            self.tmp = [AR.alloc([512], F32) for _ in range(2)]
            self.b_tmp = [Buf() for _ in range(2)]
            self.o_banks = o_banks
            self.ocnt = 0
            self.xcnt = 0

        def set_gate(self, m):
            if self.gate_m != m:
                P.dma("sp", self.gate, gates_d[self.l, self.s, m], ("gate", 0), W=[self.b_gate])
                self.gate_m = m

        def begin_tile(self, tile, first_ffn):
            i = self.xcnt % 2
            self.xcnt += 1
            P.dma("sp", self.xr[i], src_tile_ap(self.l, tile, 1, first_ffn)[:, 0, :], ("xr", i), W=[self.b_xr[i]])
            return i

        def next_obank(self):
            pb = self.o_banks[self.ocnt % len(self.o_banks)]
            self.ocnt += 1
            return pb

        def update(self, i, pb, dc):
            j = self.ocnt % 2
            P.tt("dve", self.tmp[j], bank(pb), self.gate[:, dc * 512:(dc + 1) * 512], ALU.mult,
                 R=[pbuf[pb], self.b_gate], W=[self.b_tmp[j]])
            P.tt("dve", self.xr[i][:, dc * 512:(dc + 1) * 512], self.xr[i][:, dc * 512:(dc + 1) * 512], self.tmp[j], ALU.add,
                 R=[self.b_tmp[j], self.b_xr[i]], W=[self.b_xr[i]])

        def end_tile(self, i, tile, final):
            final = final and tile >= NCT
            op = P.dma("sp", dst_tile_ap(tile, 1, final)[:, 0, :], self.xr[i], ("xst", i), R=[self.b_xr[i]])
            if final:
                store_ops.append(op)

    def phase_ffn(l, s, with_ctx, first_ffn, final):
        wi = 0 if s == 0 else 1
        m0 = AR.mark()
        w_in = AR.alloc([8, 2 * FF], BF16)
        w_out = AR.alloc([22, D], BF16)
        NG = 11
        b_win = [Buf() for _ in range(NG)]
        b_wout = [Buf() for _ in range(2)]
        wsrc = ffn_w_in[l, wi].rearrange("(k p) n -> p k n", p=128)
        for g in range(NG):
            P.dma("pool", w_in[:, :, g * 256:(g + 1) * 256], wsrc[:, :, g * 256:(g + 1) * 256], ("w", 2 * g), W=[b_win[g]])
            P.dma("pool", w_in[:, :, FF + g * 256:FF + (g + 1) * 256], wsrc[:, :, FF + g * 256:FF + (g + 1) * 256],
                  ("w", 2 * g + 1), W=[b_win[g]])
        osrc = ffn_w_out[l, wi].rearrange("(j p) d -> p j d", p=128)
        for h in range(2):
            P.dma("pool", w_out[:, h * 11:(h + 1) * 11, :], osrc[:, h * 11:(h + 1) * 11, :], ("w", 22 + h), W=[b_wout[h]])
        nctx = NormCtx(ptr_bank=6)
        rc = ResCtx(l, s, o_banks=[4, 5])
        hT = AR.alloc([8, 512], BF16)
        b_hT = Buf()
        uT = AR.alloc([22, 512], BF16)
        b_uT = Buf()
        sg = [AR.alloc([512], F32) for _ in range(2)]
        b_sg = [Buf() for _ in range(2)]
        blocks = _tiles_ffn(with_ctx)

        def stage_T(blk):
            first, n, is_ctx = blk
            ys = []
            for tt in range(n):
                ys.append(nctx.load_norm(src_tile_ap(l, first + tt, 1, first_ffn)[:, 0, :]))
                if len(ys) == 2 or tt == n - 1:
                    pass
            return ys

        def t_ln(blk, tt):
            first, n, is_ctx = blk
            return nctx.load_norm(src_tile_ap(l, first + tt, 1, first_ffn)[:, 0, :])

        def t_tr(blk, tt, ys):
            first, n, is_ctx = blk
            m = 1 if is_ctx else 0
            pb = nctx.ptr_bank
            ptv = bank_bf(pb)
            for kc in range(8):
                P.tr(ptv[:, kc * 128:(kc + 1) * 128], nctx.y[ys][:, kc * 128:(kc + 1) * 128], ident,
                     R=[nctx.b_y[ys], b_ident], W=[pbuf[pb]])
            for kc in range(8):
                P.act(hT[:, kc, tt * 128:(tt + 1) * 128], ptv[:, kc * 128:(kc + 1) * 128], AF.Identity,
                      bias=modF[:, 3 * s * 8 + kc, m:m + 1], scale=ATab[:, s, m, kc:kc + 1],
                      R=[pbuf[pb], b_modF, b_ATab], W=[b_hT])

        def stage_T_full(blk):
            for tt in range(blk[1]):
                t_tr(blk, tt, t_ln(blk, tt))

        def t_actions(blk):
            ysl = {}
            acts = []
            for j in range(blk[1]):
                acts.append(lambda j=j: ysl.__setitem__(j, t_ln(blk, j)))
                if j >= 1:
                    acts.append(lambda j=j: t_tr(blk, j - 1, ysl[j - 1]))
            acts.append(lambda: t_tr(blk, blk[1] - 1, ysl[blk[1] - 1]))
            return acts

        def stage_IN(blk):
            first, n, is_ctx = blk
            N = n * 128
            for j in range(22):
                g = j // 2
                pa = (j % 2) * 2
                pg = pa + 1
                for kc in range(8):
                    P.mm(bank(pa)[:, 0:N], w_in[:, kc, j * 128:(j + 1) * 128], hT[:, kc, 0:N], kc == 0, kc == 7,
                         R=[b_win[g], b_hT], W=[pbuf[pa]])
                for kc in range(8):
                    P.mm(bank(pg)[:, 0:N], w_in[:, kc, FF + j * 128:FF + (j + 1) * 128], hT[:, kc, 0:N], kc == 0, kc == 7,
                         R=[b_win[g], b_hT], W=[pbuf[pg]])
                i = j % 2
                P.act(sg[i][:, 0:N], bank(pg)[:, 0:N], AF.Silu, R=[pbuf[pg]], W=[b_sg[i]])
                P.tt("dve", uT[:, j, 0:N], bank(pa)[:, 0:N], sg[i][:, 0:N], ALU.mult, R=[pbuf[pa], b_sg[i]], W=[b_uT])

        def stage_OUT(blk, acts=()):
            acts = list(acts)
            first, n, is_ctx = blk
            rc.set_gate(1 if is_ctx else 0)
            if acts:
                acts.pop(0)()
            for tt in range(n):
                i = rc.begin_tile(first + tt, first_ffn)
                for dc in range(2):
                    pb = rc.next_obank()
                    for j in range(22):
                        P.mm(bank(pb), uT[:, j, tt * 128:(tt + 1) * 128], w_out[:, j, dc * 512:(dc + 1) * 512], j == 0, j == 21,
                             R=[b_uT, b_wout[j // 11]], W=[pbuf[pb]])
                    rc.update(i, pb, dc)
                rc.end_tile(i, first + tt, final)
                for _ in range(2):
                    if acts:
                        acts.pop(0)()
            while acts:
                acts.pop(0)()

        stage_T_full(blocks[0])
        for bi, blk in enumerate(blocks):
            stage_IN(blk)
            stage_OUT(blk, t_actions(blocks[bi + 1]) if bi + 1 < len(blocks) else ())
        P.barrier()
        AR.release(m0)

    def phase_attn_qkv(l, j_attn):
        m0 = AR.mark()
        wq = AR.alloc([8, 3 * D], BF16)
        b_wq = [Buf() for _ in range(6)]
        wsrc = attn_w_qkv[j_attn].rearrange("(k p) n -> p k n", p=128)
        for n in range(6):
            P.dma("pool", wq[:, :, n * 512:(n + 1) * 512], wsrc[:, :, n * 512:(n + 1) * 512], ("w", n), W=[b_wq[n]])
        cosT = AR.alloc([SEQ // 128, 64], F32)
        sinT = AR.alloc([SEQ // 128, 64], F32)
        gq = AR.alloc([4, 64], F32)
        b_tab = Buf()
        P.dma("sp", cosT, rope_cos, ("misc", 0), W=[b_tab])
        P.dma("sp", sinT, rope_sin, ("misc", 1), W=[b_tab])
        P.dma("sp", gq, attn_g[:, j_attn], ("misc", 2), W=[b_tab])
        nctx = NormCtx(ptr_bank=6)
        hT = [AR.alloc([8, 128], BF16) for _ in range(2)]
        b_hT = [Buf() for _ in range(2)]
        sqb = AR.alloc([2 * D], F32)
        b_sqb = Buf()
        qks = AR.alloc([2 * D], F32)
        b_qks = Buf()
        stg = AR.alloc([64], F32)
        b_stg = Buf()
        csk = AR.alloc([2, 2, 64], F32)
        b_csk = Buf()
        t1 = AR.alloc([2 * D], F32)
        b_t1 = Buf()
        t2 = AR.alloc([2 * D], F32)
        b_t2 = Buf()
        qkh = [AR.alloc([2 * D], BF16) for _ in range(2)]
        b_qkh = [Buf() for _ in range(2)]
        qkT = [AR.alloc([2, NH, 128], BF16) for _ in range(2)]
        b_qkT = [Buf() for _ in range(2)]
        vaug = [AR.alloc([NH, VW], BF16) for _ in range(2)]
        b_vaug = [Buf() for _ in range(2)]
        for i in range(2):
            P.memset("dve", vaug[i], 1.0, W=[b_vaug[i]])
        pending_qkt = [None]

        def qkt(jt, qi):
            ptq = bank_bf(7)
            for w in range(2):
                for h in range(NH):
                    P.tr(ptq[:, h * 128:(h + 1) * 128], qkh[qi][:, w * D + h * 128: w * D + (h + 1) * 128], ident,
                         R=[b_qkh[qi], b_ident], W=[pbuf[7]])
                P.cp("dve", qkT[qi][:, w, :, :], ptq.rearrange("p (h t) -> p h t", t=128), R=[pbuf[7]], W=[b_qkT[qi]])
                dst = (QT_d if w == 0 else KT_d)[jt].rearrange("p (h t) -> p h t", t=128)
                P.dma("sp", dst, qkT[qi][:, w, :, :], ("qkst", qi * 2 + w), R=[b_qkT[qi]])

        def prep(jt):
            m = 1 if jt < NCT else 0
            ys = nctx.load_norm(src_tile_ap(l, jt, 1, False)[:, 0, :])
            hs = jt % 2
            pb = nctx.ptr_bank
            ptv = bank_bf(pb)
            for kc in range(8):
                P.tr(ptv[:, kc * 128:(kc + 1) * 128], nctx.y[ys][:, kc * 128:(kc + 1) * 128], ident,
                     R=[nctx.b_y[ys], b_ident], W=[pbuf[pb]])
            for kc in range(8):
                P.act(hT[hs][:, kc, :], ptv[:, kc * 128:(kc + 1) * 128], AF.Identity,
                      bias=modF[:, 3 * 8 + kc, m:m + 1], scale=ATab[:, 1, m, kc:kc + 1],
                      R=[pbuf[pb], b_modF, b_ATab], W=[b_hT[hs]])

        prep(0)
        for jt in range(NT_ALL):
            is_ctx = jt < NCT
            m = 1 if is_ctx else 0
            if jt + 1 < NT_ALL:
                prep(jt + 1)
            hs = jt % 2
            for n in range(6):
                for kc in range(8):
                    P.mm(bank(n), hT[hs][:, kc, :], wq[:, kc, n * 512:(n + 1) * 512], kc == 0, kc == 7,
                         R=[b_hT[hs], b_wq[n]], W=[pbuf[n]])
            P.act(qks, bank(0, 4), AF.Copy, R=[pbuf[0], pbuf[1], pbuf[2], pbuf[3]], W=[b_qks])
            qi = jt % 2
            P.act(vaug[qi][:, :, 0:128], bank(4, 2).rearrange("p (h d) -> p h d", d=128), AF.Copy,
                  R=[pbuf[4], pbuf[5]], W=[b_vaug[qi]])
            P.dma("sp", V_d[jt].rearrange("p (h d) -> p h d", d=VW), vaug[qi], ("vst", qi), R=[b_vaug[qi]])
            if pending_qkt[0] is not None:
                qkt(*pending_qkt[0])
            qk_ps = qks
            qb = [b_qks]
            P.act(sqb, qk_ps, AF.Square, R=qb, W=[b_sqb])
            P.red("dve", stg[:, 0:32], sqb.rearrange("p (g d) -> p g d", d=64), R=[b_sqb], W=[b_stg])
            P.ts("dve", stg[:, 0:32], stg[:, 0:32], 1.0 / HD, EPS, ALU.mult, ALU.add, R=[b_stg], W=[b_stg])
            P.rsqrt(stg[:, 32:64], stg[:, 0:32], R=[b_stg], W=[b_stg])
            rs_b = stg[:, 32:64].unsqueeze(2).to_broadcast([128, 32, 64])
            qi = jt % 2
            if is_ctx:
                for w in range(2):
                    P.tt("dve", t1[:, w * D:(w + 1) * D].rearrange("p (g d) -> p g d", d=64),
                         qk_ps[:, w * D:(w + 1) * D].rearrange("p (g d) -> p g d", d=64),
                         gq[:, w:w + 1, :].to_broadcast([128, 16, 64]), ALU.mult, R=qb + [b_tab], W=[b_t1])
            else:
                jl = jt - NCT
                for w in range(2):
                    P.tt("dve", csk[:, w, 0, :], cosT[:, jl, :], gq[:, w, :], ALU.mult, R=[b_tab], W=[b_csk])
                    P.tt("dve", csk[:, w, 1, :], sinT[:, jl, :], gq[:, 2 + w, :], ALU.mult, R=[b_tab], W=[b_csk])
                for w in range(2):
                    xv = qk_ps[:, w * D:(w + 1) * D]
                    P.tt("dve", t1[:, w * D:(w + 1) * D].rearrange("p (g d) -> p g d", d=64),
                         xv.rearrange("p (g d) -> p g d", d=64),
                         csk[:, w, 0:1, :].to_broadcast([128, 16, 64]), ALU.mult, R=qb + [b_csk], W=[b_t1])
                    x5 = xv.rearrange("p (g a h d) -> p g a h d", a=2, h=2, d=16)
                    o5 = t2[:, w * D:(w + 1) * D].rearrange("p (g a h d) -> p g a h d", a=2, h=2, d=16)
                    s4 = csk[:, w, 1, :].rearrange("p (a h d) -> p a h d", a=2, h=2)
                    for hh in range(2):
                        P.tt("dve", o5[:, :, :, hh, :], x5[:, :, :, 1 - hh, :],
                             s4[:, :, hh, :].unsqueeze(1).to_broadcast([128, 16, 2, 16]), ALU.mult,
                             R=qb + [b_csk], W=[b_t2])
                P.tt("dve", t1, t1, t2, ALU.add, R=[b_t1, b_t2], W=[b_t1])
            P.tt("dve", qkh[qi].rearrange("p (g d) -> p g d", d=64), t1.rearrange("p (g d) -> p g d", d=64), rs_b, ALU.mult,
                 R=[b_t1, b_stg], W=[b_qkh[qi]])
            pending_qkt[0] = (jt, qi)
        qkt(*pending_qkt[0])
        P.barrier()
        AR.release(m0)

    def phase_attn_core(l, j_attn, ctx_out, lam_init):
        m0 = AR.mark()
        KT = AR.alloc([NT_ALL, NH, 128], BF16)
        VA = AR.alloc([NT_ALL, NH, VW], BF16)
        b_KT = [Buf() for _ in range(NT_ALL)]
        b_VA = [Buf() for _ in range(NT_ALL)]
        wo = AR.alloc([NH, D], BF16)
        b_wo = Buf()
        grp = [(0, 2)] + [(2 + 4 * i, 4) for i in range(8)]
        for gi_, (f, n) in enumerate(grp):
            o1 = P.dma("sp", KT[:, f:f + n], KT_d[f:f + n].rearrange("t p (h k) -> p t h k", k=128), ("kv", 2 * gi_),
                       W=[b_KT[t] for t in range(f, f + n)])
            o2 = P.dma("sp", VA[:, f:f + n], V_d[f:f + n].rearrange("t p (h k) -> p t h k", k=VW), ("kv", 2 * gi_ + 1),
                       W=[b_VA[t] for t in range(f, f + n)])
        P.dma("pool", wo, attn_w_o[j_attn].rearrange("(h p) d -> p h d", p=128), ("w", 0), W=[b_wo])
        lamb = AR.alloc([4, 64], F32)
        b_lam = Buf()
        P.dma("sp", lamb, attn_lam[:, j_attn], ("misc", 0), W=[b_lam])
        lw = AR.alloc([2, 64], F32)
        b_lw = Buf()
        lv = AR.alloc([8], F32)
        b_lv = Buf()
        for i in range(2):
            P.tt("dve", lw[:, i, :], lamb[:, 2 * i, :], lamb[:, 2 * i + 1, :], ALU.mult, R=[b_lam], W=[b_lw])
        P.red("dve", lv[:, 0:2], lw, R=[b_lw], W=[b_lv])
        P.act(lv[:, 2:4], lv[:, 0:2], AF.Exp, R=[b_lv], W=[b_lv])
        P.tt("dve", lv[:, 4:5], lv[:, 2:3], lv[:, 3:4], ALU.subtract, R=[b_lv], W=[b_lv])
        P.ts("dve", lv[:, 5:6], lv[:, 4:5], lam_init, -1.0, ALU.add, ALU.mult, R=[b_lv], W=[b_lv])
        sub = AR.alloc([2], F32)
        b_sub = Buf()
        P.dma("sp", sub, attn_sublnF, ("misc", 1), W=[b_sub])
        P.ts("dve", lv[:, 6:7], sub[:, j_attn:j_attn + 1], 1.0 - lam_init, None, ALU.mult, R=[b_sub, b_lv], W=[b_lv])

        QT = [AR.alloc([4, NH, 128], BF16) for _ in range(1)]
        b_QT = [Buf() for _ in range(1)]
        NPT = 4
        PT = [AR.alloc([512], BF16) for _ in range(NPT)]
        b_PT = [Buf() for _ in range(NPT)]
        ow = AR.alloc([8, 128], F32)
        b_ow = [Buf() for _ in range(8)]
        rr = AR.alloc([8, 4], F32)
        b_rr = [Buf() for _ in range(8)]
        on = [AR.alloc([128], BF16) for _ in range(4)]
        b_on = [Buf() for _ in range(4)]
        oT = AR.alloc([NH, 512], BF16)
        b_oT = Buf()
        rc = ResCtx(l, 1, o_banks=[7])
        acc_ps = bank(3, 3)
        accv = []
        b_acc = []
        for tt in range(4):
            row = []
            for c in range(2):
                idx = tt * 2 + c
                bnk = idx // 3
                off = bnk * 512 + (idx % 3) * 132
                row.append(acc_ps[:, off:off + 129])
            accv.append(row)
            b_acc.append([Buf(), Buf()])
        pto = bank_bf(6)
        scnt = [0]
        pcnt = [0]
        owc = [0]

        qblocks = []
        if ctx_out:
            qblocks.append((0, NCT, list(range(NCT)), 1))
        for b in range(SEQ // 512):
            qblocks.append((NCT + 4 * b, 4, list(range(NT_ALL)), 0))

        for qb_i, (first, n, kcs, m) in enumerate(qblocks):
            N = n * 128
            qs = 0
            P.dma("sp", QT[qs][:, 0:n], QT_d[first:first + n].rearrange("t p (h k) -> p t h k", k=128), ("qt", qs),
                  W=[b_QT[qs]])
            rc.set_gate(m)
            deferred = [None]

            def finish_pe(h_):
                for tt in range(n):
                    P.tr(pto[:, tt * 128:(tt + 1) * 128], on[tt], ident, R=[b_on[tt], b_ident], W=[pbuf[6]])
                P.act(oT[:, h_, 0:N], pto[:, 0:N], AF.Identity, scale=lv[:, 6:7], R=[pbuf[6], b_lv], W=[b_oT])

            its = [(h, ki, c) for h in range(NH) for ki in range(len(kcs)) for c in range(2)]
            info = {}
            seen = [set()]

            def S_stage(i):
                h, ki, c = its[i]
                kc = kcs[ki]
                sb = scnt[0] % 3
                scnt[0] += 1
                P.mm(bank(sb)[:, 0:N].rearrange("p (t k) -> p t k", k=128), KT[64 * c:64 * (c + 1), kc, h, :],
                     QT[qs][64 * c:64 * (c + 1), 0:n, h, :], True, True,
                     R=[b_KT[kc], b_QT[qs]], W=[pbuf[sb]])
                pi = pcnt[0] % NPT
                pcnt[0] += 1
                P.act(PT[pi][:, 0:N], bank(sb)[:, 0:N], AF.Exp, scale=HD ** -0.5, R=[pbuf[sb]], W=[b_PT[pi]])
                info[i] = pi

            def AV_stage(i):
                h, ki, c = its[i]
                kc = kcs[ki]
                pi = info.pop(i)
                if c == 0 and ki == min(2, len(kcs) - 1) and deferred[0] is not None:
                    finish_pe(deferred[0])
                    deferred[0] = None
                for tt in range(n):
                    idx = tt * 2 + c
                    first_in_bank = False
                    if ki == 0:
                        if c == 0 and tt == 0:
                            seen[0] = set()
                        if idx // 3 not in seen[0]:
                            seen[0].add(idx // 3)
                            first_in_bank = True
                    P.mm(accv[tt][c], PT[pi][:, tt * 128:(tt + 1) * 128], VA[:, kc, h, 0:129],
                         first_in_bank, ki == len(kcs) - 1, R=[b_PT[pi], b_VA[kc]], W=[b_acc[tt][c]], skip=True)
                if not (ki == len(kcs) - 1 and c == 1):
                    return
                o1s = []
                for tt in range(n):
                    ri = (h * 4 + tt) % 8
                    r = rr[:, ri, :]
                    wi_ = owc[0] % 8
                    owc[0] += 1
                    o1 = ow[:, wi_, :]
                    o1s.append((o1, wi_, r, ri))
                    for c in range(2):
                        P.recip(r[:, c:c + 1], accv[tt][c][:, 128:129], R=[b_acc[tt][c]], W=[b_rr[ri]])
                    P.ts("dve", o1, accv[tt][0][:, 0:128], r[:, 0:1], None, ALU.mult, R=[b_acc[tt][0], b_rr[ri]], W=[b_ow[wi_]])
                    P.tt("dve", r[:, 1:2], r[:, 1:2], lv[:, 5:6], ALU.mult, R=[b_rr[ri], b_lv], W=[b_rr[ri]])
                    P.stt("dve", o1, accv[tt][1][:, 0:128], r[:, 1:2], o1, ALU.mult, ALU.add,
                          R=[b_acc[tt][1], b_rr[ri], b_ow[wi_]], W=[b_ow[wi_]])
                for tt in range(n):
                    o1, wi_, r, ri = o1s[tt]
                    wj = owc[0] % 8
                    owc[0] += 1
                    P.act(ow[:, wj, :], o1, AF.Square, R=[b_ow[wi_]], W=[b_ow[wj]])
                    P.red("dve", r[:, 2:3], ow[:, wj, :], R=[b_ow[wj]], W=[b_rr[ri]])
                    P.ts("dve", r[:, 2:3], r[:, 2:3], 1.0 / 128, EPS, ALU.mult, ALU.add, R=[b_rr[ri]], W=[b_rr[ri]])
                    P.rsqrt(r[:, 3:4], r[:, 2:3], R=[b_rr[ri]], W=[b_rr[ri]], lnexp=True)
                    P.ts("dve", on[tt], o1, r[:, 3:4], None, ALU.mult, R=[b_ow[wi_], b_rr[ri]], W=[b_on[tt]])
                deferred[0] = h

            S_stage(0)
            P.mm(bank(7)[:, 0:128], ident, ident, True, True, R=[b_ident], W=[pbuf[7]])
            for i in range(len(its)):
                if i + 1 < len(its):
                    S_stage(i + 1)
                AV_stage(i)
            finish_pe(deferred[0])
            for tt in range(n):
                i = rc.begin_tile(first + tt, False)
                for dc in range(2):
                    pb = rc.next_obank()
                    for h in range(NH):
                        P.mm(bank(pb), oT[:, h, tt * 128:(tt + 1) * 128], wo[:, h, dc * 512:(dc + 1) * 512], h == 0, h == NH - 1,
                             R=[b_oT, b_wo], W=[pbuf[pb]])
                    rc.update(i, pb, dc)
                rc.end_tile(i, first + tt, False)
        P.barrier()
        AR.release(m0)

    def dwconv_segments(with_ctx):
        return ([(0, NCT, 1)] if with_ctx else []) + [(NCT, SEQ // 128, 0)]

    def phase_conf(l, ctx_out):
        PAD = CK // 2
        m0 = AR.mark()
        uL = AR.alloc([8, SEQ + 2 * PAD], BF16)
        uC = AR.alloc([8, CTX + 2 * PAD], BF16)
        b_u = Buf()
        for c in range(8):
            P.memset("dve", uL[:, c, 0:PAD], 0.0, W=[b_u])
            P.memset("dve", uL[:, c, PAD + SEQ:], 0.0, W=[b_u])
            P.memset("dve", uC[:, c, 0:PAD], 0.0, W=[b_u])
            P.memset("dve", uC[:, c, PAD + CTX:], 0.0, W=[b_u])
        blocks = _tiles_ffn(ctx_out)
        m1 = AR.mark()
        w_in = AR.alloc([8, 2 * D], BF16)
        b_win = [Buf() for _ in range(4)]
        wsrc = conv_w_in.rearrange("(k p) n -> p k n", p=128)
        for g in range(4):
            P.dma("pool", w_in[:, :, g * 256:(g + 1) * 256], wsrc[:, :, g * 256:(g + 1) * 256], ("w", 2 * g), W=[b_win[g]])
            P.dma("pool", w_in[:, :, D + g * 256:D + (g + 1) * 256], wsrc[:, :, D + g * 256:D + (g + 1) * 256], ("w", 2 * g + 1),
                  W=[b_win[g]])
        binF = AR.alloc([16], F32)
        b_bin = Buf()
        P.dma("sp", binF, conv_b_inF, ("misc", 0), W=[b_bin])
        nctx = NormCtx(ptr_bank=6)
        hT = AR.alloc([8, 512], BF16)
        b_hT = Buf()
        sg = [AR.alloc([512], F32) for _ in range(2)]
        b_sg = [Buf() for _ in range(2)]
        for (first, n, is_ctx) in blocks:
            m = 1 if is_ctx else 0
            N = n * 128
            pb = nctx.ptr_bank
            ptv = bank_bf(pb)
            for tt in range(n):
                ys = nctx.load_norm(src_tile_ap(l, first + tt, 1, False)[:, 0, :])
                for kc in range(8):
                    P.tr(ptv[:, kc * 128:(kc + 1) * 128], nctx.y[ys][:, kc * 128:(kc + 1) * 128], ident,
                         R=[nctx.b_y[ys], b_ident], W=[pbuf[pb]])
                for kc in range(8):
                    P.act(hT[:, kc, tt * 128:(tt + 1) * 128], ptv[:, kc * 128:(kc + 1) * 128], AF.Identity,
                          bias=modF[:, 3 * 8 + kc, m:m + 1], scale=ATab[:, 1, m, kc:kc + 1],
                          R=[pbuf[pb], b_modF, b_ATab], W=[b_hT])
            ubuf = uC if is_ctx else uL
            t0 = PAD + (first * 128 if is_ctx else (first - NCT) * 128)
            for c in range(8):
                g = c // 2
                pa = (c % 2) * 2
                pg = pa + 1
                for kc in range(8):
                    P.mm(bank(pa)[:, 0:N], w_in[:, kc, c * 128:(c + 1) * 128], hT[:, kc, 0:N], kc == 0, kc == 7,
                         R=[b_win[g], b_hT], W=[pbuf[pa]])
                for kc in range(8):
                    P.mm(bank(pg)[:, 0:N], w_in[:, kc, D + c * 128:D + (c + 1) * 128], hT[:, kc, 0:N], kc == 0, kc == 7,
                         R=[b_win[g], b_hT], W=[pbuf[pg]])
                i = c % 2
                P.act(sg[i][:, 0:N], bank(pg)[:, 0:N], AF.Sigmoid, bias=binF[:, 8 + c:9 + c], R=[pbuf[pg], b_bin], W=[b_sg[i]])
                P.stt("dve", ubuf[:, c, t0:t0 + N], bank(pa)[:, 0:N], binF[:, c:c + 1], sg[i][:, 0:N], ALU.add, ALU.mult,
                      R=[pbuf[pa], b_sg[i], b_bin], W=[b_u])
        P.barrier()
        AR.release(m1)
        NB = 256
        dg = AR.alloc([8, CK, 128], BF16)
        b_dg = Buf()
        dwF = AR.alloc([CK, 8], F32)
        vecF = AR.alloc([3, 8], F32)
        b_dw = Buf()
        P.dma("sp", dwF, conv_dwF, ("misc", 0), W=[b_dw])
        P.dma("sp", vecF, conv_vecF, ("misc", 1), W=[b_dw])
        for c in range(8):
            for k in range(CK):
                P.ts("dve", dg[:, c, k, :], identf, dwF[:, k, c:c + 1], None, ALU.mult, R=[b_ident, b_dw], W=[b_dg])
        w_out = AR.alloc([8, D], BF16)
        b_wout = Buf()
        P.dma("pool", w_out, conv_w_out.rearrange("(k p) n -> p k n", p=128), ("w", 0), W=[b_wout])
        bo = AR.alloc([D], BF16)
        b_bo = Buf()
        P.dma("pool", bo[0:1, :], conv_b_out, ("w", 1), W=[b_bo])
        om = AR.alloc([128], BF16)
        b_om = Buf()
        P.memset("dve", om, 1.0 / D, W=[b_om])
        vT = AR.alloc([8, NB], F32)
        b_vT = Buf()
        vb = AR.alloc([8, NB], BF16)
        b_vb = Buf()
        v2 = AR.alloc([8, NB], BF16)
        b_v2 = Buf()
        mr = AR.alloc([3, NB], F32)
        b_mr = Buf()
        zt = [AR.alloc([NB], F32) for _ in range(2)]
        b_zt = [Buf() for _ in range(2)]
        sT = AR.alloc([8, NB], BF16)
        b_sT = Buf()
        rc = ResCtx(l, 1, o_banks=[6, 7])
        segs = ([(0, CTX, uC, 1)] if ctx_out else []) + [(NCT, SEQ, uL, 0)]
        ccnt = [0]
        for (ft, ntok, ubuf, m) in segs:
            rc.set_gate(m)
            for b0 in range(0, ntok, NB):
                for c in range(8):
                    pb = ccnt[0] % 2
                    ccnt[0] += 1
                    for k in range(CK):
                        P.mm(bank(pb)[:, 0:NB], dg[:, c, k, :], ubuf[:, c, b0 + k:b0 + k + NB], k == 0, k == CK - 1,
                             R=[b_dg, b_u], W=[pbuf[pb]])
                    P.act(vT[:, c, :], bank(pb)[:, 0:NB], AF.Identity, bias=vecF[:, 0, c:c + 1], R=[pbuf[pb], b_dw], W=[b_vT])
                    P.cp("dve", vb[:, c, :], vT[:, c, :], R=[b_vT], W=[b_vb])
                    P.tt("dve", v2[:, c, :], vT[:, c, :], vT[:, c, :], ALU.mult, R=[b_vT], W=[b_v2])
                for c in range(8):
                    P.mm(bank(2)[:, 0:NB], om, vb[:, c, :], c == 0, c == 7, R=[b_om, b_vb], W=[pbuf[2]])
                for c in range(8):
                    P.mm(bank(3)[:, 0:NB], om, v2[:, c, :], c == 0, c == 7, R=[b_om, b_v2], W=[pbuf[3]])
                P.cp("dve", mr[:, 0, :], bank(2)[:, 0:NB], R=[pbuf[2]], W=[b_mr])
                P.tt("dve", mr[:, 2, :], mr[:, 0, :], mr[:, 0, :], ALU.mult, R=[b_mr], W=[b_mr])
                P.tt("dve", mr[:, 1, :], bank(3)[:, 0:NB], mr[:, 2, :], ALU.subtract, R=[pbuf[3], b_mr], W=[b_mr])
                P.ts("dve", mr[:, 1, :], mr[:, 1, :], EPS, None, ALU.add, R=[b_mr], W=[b_mr])
                P.rsqrt(mr[:, 1, :], mr[:, 1, :], R=[b_mr], W=[b_mr])
                for c in range(8):
                    zi = c % 2
                    P.tt("dve", zt[zi], vT[:, c, :], mr[:, 0, :], ALU.subtract, R=[b_vT, b_mr], W=[b_zt[zi]])
                    P.tt("dve", zt[zi], zt[zi], mr[:, 1, :], ALU.mult, R=[b_zt[zi], b_mr], W=[b_zt[zi]])
                    P.act(sT[:, c, :], zt[zi], AF.Silu, bias=vecF[:, 2, c:c + 1], scale=vecF[:, 1, c:c + 1],
                          R=[b_zt[zi], b_dw], W=[b_sT])
                for tt in range(NB // 128):
                    tile = ft + b0 // 128 + tt
                    i = rc.begin_tile(tile, False)
                    for dc in range(2):
                        pb = rc.next_obank()
                        for c in range(8):
                            P.mm(bank(pb), sT[:, c, tt * 128:(tt + 1) * 128], w_out[:, c, dc * 512:(dc + 1) * 512], c == 0, False,
                                 R=[b_sT, b_wout], W=[pbuf[pb]])
                        P.mm(bank(pb), ones_row[0:1, :], bo[0:1, dc * 512:(dc + 1) * 512], False, True,
                             R=[b_ones, b_bo], W=[pbuf[pb]])
                        rc.update(i, pb, dc)
                    rc.end_tile(i, tile, False)
        P.barrier()
        AR.release(m0)

    def phase_sconv(l, ctx_out):
        PAD = SK // 2
        m0 = AR.mark()
        pL = AR.alloc([8, SEQ + 2 * PAD], BF16)
        pC = AR.alloc([8, CTX + 2 * PAD], BF16)
        b_p = Buf()
        for c in range(8):
            P.memset("dve", pL[:, c, 0:PAD], 0.0, W=[b_p])
            P.memset("dve", pL[:, c, PAD + SEQ:], 0.0, W=[b_p])
            P.memset("dve", pC[:, c, 0:PAD], 0.0, W=[b_p])
            P.memset("dve", pC[:, c, PAD + CTX:], 0.0, W=[b_p])
        blocks = _tiles_ffn(ctx_out)
        m1 = AR.mark()
        w_in = AR.alloc([8, 3 * D], BF16)
        b_win = [Buf() for _ in range(4)]
        wsrc = sc_w_in.rearrange("(k p) n -> p k n", p=128)
        for g in range(4):
            for part in range(3):
                P.dma("pool", w_in[:, :, part * D + g * 256:part * D + (g + 1) * 256],
                      wsrc[:, :, part * D + g * 256:part * D + (g + 1) * 256], ("w", 3 * g + part), W=[b_win[g]])
        nctx = NormCtx(ptr_bank=6)
        hT = AR.alloc([8, 512], BF16)
        b_hT = Buf()
        xh = [AR.alloc([512], F32) for _ in range(2)]
        b_xh = [Buf() for _ in range(2)]
        bT = [AR.alloc([8, 128], BF16) for _ in range(8)]
        b_bT = [Buf() for _ in range(8)]
        btc = [0]
        for (first, n, is_ctx) in blocks:
            m = 1 if is_ctx else 0
            N = n * 128
            pb = nctx.ptr_bank
            ptv = bank_bf(pb)
            for tt in range(n):
                ys = nctx.load_norm(src_tile_ap(l, first + tt, 1, False)[:, 0, :])
                for kc in range(8):
                    P.tr(ptv[:, kc * 128:(kc + 1) * 128], nctx.y[ys][:, kc * 128:(kc + 1) * 128], ident,
                         R=[nctx.b_y[ys], b_ident], W=[pbuf[pb]])
                for kc in range(8):
                    P.act(hT[:, kc, tt * 128:(tt + 1) * 128], ptv[:, kc * 128:(kc + 1) * 128], AF.Identity,
                          bias=modF[:, 3 * 8 + kc, m:m + 1], scale=ATab[:, 1, m, kc:kc + 1],
                          R=[pbuf[pb], b_modF, b_ATab], W=[b_hT])
            pbuf_ = pC if is_ctx else pL
            t0 = PAD + (first * 128 if is_ctx else (first - NCT) * 128)
            slots = []
            for tt in range(n):
                slots.append(btc[0] % 8)
                btc[0] += 1
            for c in range(8):
                g = c // 2
                base = (c % 2) * 3
                for part in range(3):
                    for kc in range(8):
                        P.mm(bank(base + part)[:, 0:N], w_in[:, kc, part * D + c * 128:part * D + (c + 1) * 128], hT[:, kc, 0:N],
                             kc == 0, kc == 7, R=[b_win[g], b_hT], W=[pbuf[base + part]])
                i = c % 2
                P.act(xh[i][:, 0:N], bank(base + 2)[:, 0:N], AF.Copy, R=[pbuf[base + 2]], W=[b_xh[i]])
                P.tt("dve", pbuf_[:, c, t0:t0 + N], bank(base + 1)[:, 0:N], xh[i][:, 0:N], ALU.mult,
                     R=[pbuf[base + 1], b_xh[i]], W=[b_p])
                for tt in range(n):
                    P.act(bT[slots[tt]][:, c, :], bank(base)[:, tt * 128:(tt + 1) * 128], AF.Copy, R=[pbuf[base]],
                          W=[b_bT[slots[tt]]])
            for tt in range(n):
                P.dma("sp", bT_d[first + tt].rearrange("p (c t) -> p c t", t=128), bT[slots[tt]], ("bst", slots[tt]),
                      R=[b_bT[slots[tt]]])
        P.barrier()
        AR.release(m1)
        dg = AR.alloc([8, SK, 128], BF16)
        b_dg = Buf()
        dwF = AR.alloc([SK, 8], F32)
        b_dw = Buf()
        P.dma("sp", dwF, sc_dwF, ("misc", 0), W=[b_dw])
        for c in range(8):
            for k in range(SK):
                P.ts("dve", dg[:, c, k, :], identf, dwF[:, k, c:c + 1], None, ALU.mult, R=[b_ident, b_dw], W=[b_dg])
        w_out = AR.alloc([8, D], BF16)
        b_wout = Buf()
        P.dma("pool", w_out, sc_w_out.rearrange("(k p) n -> p k n", p=128), ("w", 0), W=[b_wout])
        bTl = [AR.alloc([4, 8, 128], BF16) for _ in range(2)]
        b_bTl = [Buf() for _ in range(2)]
        yT = AR.alloc([8, 512], BF16)
        b_yT = Buf()
        rc = ResCtx(l, 1, o_banks=[6, 7])
        ccnt = [0]
        for bi, (first, n, is_ctx) in enumerate(blocks):
            m = 1 if is_ctx else 0
            N = n * 128
            rc.set_gate(m)
            pbuf_ = pC if is_ctx else pL
            b0 = first * 128 if is_ctx else (first - NCT) * 128
            bs = bi % 2
            P.dma("sp", bTl[bs][:, 0:n], bT_d[first:first + n].rearrange("t p (c k) -> p t c k", k=128), ("btl", bs),
                  W=[b_bTl[bs]])
            for c in range(8):
                pb = ccnt[0] % 4
                ccnt[0] += 1
                for k in range(SK):
                    P.mm(bank(pb)[:, 0:N], dg[:, c, k, :], pbuf_[:, c, b0 + k:b0 + k + N], k == 0, k == SK - 1,
                         R=[b_dg, b_p], W=[pbuf[pb]])
                P.tt("dve", yT[:, c, 0:N].rearrange("p (t k) -> p t k", k=128), bank(pb)[:, 0:N].rearrange("p (t k) -> p t k", k=128),
                     bTl[bs][:, 0:n, c, :], ALU.mult, R=[pbuf[pb], b_bTl[bs]], W=[b_yT])
            for tt in range(n):
                i = rc.begin_tile(first + tt, False)
                for dc in range(2):
                    pb = rc.next_obank()
                    for c in range(8):
                        P.mm(bank(pb), yT[:, c, tt * 128:(tt + 1) * 128], w_out[:, c, dc * 512:(dc + 1) * 512], c == 0, c == 7,
                             R=[b_yT, b_wout], W=[pbuf[pb]])
                    rc.update(i, pb, dc)
                rc.end_tile(i, first + tt, False)
        P.barrier()
        AR.release(m0)

    for l in range(n_layers):
        kind = l % 3
        j = l // 3
        last = l == DEPTH - 1
        ctx_in_needed = (not last) or kind == 0
        ctx_out_needed = not last
        phase_mods(l)
        phase_ffn(l, 0, ctx_in_needed, first_ffn=(l == 0), final=False)
        if kind == 0:
            lam_init = 0.8 - 0.6 * math.exp(-0.3 * l)
            phase_attn_qkv(l, j)
            phase_attn_core(l, j, ctx_out_needed, lam_init)
        elif kind == 1:
            phase_conf(l, ctx_out_needed)
        else:
            phase_sconv(l, ctx_out_needed)
        phase_ffn(l, 2, ctx_out_needed, first_ffn=False, final=(l == n_layers - 1))
    P.wait_all("sp", store_ops)
    P.emit()
    return nc, AR.peak, {e: len(q) for e, q in P.q.items()}


def _fm(v, nchunk):
    return np.ascontiguousarray(np.asarray(v, np.float32).reshape(nchunk, 128).T)


def _rope_tables():
    rows = SEQ // 64
    row_ids = np.repeat(np.arange(rows, dtype=np.float32), 64)
    col_ids = np.tile(np.arange(64, dtype=np.float32), rows)
    half = HD // 2
    inv_freq = (np.float32(10000.0) ** (-np.arange(0, half, 2, dtype=np.float32) / np.float32(half))).astype(np.float32)
    ang_r = row_ids[:, None] * inv_freq
    ang_c = col_ids[:, None] * inv_freq
    ang = np.concatenate([ang_r, ang_r, ang_c, ang_c], axis=-1)
    cos = np.cos(ang).astype(np.float32)
    sin = np.sin(ang).astype(np.float32)
    sgn = np.concatenate([-np.ones(16), np.ones(16), -np.ones(16), np.ones(16)]).astype(np.float32)
    sin_s = sin * sgn
    to = lambda t: np.ascontiguousarray(t.reshape(SEQ // 128, 128, 64).transpose(1, 0, 2))
    return to(cos), to(sin_s)


def _swap_idx():
    return np.concatenate([np.arange(16, 32), np.arange(0, 16), np.arange(48, 64), np.arange(32, 48)])


def make_in_maps(inputs):
    f = lambda k: np.asarray(inputs[k], np.float32)
    x, c, ctx, c_ctx = f("x"), f("c"), f("ctx"), f("c_ctx")
    B = x.shape[0]
    cos, sin_s = _rope_tables()
    norm_g = f("norm_g")
    norm_gF = np.ascontiguousarray(norm_g.reshape(DEPTH, 3, 8, 128).transpose(3, 0, 1, 2))
    qg, kg = f("attn_q_g"), f("attn_k_g")
    sw = _swap_idx()
    ag = np.stack([qg, kg, qg[:, sw], kg[:, sw]], axis=1)
    attn_g = np.ascontiguousarray(np.broadcast_to(ag[None], (128, 2, 4, 64)))
    attn_lam = np.ascontiguousarray(np.broadcast_to(f("attn_lambda")[None], (128, 2, 4, 64)))
    attn_sublnF = np.ascontiguousarray(f("attn_subln_g").T)
    conv_b_inF = _fm(f("conv_b_in")[0], 16)
    conv_dwF = np.ascontiguousarray(f("conv_dw_w")[0].reshape(CK, 8, 128).transpose(2, 0, 1))
    conv_vecF = np.ascontiguousarray(np.stack([_fm(f("conv_dw_b")[0], 8), _fm(f("conv_ln_g")[0], 8), _fm(f("conv_ln_b")[0], 8)], axis=1))
    sc_dwF = np.ascontiguousarray(f("sc_dw_w")[0].reshape(SK, 8, 128).transpose(2, 0, 1))
    shared = {
        "ada_w": f("ada_w"), "ada_b": f("ada_b").reshape(DEPTH, 1, NMOD * D), "norm_gF": norm_gF,
        "ffn_w_in": f("ffn_w_in"), "ffn_w_out": f("ffn_w_out"),
        "attn_w_qkv": f("attn_w_qkv"), "attn_w_o": f("attn_w_o"), "attn_g": attn_g, "attn_lam": attn_lam,
        "attn_sublnF": attn_sublnF, "rope_cos": cos, "rope_sin": sin_s,
        "conv_w_in": f("conv_w_in")[0], "conv_b_inF": conv_b_inF, "conv_dwF": conv_dwF, "conv_vecF": conv_vecF,
        "conv_w_out": f("conv_w_out")[0], "conv_b_out": f("conv_b_out").reshape(1, D),
        "sc_w_in": f("sc_w_in")[0], "sc_dwF": sc_dwF, "sc_w_out": f("sc_w_out")[0],
    }
    maps = []
    for b in range(B):
        cv = np.ascontiguousarray(np.stack([c[b].reshape(8, 128).T, c_ctx.reshape(8, 128).T], axis=-1))
        mp = dict(shared)
        mp.update({"x": np.ascontiguousarray(x[b]), "ctx": np.ascontiguousarray(ctx[b]), "cvec": cv})
        maps.append(mp)
    return maps


def run(inputs, n_layers=DEPTH, debug=False, trace=False):
    nc, peak, counts = build_program(n_layers=n_layers, debug=debug)
    maps = make_in_maps(inputs)
    res = run_bass_kernel_spmd(nc, maps, core_ids=list(range(len(maps))), trace=trace)
    out = np.stack([r["out"] for r in res.results], axis=0)
    if debug:
        return out, res
    return out


def kernel(**inputs):
    return run(inputs).astype(np.float32)
```
